# Optimizing a Trainium2 kernel written in Bass

```python
import math
import jax, jax.numpy as jnp
from jax import lax
import numpy as np

D_MODEL = 1024
BATCH = 8
SEQ = 4096
DEPTH = 2
DEC_BATCH = 32
DEC_SEQ = 2048
PAST_LEN = 128

W_A = 512
CONV_W = 3
N_HEADS_B = 8
QK_NOPE = 64
QK_ROPE = 32
V_HEAD = 64
Q_LORA = 256
KV_LORA = 128
W_B = N_HEADS_B * V_HEAD
ROPE_THETA = 10000.0
CHUNK = 128
N_GROUPS_C = 4
W_C = 512
GC = W_C // N_GROUPS_C
N_HEADS_D = 8
HD_D = 64
W_D = N_HEADS_D * HD_D
DIL_PAIRS = ((128, 1), (512, 4), (2048, 16))
N_BUCKETS = 32
MAX_DIST = 1024
Q_BLOCK = 128
N_EVEN = (DEPTH + 1) // 2
N_ODD = DEPTH // 2
EPS = 1e-6
NEG = -1e30
F32 = jnp.float32

EVEN_SPLITS = (W_A, W_A, W_A, W_A, Q_LORA, KV_LORA, QK_ROPE, W_B)
ODD_SPLITS = (W_C, W_C, W_C, W_D, W_D, W_D, W_D)
EVEN_IN = sum(EVEN_SPLITS)
ODD_IN = sum(ODD_SPLITS)

kernel_name = 'hybrid_bidir_encoder_conv_mla_gmlp_dilated'


def _split(x, sizes):
    idx = np.cumsum(sizes)[:-1].tolist()
    return jnp.split(x, idx, axis=-1)


def rmsnorm(x, g):
    xf = x.astype(F32)
    y = xf * lax.rsqrt(jnp.mean(xf * xf, axis=-1, keepdims=True) + EPS)
    return (y * g.astype(F32)).astype(x.dtype)


def layernorm(x, g, b):
    xf = x.astype(F32)
    xc = xf - jnp.mean(xf, axis=-1, keepdims=True)
    y = xc * lax.rsqrt(jnp.mean(xc * xc, axis=-1, keepdims=True) + EPS)
    return (y * g.astype(F32) + b.astype(F32)).astype(x.dtype)


def rope(x):
    S, R = x.shape[1], x.shape[-1]
    half = R // 2
    inv = ROPE_THETA ** (-jnp.arange(half, dtype=F32) * 2.0 / R)
    ang = jnp.arange(S, dtype=F32)[:, None] * inv[None, :]
    cos = jnp.cos(ang)[None, :, None, :]
    sin = jnp.sin(ang)[None, :, None, :]
    xf = x.astype(F32)
    x1, x2 = xf[..., :half], xf[..., half:]
    return jnp.concatenate([x1 * cos - x2 * sin, x1 * sin + x2 * cos], axis=-1).astype(x.dtype)


def t5_bucket(rel):
    half = N_BUCKETS // 2
    max_exact = half // 2
    n = np.abs(rel)
    large = max_exact + (np.log(np.maximum(n, 1) / max_exact) / np.log(MAX_DIST / max_exact)
                         * (half - max_exact)).astype(np.int32)
    large = np.minimum(large, half - 1)
    return ((rel > 0).astype(np.int32) * half + np.where(n < max_exact, n, large)).astype(np.int32)


def dilated_offsets():
    return np.stack([d * np.arange(-(w // (2 * d)), w // (2 * d) + 1) for w, d in DIL_PAIRS]).astype(np.int32)


def short_conv(h, w):
    C = h.shape[-1]
    return lax.conv_general_dilated(h, w[:, None, :].astype(h.dtype), window_strides=(1,),
                                    padding=((CONV_W // 2, CONV_W // 2),),
                                    dimension_numbers=('NWC', 'WIO', 'NWC'),
                                    feature_group_count=C)


def dense_attention(q, k, v):
    B, S, H, dk = q.shape
    nb = S // Q_BLOCK
    scale = dk ** -0.5
    qb = q.reshape(B, nb, Q_BLOCK, H, dk).transpose(1, 0, 2, 3, 4)

    def block(qi):
        s = jnp.einsum('bqhd,bkhd->bhqk', qi, k).astype(F32) * scale
        p = jax.nn.softmax(s, axis=-1).astype(v.dtype)
        return jnp.einsum('bhqk,bkhd->bqhd', p, v)

    o = lax.map(block, qb)
    return o.transpose(1, 0, 2, 3, 4).reshape(B, S, H * v.shape[-1])


def dilated_attention(q, k, v, rel_bias):
    B, S, H, dh = q.shape
    offs = dilated_offsets()
    pad = int(np.abs(offs).max())
    bias = jnp.transpose(rel_bias[t5_bucket(offs)].astype(F32), (0, 2, 1))
    kp = jnp.pad(k, ((0, 0), (pad, pad), (0, 0), (0, 0)))
    vp = jnp.pad(v, ((0, 0), (pad, pad), (0, 0), (0, 0)))
    nb = S // Q_BLOCK
    qb = q.reshape(B, nb, Q_BLOCK, H, dh)
    scale = dh ** -0.5
    offs_j = jnp.asarray(offs)

    def one_seq(args):
        qs, ks, vs = args

        def one_block(inp):
            i, qi = inp
            pos = i * Q_BLOCK + jnp.arange(Q_BLOCK, dtype=jnp.int32)
            kidx = pos[:, None, None] + offs_j[None]
            valid = (kidx >= 0) & (kidx < S)
            kg = ks[kidx + pad]
            vg = vs[kidx + pad]
            s = jnp.einsum('qhd,qgkhd->qghk', qi, kg).astype(F32) * scale + bias[None]
            s = jnp.where(valid[:, :, None, :], s, NEG)
            lse = jax.nn.logsumexp(s, axis=-1)
            p = jnp.exp(s - lse[..., None]).astype(vs.dtype)
            og = jnp.einsum('qghk,qgkhd->qghd', p, vg)
            alpha = jax.nn.softmax(lse, axis=1).astype(vs.dtype)
            return jnp.einsum('qgh,qghd->qhd', alpha, og)

        return lax.map(one_block, (jnp.arange(nb, dtype=jnp.int32), qs))

    o = lax.map(one_seq, (qb, kp, vp))
    return o.reshape(B, S, H * dh)


def even_mixer(h, w_in, a_conv, q_norm, w_uq, kv_norm, w_ukv, q_gain, k_gain, w_out):
    B, S, _ = h.shape
    a_b, a_c, a_x, a_z, cq, ckv, kr, b_z = _split(h @ w_in, EVEN_SPLITS)
    ya = a_b * short_conv(a_c * a_x, a_conv) * jax.nn.silu(a_z)
    q = (rmsnorm(cq, q_norm) @ w_uq).reshape(B, S, N_HEADS_B, QK_NOPE + QK_ROPE)
    kv = (rmsnorm(ckv, kv_norm) @ w_ukv).reshape(B, S, N_HEADS_B, QK_NOPE + V_HEAD)
    k_nope, v = kv[..., :QK_NOPE], kv[..., QK_NOPE:]
    k_rope = jnp.broadcast_to(kr[:, :, None, :], (B, S, N_HEADS_B, QK_ROPE))
    k = jnp.concatenate([k_nope, k_rope], axis=-1)
    q = rmsnorm(q, q_gain)
    k = rmsnorm(k, k_gain)
    q = jnp.concatenate([q[..., :QK_NOPE], rope(q[..., QK_NOPE:])], axis=-1)
    k = jnp.concatenate([k[..., :QK_NOPE], rope(k[..., QK_NOPE:])], axis=-1)
    yb = dense_attention(q, k, v) * jax.nn.silu(b_z)
    return jnp.concatenate([ya, yb], axis=-1) @ w_out


def odd_mixer(h, w_in, c_vnorm_g, c_vnorm_b, c_ws, c_bs, dq_gain, dk_gain, rel_bias, w_out):
    B, S, _ = h.shape
    cu, cv, cz, dq, dk, dv, dz = _split(h @ w_in, ODD_SPLITS)
    u = jax.nn.gelu(cu)
    vv = layernorm(jax.nn.gelu(cv), c_vnorm_g, c_vnorm_b)
    vv = vv.reshape(B, S // CHUNK, CHUNK, N_GROUPS_C, GC)
    s = jnp.einsum('gpq,bnqgc->bnpgc', c_ws, vv) + c_bs.T[None, None, :, :, None]
    yc = u * s.reshape(B, S, W_C) * jax.nn.silu(cz)
    q = rmsnorm(dq.reshape(B, S, N_HEADS_D, HD_D), dq_gain)
    k = rmsnorm(dk.reshape(B, S, N_HEADS_D, HD_D), dk_gain)
    v = dv.reshape(B, S, N_HEADS_D, HD_D)
    yd = dilated_attention(q, k, v, rel_bias) * jax.nn.silu(dz)
    return jnp.concatenate([yc, yd], axis=-1) @ w_out


def trunk(x, c, norm_g, w_mod, b_mod, rel_bias, w_in_e, a_conv, mla_q_norm, mla_w_uq, mla_kv_norm,
          mla_w_ukv, mla_q_gain, mla_k_gain, w_out_e, w_in_o, c_vnorm_g, c_vnorm_b, c_ws, c_bs,
          d_q_gain, d_k_gain, w_out_o):
    for layer in range(DEPTH):
        mod = jax.nn.silu(c) @ w_mod[layer] + b_mod[layer]
        shift, scale, gate = jnp.split(mod[:, None, :], 3, axis=-1)
        h = rmsnorm(x, norm_g[layer]) * (1 + scale) + shift
        i = layer // 2
        if layer % 2 == 0:
            y = even_mixer(h, w_in_e[i], a_conv[i], mla_q_norm[i], mla_w_uq[i], mla_kv_norm[i],
                           mla_w_ukv[i], mla_q_gain[i], mla_k_gain[i], w_out_e[i])
        else:
            y = odd_mixer(h, w_in_o[i], c_vnorm_g[i], c_vnorm_b[i], c_ws[i], c_bs[i],
                          d_q_gain[i], d_k_gain[i], rel_bias, w_out_o[i])
        x = x + gate * y
    return x


def setup_inputs(seed: int = 0) -> dict:
    key = jax.random.key(seed)
    ks = jax.random.split(key, 32)

    def nrm(k, shape, s):
        return jax.random.normal(k, shape, F32) * s

    def gain(k, shape):
        return 1.0 + 0.05 * jax.random.normal(k, shape, F32)

    D = D_MODEL
    return {
        'x_prompt': nrm(ks[0], (BATCH, SEQ, D), 1.0),
        'x_sample': nrm(ks[1], (DEC_BATCH, DEC_SEQ, D), 1.0),
        'c_prompt': nrm(ks[2], (BATCH, D), 1.0),
        'c_sample': nrm(ks[3], (DEC_BATCH, D), 1.0),
        'norm_g': gain(ks[4], (DEPTH, D)),
        'w_mod': nrm(ks[5], (DEPTH, D, 3 * D), 0.5 * D ** -0.5),
        'b_mod': nrm(ks[6], (DEPTH, 3 * D), 0.02),
        'rel_bias': nrm(ks[7], (N_BUCKETS, N_HEADS_D), 0.5),
        'w_in_e': nrm(ks[8], (N_EVEN, D, EVEN_IN), D ** -0.5),
        'a_conv': nrm(ks[9], (N_EVEN, CONV_W, W_A), CONV_W ** -0.5),
        'mla_q_norm': gain(ks[10], (N_EVEN, Q_LORA)),
        'mla_w_uq': nrm(ks[11], (N_EVEN, Q_LORA, N_HEADS_B * (QK_NOPE + QK_ROPE)), Q_LORA ** -0.5),
        'mla_kv_norm': gain(ks[12], (N_EVEN, KV_LORA)),
        'mla_w_ukv': nrm(ks[13], (N_EVEN, KV_LORA, N_HEADS_B * (QK_NOPE + V_HEAD)), KV_LORA ** -0.5),
        'mla_q_gain': gain(ks[14], (N_EVEN, QK_NOPE + QK_ROPE)),
        'mla_k_gain': gain(ks[15], (N_EVEN, QK_NOPE + QK_ROPE)),
        'w_out_e': nrm(ks[16], (N_EVEN, W_A + W_B, D), (W_A + W_B) ** -0.5),
        'w_in_o': nrm(ks[17], (N_ODD, D, ODD_IN), D ** -0.5),
        'c_vnorm_g': gain(ks[18], (N_ODD, W_C)),
        'c_vnorm_b': nrm(ks[19], (N_ODD, W_C), 0.02),
        'c_ws': nrm(ks[20], (N_ODD, N_GROUPS_C, CHUNK, CHUNK), CHUNK ** -0.5),
        'c_bs': gain(ks[21], (N_ODD, N_GROUPS_C, CHUNK)),
        'd_q_gain': gain(ks[22], (N_ODD, HD_D)),
        'd_k_gain': gain(ks[23], (N_ODD, HD_D)),
        'w_out_o': nrm(ks[24], (N_ODD, W_C + W_D, D), (W_C + W_D) ** -0.5),
    }


def reference(x_prompt, x_sample, c_prompt, c_sample, norm_g, w_mod, b_mod, rel_bias, w_in_e, a_conv,
              mla_q_norm, mla_w_uq, mla_kv_norm, mla_w_ukv, mla_q_gain, mla_k_gain, w_out_e, w_in_o,
              c_vnorm_g, c_vnorm_b, c_ws, c_bs, d_q_gain, d_k_gain, w_out_o):
    y_prompt = trunk(x_prompt, c_prompt, norm_g, w_mod, b_mod, rel_bias, w_in_e, a_conv, mla_q_norm,
                     mla_w_uq, mla_kv_norm, mla_w_ukv, mla_q_gain, mla_k_gain, w_out_e, w_in_o,
                     c_vnorm_g, c_vnorm_b, c_ws, c_bs, d_q_gain, d_k_gain, w_out_o)
    y_sample = trunk(x_sample, c_sample, norm_g, w_mod, b_mod, rel_bias, w_in_e, a_conv, mla_q_norm,
                     mla_w_uq, mla_kv_norm, mla_w_ukv, mla_q_gain, mla_k_gain, w_out_e, w_in_o,
                     c_vnorm_g, c_vnorm_b, c_ws, c_bs, d_q_gain, d_k_gain, w_out_o)
    return (y_prompt, y_sample)
```

```python
import numpy as np
import concourse.bass as bass
import concourse.mybir as mybir
from concourse.bass_utils import run_bass_kernel_spmd
from contextlib import ExitStack

F32 = mybir.dt.float32
BF16 = mybir.dt.bfloat16
AF = mybir.ActivationFunctionType
ALU = mybir.AluOpType
AX = mybir.AxisListType
EPS = 1e-6
D = 1024
G = 512
FAST_RECIP = False


def RECIP(e, out, in_):
    if FAST_RECIP:
        return e.reciprocal_approx_fast(out=out, in_=in_)
    return e.reciprocal(out=out, in_=in_)


C0 = 1408
TW = 2944
WL = 3072


class Buf:
    def __init__(self, name):
        self.name = name
        self.excl = name.startswith('PS') or name.startswith('PO') or name.startswith('PT')
        self.lw = None
        self.rd = []
        self.sems = {}


class Sched:
    ENG = ('pe', 'act', 'dve', 'pool', 'sp')

    def __init__(self, nc, es):
        self.nc, self.es = nc, es
        self.q = {e: [] for e in self.ENG}
        self.cur = {e: None for e in self.ENG}
        self.waited = {e: {} for e in self.ENG}
        self.nsem = 0
        self.ninst = 0

    def newsem(self):
        self.nsem += 1
        return self.es.enter_context(self.nc.semaphore("s%d" % self.nsem))

    def _deps(self, R, W, extra):
        d = list(extra)
        for b in R:
            d.append(b.lw)
            if b.excl:
                d.extend(b.rd)
        for b in W:
            d.append(b.lw)
            d.extend(b.rd)
        return d

    def _wait(self, eng, deps):
        best = {}
        for t in deps:
            if t is None:
                continue
            sem, val, te = t
            if eng == 'pe' and te == 'pe':
                continue
            k = id(sem)
            if k not in best or best[k][1] < val:
                best[k] = (sem, val)
        w = self.waited[eng]
        for k, (sem, val) in best.items():
            if w.get(k, 0) >= val:
                continue
            w[k] = val
            self.q[eng].append(lambda e, sem=sem, val=val: e.wait_ge(sem, val))

    def _mark(self, tk, R, W):
        for b in R:
            b.rd.append(tk)
        for b in W:
            b.lw = tk
            b.rd = []

    def op(self, eng, fn, R=(), W=(), deps=()):
        self._wait(eng, self._deps(R, W, deps))
        c = self.cur[eng]
        if c is None or c[1] >= 30000:
            c = self.cur[eng] = [self.newsem(), 0]
        c[1] += 1
        sem, val = c[0], c[1]
        self.q[eng].append(lambda e, fn=fn, sem=sem: fn(e).then_inc(sem, 1))
        self.ninst += 1
        tk = (sem, val, eng)
        self._mark(tk, R, W)
        return tk

    def group(self, eng, fns, R=(), W=(), deps=()):
        self._wait(eng, self._deps(R, W, deps))
        for fn in fns[:-1]:
            self.q[eng].append(lambda e, fn=fn: fn(e))
            self.ninst += 1
        return self.op(eng, fns[-1], R, W, deps=())

    def dma(self, queue, out, in_, R=(), W=(), sembuf=None, deps=(), **kw):
        self._wait(queue, self._deps(R, W, deps))
        sb = sembuf if sembuf is not None else (W[0] if W else R[0])
        if queue not in sb.sems:
            sb.sems[queue] = [self.newsem(), 0]
        ent = sb.sems[queue]
        ent[1] += 16
        sem, val = ent[0], ent[1]
        self.q[queue].append(lambda e, sem=sem: e.dma_start(out=out, in_=in_, **kw).then_inc(sem, 16))
        self.ninst += 1
        tk = (sem, val, 'dma')
        self._mark(tk, R, W)
        return tk

    def fence(self, extra=()):
        tks = [(c[0], c[1], e) for e, c in self.cur.items() if c is not None and e != 'sp']
        tks += list(extra)
        for e in ('pe', 'act', 'dve', 'pool'):
            self._wait(e, [t for t in tks if t[2] != e or e != 'pe'])

    def run(self, block, final_deps):
        for e in self.ENG:
            self._wait(e, final_deps)
        q = self.q

        @block.tensor
        def _(t):
            for f in q['pe']:
                f(t)

        @block.scalar
        def _(a):
            for f in q['act']:
                f(a)

        @block.vector
        def _(v):
            for f in q['dve']:
                f(v)

        @block.gpsimd
        def _(g):
            for f in q['pool']:
                f(g)

        @block.sync
        def _(s):
            for f in q['sp']:
                f(s)


def _t5_bucket(rel):
    half, max_exact = 16, 8
    n = np.abs(rel)
    large = max_exact + (np.log(np.maximum(n, 1) / max_exact) / np.log(1024 / max_exact)
                         * (half - max_exact)).astype(np.int32)
    large = np.minimum(large, half - 1)
    return ((rel > 0).astype(np.int32) * half + np.where(n < max_exact, n, large)).astype(np.int32)


def _host_consts():
    o = 1536 - np.arange(WL)
    mult = np.zeros(WL, np.float32)
    for w, d in ((128, 1), (512, 4), (2048, 16)):
        offs = d * np.arange(-(w // (2 * d)), w // (2 * d) + 1)
        mult += np.isin(o, offs).astype(np.float32)
    oh = np.zeros((32, WL), np.float32)
    bk = _t5_bucket(o)
    oh[bk, np.arange(WL)] = (mult > 0).astype(np.float32)
    half = 16
    inv = 10000.0 ** (-np.arange(half, dtype=np.float32) * 2.0 / 32)
    ang = np.arange(4096, dtype=np.float32)[:, None] * inv[None, :]
    return dict(oh=oh, mult=mult[None, :].copy(), cos=np.cos(ang).astype(np.float32),
                sin=np.sin(ang).astype(np.float32), ident=np.eye(128, dtype=np.float32))


CH = {}
_names = (['E_KV', 'UKV', 'E_AC', 'E_AX'] + ['EA%d' % j for j in range(4)] + ['E_BZ', 'E_CQ', 'UQ'] +
          ['EO%d' % j for j in range(4)] + ['O_K', 'O_V', 'O_CU', 'O_CV', 'O_CZ', 'O_DQ', 'O_DZ'] +
          ['OO%d' % j for j in range(4)] + ['T%d' % h for h in range(8)])
for _i, _n in enumerate(_names):
    CH[_n] = _i
NCH = len(_names)


def build(seqs, nlayers=2):
    nseq = len(seqs)
    ntok = sum(seqs)
    smax = max(seqs)
    ntmax = smax // 128
    ngmax = smax // G
    nc = bass.Bass("TRN2", target_bir_lowering=False)

    def din(name, shape, dt=F32):
        return nc.dram_tensor(name, list(shape), dt, kind="ExternalInput").ap()

    x = din("x", [ntok, D])
    cT = din("cT", [D, nseq])
    norm_gT = din("norm_gT", [2, 128, 8])
    w_mod = din("w_mod", [2, D, 3 * D])
    b_mod = din("b_mod", [2, 3 * D])
    rel_bias = din("rel_bias", [32, 8])
    w_in_e = din("w_in_e", [D, 2976])
    w_uq = din("w_uq", [256, 768])
    w_ukv = din("w_ukv", [128, 1024])
    w_out_e = din("w_out_e", [D, D])
    w_in_o = din("w_in_o", [D, 3584])
    w_out_o = din("w_out_o", [D, D])
    colv_d = din("colv", [128, 24])
    rowv_d = din("rowv", [1, 1088])
    c_bs_d = din("c_bs", [1, 512])
    c_wsT_d = din("c_wsT", [128, 512])
    cos_d = din("cos", [4096, 16])
    sin_d = din("sin", [4096, 16])
    oh_d = din("oh", [32, WL])
    mult_d = din("mult", [1, WL])
    ident_d = din("ident", [128, 128])
    y = nc.dram_tensor("y", [ntok, D], F32, kind="ExternalOutput").ap()
    wsc = nc.dram_tensor("wsc", [NCH, 128, 4096], BF16, kind="Internal").ap()
    modsc = nc.dram_tensor("modsc", [2, nseq, 3 * D], F32, kind="Internal").ap()
    wvec = nc.dram_tensor("wvec", [8, WL], BF16, kind="Internal").ap()

    es = ExitStack()
    with es:
        S = Sched(nc, es)

        def sb(name, shape, dt):
            return es.enter_context(nc.sbuf_tensor("sb_" + name, list(shape), dt))

        def ps(name, shape, dt):
            return es.enter_context(nc.psum_tensor("ps_" + name, list(shape), dt))

        ident = sb("ident", [128, 128], BF16)
        onesf = sb("onesf", [128, 128], F32)
        cosT = sb("cosT", [128, ntmax, 16], F32)
        sinT = sb("sinT", [128, ntmax, 16], F32)
        colv = sb("colv", [128, 24], F32)
        rowv = sb("rowv", [128, 1088], F32)
        cbs = sb("cbs", [1, 512], F32)
        wsT = sb("wsT", [128, 512], BF16)
        KT = sb("KT", [128, 4, smax], BF16)
        KrT = sb("KrT", [128, 4096], BF16)
        rsK = sb("rsK", [128, ntmax, 8], F32)
        V = sb("V", [128, ntmax, 584], BF16)
        pedge = sb("pedge", [128, 4, ngmax + 2, 2], BF16)
        ring = [sb("ring%d" % i, [128, 4096], BF16) for i in range(2)]
        ringcfg = {'slots': [0, 1]}
        gate_bc = sb("gate_bc", [128, D], F32)
        modv = sb("modv", [128, 32], F32)
        xin = sb("xin", [128, 4, D], F32)
        xs = [sb("xs%d" % i, [128, D], BF16) for i in range(2)]
        hT = sb("hT", [128, 8, G], BF16)
        stat = sb("stat", [128, 64], F32)
        tA = sb("tA", [128, 1024], F32)
        tB = sb("tB", [128, 1024], F32)
        tC = sb("tC", [128, 1024], F32)
        b1 = sb("b1", [128, 514], BF16)
        pcv = sb("pcv", [128, 514], BF16)
        gz = sb("gz", [128, 4, G], BF16)
        gz2 = sb("gz2", [64, 8, G], BF16)
        ycA = sb("ycA", [128, 4, G], BF16)
        ybT = sb("ybT", [128, 8, G], BF16)
        QT = sb("QT", [128, 8, G], BF16)
        QrT = sb("QrT", [128, 8, G], BF16)
        ring.append(KrT)
        ring.append(QrT[:].rearrange("p h g -> p (h g)"))
        ring.append(QT[:].rearrange("p h g -> p (h g)"))
        ring.append(ybT[:].rearrange("p h g -> p (h g)"))
        tmb = sb("tmb", [128, 1024], BF16)
        tmb2 = sb("tmb2", [128, 512], BF16)
        ckT = sb("ckT", [128, 2, G], BF16)
        Pb = [sb("P%d" % i, [128, 1024], BF16) for i in range(2)]
        bcs = sb("bcs", [64, G], F32)
        kvr = sb("kvr", [128, 4, 160], F32)
        scT = sb("scT", [128, 8, nseq], BF16)
        relb = sb("relb", [32, 8], BF16)
        onesb = sb("onesb", [1, 128], BF16)
        bmodc = sb("bmodc", [1, 512], BF16)
        PS = [ps("PS%d" % i, [128, 1024], F32) for i in range(2)]
        PO = [ps("PO%d" % i, [128, 512], F32) for i in range(2)]
        PT = [ps("PT%d" % i, [128, 1024], BF16) for i in range(2)]

        B = {}

        def bf(name):
            if name not in B:
                B[name] = Buf(name)
            return B[name]

        cnt = {'ps': 0, 'po': 0, 'pt': 0, 'ring': 0, 'xs': 0, 'P': 0, 'xo': 0, 'sc': 0}

        def rbuf(r):
            return bf(('ring0', 'ring1', 'KrT', 'QrT', 'QT', 'ybT')[r])

        def psb(i):
            return [bf('PS%da' % i), bf('PS%db' % i)]

        def sc_slot(sl):
            return PS[sl // 2][:, (sl % 2) * 512:(sl % 2 + 1) * 512], bf('PS%d%s' % (sl // 2, 'ab'[sl % 2]))

        def p_slot(sl):
            return Pb[sl // 2][:, (sl % 2) * 512:(sl % 2 + 1) * 512], bf('P%d%s' % (sl // 2, 'ab'[sl % 2]))

        def run_pipeline(items, depth=2, delay=6):
            n = len(items)
            pending = []
            i = 0
            while i < n + depth or pending:
                if i < n:
                    items[i][0]()
                nxt = []
                for cd, f in pending:
                    if cd <= 0:
                        f()
                    else:
                        nxt.append((cd - 1, f))
                pending = nxt
                if 0 <= i - depth < n:
                    fin = items[i - depth][1]()
                    if fin is not None:
                        fin[0]()
                        pending.append((delay, fin[1]))
                i += 1

        def rot(kind, n):
            i = cnt[kind] % n
            cnt[kind] += 1
            return i

        S.dma('sp', tA[:, 0:128], ident_d, W=[bf('tA')])
        S.op('dve', lambda e: e.tensor_copy(out=ident[:], in_=tA[:, 0:128]), R=[bf('tA')], W=[bf('ident')])
        S.op('dve', lambda e: e.memset(onesf[:], 1.0), W=[bf('onesf')])
        S.op('dve', lambda e: e.memset(onesb[:], 1.0), W=[bf('onesb')])
        S.op('dve', lambda e: e.memset(V[:], 0.0), W=[bf('V')])
        S.op('dve', lambda e: e.memset(V[:, :, 0:520].rearrange("p t (h d) -> p t h d", d=65)[:, :, :, 64:65], 1.0), W=[bf('V')])
        S.op('dve', lambda e: e.memset(QT[:], 0.0), W=[bf('QT')])
        S.op('dve', lambda e: e.memset(QrT[:], 0.0), W=[bf('QrT')])
        S.op('dve', lambda e: e.memset(KrT[:], 0.0), W=[bf('KrT')])
        S.op('dve', lambda e: e.memset(ybT[:], 0.0), W=[bf('ybT')])
        for i in range(2):
            S.op('dve', lambda e, i=i: e.memset(ring[i][:], 0.0), W=[rbuf(i)])
        S.op('dve', lambda e: e.memset(pedge[:], 0.0), W=[bf('pedge')])
        S.dma('sp', cosT[:], cos_d[0:smax, :].rearrange("(t p) d -> p t d", p=128), W=[bf('cos')])
        S.dma('sp', sinT[:], sin_d[0:smax, :].rearrange("(t p) d -> p t d", p=128), W=[bf('sin')])
        S.dma('sp', colv[:], colv_d, W=[bf('colv')])
        S.dma('sp', rowv[:], rowv_d.partition_broadcast(128), W=[bf('rowv')])
        S.dma('sp', cbs[:], c_bs_d, W=[bf('cbs')])
        S.dma('pool', wsT[:], c_wsT_d, W=[bf('wsT')])

        def wchunk_k1024(name, w, c0, ncols, width=512, off=0):
            dst = wsc[CH[name]][:, 0:8 * width].rearrange("p (kc j) -> p kc j", j=width)[:, :, off:off + ncols]
            S.dma('pool', dst, w[:, c0:c0 + ncols].rearrange("(kc p) j -> p kc j", p=128), W=[bf('wsc_' + name + str(off))],
                  sembuf=bf('wsc_' + name))

        wchunk_k1024('E_KV', w_in_e, 2304, 160, width=160)
        S.dma('pool', wsc[CH['UKV']][:, 0:1024], w_ukv, W=[bf('wsc_UKV0')], sembuf=bf('wsc_UKV'))
        wchunk_k1024('E_AC', w_in_e, 512, 512)
        wchunk_k1024('E_AX', w_in_e, 1024, 512)
        for j in range(4):
            for pi, c0 in enumerate((512, 1024, 0, 1536)):
                wchunk_k1024('EA%d' % j, w_in_e, c0 + j * 128, 128, off=pi * 128)
        wchunk_k1024('E_BZ', w_in_e, 2464, 512)
        wchunk_k1024('E_CQ', w_in_e, 2048, 256, width=256)
        S.dma('pool', wsc[CH['UQ']][:, 0:1536].rearrange("p (kc j) -> p kc j", j=768),
              w_uq.rearrange("(kc p) j -> p kc j", p=128), W=[bf('wsc_UQ0')], sembuf=bf('wsc_UQ'))
        for nm, wo in (('EO', w_out_e), ('OO', w_out_o)):
            for j in range(4):
                S.dma('sp', wsc[CH['%s%d' % (nm, j)]][64:128, 1024:3072], ring[0][64:128, 1024:3072], R=[bf('ring0')],
                      W=[bf('wsc_%s%dz' % (nm, j))], sembuf=bf('wsc_%s%d' % (nm, j)))
                S.dma('pool', wsc[CH['%s%d' % (nm, j)]][:, 0:1024].rearrange("p (kc j) -> p kc j", j=256),
                      wo[0:512, j * 256:(j + 1) * 256].rearrange("(kc p) j -> p kc j", p=128),
                      W=[bf('wsc_%s%da' % (nm, j))], sembuf=bf('wsc_%s%d' % (nm, j)))
                S.dma('pool', wsc[CH['%s%d' % (nm, j)]][0:64, 1024:3072].rearrange("p (h j) -> p h j", j=256),
                      wo[512:1024, j * 256:(j + 1) * 256].rearrange("(h p) j -> p h j", p=64),
                      W=[bf('wsc_%s%db' % (nm, j))], sembuf=bf('wsc_%s%d' % (nm, j)))
        for nm, c0 in (('O_K', 2048), ('O_V', 2560), ('O_CU', 0), ('O_CV', 512), ('O_CZ', 1024), ('O_DQ', 1536),
                       ('O_DZ', 3072)):
            wchunk_k1024(nm, w_in_o, c0, 512)

        def chunk_tickets(name):
            return [(e_[0], e_[1], 'dma') for k, b in B.items() if k == 'wsc_' + name for e_ in b.sems.values()]

        S.dma('pool', relb[:], rel_bias, W=[bf('relb')])
        for cc in range(6):
            S.dma('pool', tmb2[0:32, :], oh_d[:, cc * 512:(cc + 1) * 512], W=[bf('tmb2')])
            S.dma('sp', tB[0:8, 0:512], mult_d[:, cc * 512:(cc + 1) * 512].partition_broadcast(8), W=[bf('tB')])
            po = rot('po', 2)
            S.op('pe', lambda e, po=po: e.matmul(PO[po][0:8, :], lhsT=relb[:], rhs=tmb2[0:32, :], start=True, stop=True),
                 R=[bf('relb'), bf('tmb2')], W=[bf('PO%d' % po)])
            S.op('act', lambda e, po=po: e.activation(out=tA[0:8, 0:512], in_=PO[po][0:8, :], func=AF.Exp),
                 R=[bf('PO%d' % po)], W=[bf('tA')])
            S.op('dve', lambda e: e.tensor_tensor(out=tmb[0:8, 0:512], in0=tA[0:8, 0:512], in1=tB[0:8, 0:512], op=ALU.mult),
                 R=[bf('tA'), bf('tB')], W=[bf('tmb')])
            S.dma('sp', wvec[:, cc * 512:(cc + 1) * 512], tmb[0:8, 0:512], R=[bf('tmb')], W=[bf('wvec')])
        for h in range(8):
            for kl in range(128):
                S.dma('sp' if kl % 2 else 'pool', wsc[CH['T%d' % h]][kl:kl + 1, 0:TW], wvec[h:h + 1, 128 - kl:128 - kl + TW],
                      R=[bf('wvec')], W=[bf('wsc_T%d_%d' % (h, kl))], sembuf=bf('wsc_T%d' % h))

        S.dma('sp', tA[:, 0:8 * nseq].rearrange("p (k s) -> p k s", s=nseq), cT.rearrange("(k p) s -> p k s", p=128),
              W=[bf('tA')], allow_slow_non_contiguous=True)
        S.op('act', lambda e: e.activation(out=tB[:, 0:8 * nseq], in_=tA[:, 0:8 * nseq], func=AF.Sigmoid),
             R=[bf('tA')], W=[bf('tB')])
        S.op('dve', lambda e: e.tensor_tensor(out=scT[:].rearrange("p k s -> p (k s)"), in0=tA[:, 0:8 * nseq],
                                              in1=tB[:, 0:8 * nseq], op=ALU.mult), R=[bf('tA'), bf('tB')], W=[bf('scT')])
        for l in range(nlayers):
            for cc in range(6):
                r = rot('ring', 2)
                S.dma('pool', ring[r][:].rearrange("p (kc j) -> p kc j", j=512),
                      w_mod[l][:, cc * 512:(cc + 1) * 512].rearrange("(kc p) j -> p kc j", p=128), W=[rbuf(r)])
                S.dma('pool', bmodc[:], b_mod[l:l + 1, cc * 512:(cc + 1) * 512], W=[bf('bmodc')])
                po = rot('po', 2)
                fns = [lambda e, po=po, r=r, kc=kc: e.matmul(PO[po][0:nseq, :], lhsT=scT[:, kc, :],
                                                             rhs=ring[r][:, kc * 512:(kc + 1) * 512], start=(kc == 0), stop=False)
                       for kc in range(8)]
                fns.append(lambda e, po=po, l=l, cc=cc: e.matmul(
                    PO[po][0:nseq, :], lhsT=onesb[0:1, 0:nseq],
                    rhs=bmodc[0:1, :], start=False, stop=True))
                S.group('pe', fns, R=[bf('scT'), rbuf(r), bf('onesb'), bf('bmodc')], W=[bf('PO%d' % po)])
                S.op('dve', lambda e, po=po: e.tensor_copy(out=tC[0:nseq, 0:512], in_=PO[po][0:nseq, :]),
                     R=[bf('PO%d' % po)], W=[bf('tC')])
                S.dma('sp', modsc[l][:, cc * 512:(cc + 1) * 512], tC[0:nseq, 0:512], R=[bf('tC')], W=[bf('modsc')])

        def load_chunk(name, slot=None):
            r = slot if slot is not None else ringcfg['slots'][rot('ring', len(ringcfg['slots']))]
            deps = chunk_tickets(name)
            nc_ = (3072 if name[:2] in ('EO', 'OO') else 2944 if name[0] == 'T' else 1024 if name == 'UKV' else 1536 if name == 'UQ'
                   else 1280 if name == 'E_KV' else 2048 if name == 'E_CQ' else 4096)
            S.dma('sp', ring[r][:, 0:nc_], wsc[CH[name]][:, 0:nc_], W=[rbuf(r)], deps=deps)
            return r

        def rstd_from(ssv, outv, n, dim):
            S.op('act', lambda e: e.activation(out=stat[:, 56:56 + n], in_=ssv, func=AF.Ln, scale=1.0 / dim, bias=EPS),
                 R=[bf('stat')], W=[bf('stat2')])
            S.op('act', lambda e: e.activation(out=outv, in_=stat[:, 56:56 + n], func=AF.Exp, scale=-0.5),
                 R=[bf('stat2')], W=[bf('stat')])

        def seq_layer_setup(l, si):
            S.dma('sp', modv[:, 0:16].rearrange("p (a j) -> p a j", j=8),
                  modsc[l, si, 0:2048].rearrange("(a j p) -> p a j", p=128, j=8), R=[bf('modsc')], W=[bf('modv')],
                  allow_slow_non_contiguous=True)
            S.dma('sp', modv[:, 24:32], norm_gT[l], W=[bf('modvg')])
            S.dma('sp', gate_bc[:], modsc[l, si:si + 1, 2048:3072].partition_broadcast(128), R=[bf('modsc')], W=[bf('gate_bc')])
            S.op('dve', lambda e: e.scalar_tensor_tensor(out=modv[:, 16:24], in0=modv[:, 8:16], scalar=1.0, in1=modv[:, 24:32],
                                                         op0=ALU.add, op1=ALU.mult), R=[bf('modv'), bf('modvg')], W=[bf('modv2')])

        pref = {'key': None, 'plan': [], 'idx': 0}

        def x_load(src, t0):
            rdep = [bf('ydram')] if src is y else []
            S.dma('sp', xin[:], src[t0:t0 + G, :].rearrange("(t p) f -> p t f", p=128), R=rdep, W=[bf('xin')], sembuf=bf('xin'))

        def stage_n(src, t0):
            if pref['key'] != (src is y, t0):
                x_load(src, t0)
            pref['key'] = None
            stage_n_compute()
            plan = pref['plan']
            if plan:
                i = pref['idx']
                assert plan[i][0:2] == (src is y, t0), (plan[i], src is y, t0)
                pref['idx'] = i + 1
                if i + 1 < len(plan) and plan[i + 1][2]:
                    nsrc = y if plan[i + 1][0] else x
                    x_load(nsrc, plan[i + 1][1])
                    pref['key'] = (plan[i + 1][0], plan[i + 1][1])

        def stage_n_compute():
            S.op('dve', lambda e: e.memset(stat[:, 0:4], 0.0), W=[bf('stat')])
            for t in range(4):
                S.op('act', lambda e, t=t: e.activation(out=tA[:], in_=xin[:, t, :], func=AF.Square, accum_out=stat[:, t:t + 1]),
                     R=[bf('xin')], W=[bf('tA'), bf('stat')])
            rstd_from(stat[:, 0:4], stat[:, 4:8], 4, float(D))
            for t in range(4):
                xi = rot('xs', 2)
                S.op('act', lambda e, t=t, xi=xi: e.activation(out=xs[xi][:], in_=xin[:, t, :], func=AF.Copy, scale=stat[:, 4 + t:5 + t]),
                     R=[bf('xin'), bf('stat')], W=[bf('xs%d' % xi)])
                pt = rot('pt', 2)
                S.group('pe', [lambda e, kc=kc, xi=xi, pt=pt: e.transpose(out=PT[pt][:, kc * 128:(kc + 1) * 128],
                                                                            in_=xs[xi][:, kc * 128:(kc + 1) * 128], identity=ident[:])
                               for kc in range(8)], R=[bf('xs%d' % xi), bf('ident')], W=[bf('PT%d' % pt)])
                S.op('dve', lambda e, pt=pt: e.tensor_tensor(out=tmb[:].rearrange("p (k j) -> p k j", j=128),
                                                             in0=PT[pt][:].rearrange("p (k j) -> p k j", j=128),
                                                             in1=modv[:, 16:24].unsqueeze(2).to_broadcast([128, 8, 128]), op=ALU.mult),
                     R=[bf('PT%d' % pt), bf('modv2')], W=[bf('tmb')])
                S.op('dve', lambda e, t=t: e.tensor_tensor(out=hT[:, :, t * 128:(t + 1) * 128],
                                                           in0=tmb[:].rearrange("p (k j) -> p k j", j=128),
                                                           in1=modv[:, 0:8].unsqueeze(2).to_broadcast([128, 8, 128]), op=ALU.add),
                     R=[bf('tmb'), bf('modv')], W=[bf('hT')])

        def proj_fm(r, c0, width, m0, msz, po):
            S.group('pe', [lambda e, kc=kc: e.matmul(PO[po][0:msz, :], lhsT=ring[r][:, kc * width + c0 + m0:kc * width + c0 + m0 + msz],
                                                     rhs=hT[:, kc, :], start=(kc == 0), stop=(kc == 7)) for kc in range(8)],
                    R=[rbuf(r), bf('hT')], W=[bf('PO%d' % po)])

        def proj_tm(r, width, ncols, t, pst, c0=0, o0=0):
            fns = []
            for n0 in range(0, ncols, 512):
                nn = min(512, ncols - n0)
                for kc in range(8):
                    fns.append(lambda e, kc=kc, n0=n0, nn=nn: e.matmul(
                        PS[pst][:, o0 + n0:o0 + n0 + nn], lhsT=hT[:, kc, t * 128:(t + 1) * 128],
                        rhs=ring[r][:, kc * width + c0 + n0:kc * width + c0 + n0 + nn], start=(kc == 0), stop=(kc == 7)))
            S.group('pe', fns, R=[rbuf(r), bf('hT')], W=[*psb(pst)])

        def rope_tm(src3, gain_bc, t_abs, nh, dst):
            cosb = cosT[:, t_abs, :].unsqueeze(1).to_broadcast([128, nh, 16])
            sinb = sinT[:, t_abs, :].unsqueeze(1).to_broadcast([128, nh, 16])
            g3 = tC[:, 0:nh * 32].rearrange("p (h d) -> p h d", d=32)
            w1 = tC[:, 256:256 + nh * 16].rearrange("p (h d) -> p h d", d=16)
            w2 = tC[:, 512:512 + nh * 16].rearrange("p (h d) -> p h d", d=16)
            S.op('dve', lambda e: e.tensor_tensor(out=g3, in0=src3, in1=gain_bc.unsqueeze(1).to_broadcast([128, nh, 32]), op=ALU.mult),
                 R=[bf('tB'), bf('rowv')], W=[bf('tC')])
            S.op('dve', lambda e: e.tensor_tensor(out=w1, in0=g3[:, :, 0:16], in1=cosb, op=ALU.mult), R=[bf('tC'), bf('cos')], W=[bf('tCw1')])
            S.op('dve', lambda e: e.tensor_tensor(out=w2, in0=g3[:, :, 16:32], in1=sinb, op=ALU.mult), R=[bf('tC'), bf('sin')], W=[bf('tCw2')])
            S.op('dve', lambda e: e.tensor_tensor(out=dst[:, :, 0:16], in0=w1, in1=w2, op=ALU.subtract), R=[bf('tCw1'), bf('tCw2')], W=[bf('ropeo')])
            S.op('dve', lambda e: e.tensor_tensor(out=w1, in0=g3[:, :, 0:16], in1=sinb, op=ALU.mult), R=[bf('tC'), bf('sin'), bf('ropeo')], W=[bf('tCw1')])
            S.op('dve', lambda e: e.tensor_tensor(out=w2, in0=g3[:, :, 16:32], in1=cosb, op=ALU.mult), R=[bf('tC'), bf('cos'), bf('ropeo')], W=[bf('tCw2')])
            S.op('dve', lambda e: e.tensor_tensor(out=dst[:, :, 16:32], in0=w1, in1=w2, op=ALU.add), R=[bf('tCw1'), bf('tCw2')], W=[bf('ropeo')])

        def head_norm(pst, ncol_h, nh, dim, dst_scaled, o0=0):
            v3 = PS[pst][:, o0:o0 + nh * ncol_h].rearrange("p (h d) -> p h d", d=ncol_h)
            a3 = tA[:, 0:nh * ncol_h].rearrange("p (h d) -> p h d", d=ncol_h)
            S.op('act', lambda e: e.activation(out=tA[:, 0:nh * ncol_h], in_=PS[pst][:, o0:o0 + nh * ncol_h], func=AF.Square),
                 W=[bf('tA'), *psb(pst)])
            if nlayers == -12:
                return v3
            S.op('dve', lambda e: e.tensor_reduce(out=stat[:, 16:16 + nh], in_=a3[:, :, 0:dim], axis=AX.X, op=ALU.add),
                 R=[bf('tA')], W=[bf('stat')])
            return v3

        def finalize_a(po, on_act=False):
            if on_act:
                S.op('act', lambda e: e.activation(out=tC[64:65, 0:512], in_=PO[po][64:65, :], func=AF.Ln), R=[bf('PO%d' % po)], W=[bf('rden')])
                S.op('act', lambda e: e.activation(out=tC[64:65, 0:512], in_=tC[64:65, 0:512], func=AF.Exp, scale=-1.0), W=[bf('rden')])
            else:
                S.op('dve', lambda e: RECIP(e, tC[64:65, 0:512], PO[po][64:65, :]), R=[bf('PO%d' % po)], W=[bf('rden')])

        def finalize_b(po, h, gate_ap, gate_buf):
            sap, sbuf_ = sc_slot(rot('sc', 4))
            S.op('pe', lambda e: e.matmul(sap[0:64, :], lhsT=onesf[64:65, 0:64], rhs=tC[64:65, 0:512], start=True, stop=True),
                 R=[bf('onesf'), bf('rden')], W=[sbuf_])
            S.op('dve', lambda e: e.tensor_tensor(out=bcs[:], in0=sap[0:64, :], in1=gate_ap, op=ALU.mult),
                 R=[sbuf_, gate_buf], W=[bf('bcs')])
            S.op('dve', lambda e: e.tensor_tensor(out=ybT[0:64, h, :], in0=PO[po][0:64, :], in1=bcs[:], op=ALU.mult),
                 R=[bf('PO%d' % po), bf('bcs')], W=[bf('ybT')])

        def out_proj2(prefix, dst, t0):
            tk = None
            over = {2: 4, 3: 3} if prefix == 'EO' else {2: 4}
            rs = [load_chunk('%s%d' % (prefix, j), over.get(j)) for j in range(4)]
            for j in range(4):
                r = rs[j]
                tX, tXn = (tC, 'tC') if j % 2 == 0 else (tB, 'tB')
                for t in range(4):
                    pst = rot('ps', 2)
                    fns = [lambda e, kc=kc, t=t, pst=pst, r=r: e.matmul(PS[pst][:, 0:256], lhsT=ycA[:, kc, t * 128:(t + 1) * 128],
                                                                       rhs=ring[r][:, kc * 256:(kc + 1) * 256], start=(kc == 0), stop=False)
                           for kc in range(4)]
                    fns += [lambda e, h=h, t=t, pst=pst, r=r: e.matmul(PS[pst][:, 0:256], lhsT=ybT[:, h, t * 128:(t + 1) * 128],
                                                                      rhs=ring[r][:, 1024 + h * 256:1024 + (h + 1) * 256],
                                                                      start=False, stop=(h == 7)) for h in range(8)]
                    S.group('pe', fns, R=[rbuf(r), bf('ycA'), bf('ybT')], W=[*psb(pst)])
                    S.op('dve', lambda e, t=t, pst=pst, j=j, tX=tX: e.tensor_tensor(out=tX[:, t * 256:(t + 1) * 256], in0=PS[pst][:, 0:256],
                                                                                    in1=gate_bc[:, j * 256:(j + 1) * 256], op=ALU.mult),
                         R=[*psb(pst), bf('gate_bc')], W=[bf(tXn)])
                tk = S.dma('pool', dst[t0:t0 + G, j * 256:(j + 1) * 256].rearrange("(t p) f -> p t f", p=128),
                           tX[:].rearrange("p (t f) -> p t f", f=256), R=[bf(tXn)], W=[bf('ydram')], sembuf=bf('ystore%d' % (j % 2)),
                           accum_op=ALU.add)
            S.op('dve', lambda e: e.memset(QT[:], 0.0), W=[bf('QT')])
            if prefix == 'EO':
                S.op('dve', lambda e: e.memset(QrT[:], 0.0), W=[bf('QrT')])
            return tk

        def l0_pass_a(src, base, slen):
            ng = slen // G
            ringcfg['slots'] = [0, 1, 4, 5]
            S.op('dve', lambda e: e.memset(KrT[:], 0.0), W=[bf('KrT')])
            S.op('dve', lambda e: e.memset(QrT[:], 0.0), W=[bf('QrT')])
            S.op('dve', lambda e: e.memset(pedge[:], 0.0), W=[bf('pedge')])
            for g in range(ng):
                stage_n(src, base + g * G)
                r = load_chunk('E_KV')
                for t in range(4):
                    pst = rot('ps', 2)
                    proj_tm(r, 160, 160, t, pst)
                    S.op('dve', lambda e, t=t, pst=pst: e.tensor_copy(out=kvr[:, t, :], in_=PS[pst][:, 0:160]),
                         R=[*psb(pst)], W=[bf('kvr')])
                if nlayers == -5:
                    return
                rc = load_chunk('E_AC')
                rx = load_chunk('E_AX')
                for j in range(4 if nlayers != -4 else 0):
                    for which, rr in ((0, rc), (1, rx)):
                        po = rot('po', 2)
                        fns = []
                        for col in range(2):
                            cidx = col * (G - 1)
                            for kc in range(8):
                                fns.append(lambda e, kc=kc, cidx=cidx, col=col, rr=rr, po=po, j=j: e.matmul(
                                    PO[po][:, col:col + 1], lhsT=ring[rr][:, kc * 512 + j * 128:kc * 512 + (j + 1) * 128],
                                    rhs=hT[:, kc, cidx:cidx + 1], start=(kc == 0), stop=(kc == 7)))
                        S.group('pe', fns, R=[rbuf(rr), bf('hT')], W=[bf('PO%d' % po)])
                        if which == 0:
                            S.op('dve', lambda e, po=po: e.tensor_copy(out=stat[:, 32:34], in_=PO[po][:, 0:2]),
                                 R=[bf('PO%d' % po)], W=[bf('stat3')])
                        else:
                            S.op('dve', lambda e, po=po, j=j, g=g: e.tensor_tensor(out=pedge[:, j, g + 1, :], in0=PO[po][:, 0:2],
                                                                                   in1=stat[:, 32:34], op=ALU.mult),
                                 R=[bf('PO%d' % po), bf('stat3')], W=[bf('pedge')])
                S.op('dve', lambda e: e.memset(stat[:, 0:4], 0.0), W=[bf('stat')])
                for t in range(4):
                    S.op('act', lambda e, t=t: e.activation(out=tA[:, 0:128], in_=kvr[:, t, 0:128], func=AF.Square,
                                                            accum_out=stat[:, t:t + 1]), R=[bf('kvr')], W=[bf('tA'), bf('stat')])
                rstd_from(stat[:, 0:4], stat[:, 4:8], 4, 128.0)
                pt = rot('pt', 2)
                for t in range(4):
                    S.op('act', lambda e, t=t: e.activation(out=tmb2[:, t * 128:(t + 1) * 128], in_=kvr[:, t, 0:128], func=AF.Copy,
                                                            scale=stat[:, 4 + t:5 + t]), R=[bf('kvr'), bf('stat')], W=[bf('tmb2')])
                S.group('pe', [lambda e, t=t, pt=pt: e.transpose(out=PT[pt][:, t * 128:(t + 1) * 128], in_=tmb2[:, t * 128:(t + 1) * 128],
                                                                identity=ident[:]) for t in range(4)],
                        R=[bf('tmb2'), bf('ident')], W=[bf('PT%d' % pt)])
                S.op('dve', lambda e, pt=pt: e.tensor_scalar(out=ckT[:, 0, :], in0=PT[pt][:, 0:512], scalar1=colv[:, 2:3], scalar2=None,
                                                             op0=ALU.mult), R=[bf('PT%d' % pt), bf('colv')], W=[bf('ckT')])
                if nlayers == -6:
                    return
                ru = load_chunk('UKV')
                psts = {}

                def projA(t, ru=ru, psts=psts):
                    pst = rot('ps', 2)
                    psts[t] = pst
                    S.group('pe', [lambda e, n0=n0, t=t, pst=pst, ru=ru: e.matmul(PS[pst][:, n0:n0 + 512], lhsT=ckT[:, 0, t * 128:(t + 1) * 128],
                                                                          rhs=ring[ru][:, n0:n0 + 512], start=True, stop=True)
                                   for n0 in (0, 512)], R=[rbuf(ru), bf('ckT')], W=[*psb(pst)])
                projA(0)
                for t in range(4):
                    tt = (base - base) + g * 4 + t
                    if t + 1 < 4:
                        projA(t + 1)
                    pst = psts[t]
                    v3 = PS[pst][:].rearrange("p (h d) -> p h d", d=128)
                    S.op('dve', lambda e, tt=tt, v3=v3: e.tensor_copy(out=V[:, tt, 0:520].rearrange("p (h d) -> p h d", d=65)[:, :, 0:64], in_=v3[:, :, 64:128]),
                         R=[*psb(pst)], W=[bf('V')])
                    if nlayers == -8:
                        continue
                    head_norm(pst, 128, 8, 64, None)
                    if nlayers in (-11, -12):
                        continue
                    S.op('dve', lambda e: e.memset(stat[:, 24:25], 0.0), W=[bf('stat4')])
                    S.op('act', lambda e, t=t: e.activation(out=tB[:, 0:32], in_=kvr[:, t, 128:160], func=AF.Square,
                                                            accum_out=stat[:, 24:25]), R=[bf('kvr')], W=[bf('tB'), bf('stat4')])
                    S.op('dve', lambda e: e.tensor_scalar(out=stat[:, 16:24], in0=stat[:, 16:24], scalar1=stat[:, 24:25], scalar2=None,
                                                          op0=ALU.add), R=[bf('stat4')], W=[bf('stat')])
                    rstd_from(stat[:, 16:24], stat[:, 40:48], 8, 96.0)
                    S.op('dve', lambda e, tt=tt: e.tensor_scalar(out=rsK[:, tt, :], in0=stat[:, 40:48], scalar1=96.0 ** -0.5, scalar2=None,
                                                                 op0=ALU.mult), R=[bf('stat')], W=[bf('rsK')])
                    if nlayers == -9:
                        continue
                    S.op('dve', lambda e, v3=v3: e.tensor_copy(out=tmb[:, 0:512].rearrange("p (h d) -> p h d", d=64), in_=v3[:, :, 0:64]),
                         R=[*psb(pst)], W=[bf('tmb')])
                    pt = rot('pt', 2)
                    S.group('pe', [lambda e, j=j, pt=pt: e.transpose(out=PT[pt][:, j * 128:(j + 1) * 128], in_=tmb[:, j * 128:(j + 1) * 128],
                                                                    identity=ident[:]) for j in range(4)],
                            R=[bf('tmb'), bf('ident')], W=[bf('PT%d' % pt)])
                    if nlayers == -10:
                        continue
                    S.op('dve', lambda e, pt=pt, tt=tt: e.tensor_scalar(out=KT[:, :, tt * 128:(tt + 1) * 128],
                                                                        in0=PT[pt][:, 0:512].rearrange("p (j c) -> p j c", c=128),
                                                                        scalar1=colv[:, 3:4], scalar2=None, op0=ALU.mult),
                         R=[bf('PT%d' % pt), bf('colv')], W=[bf('KT')])
                    if nlayers == -7:
                        continue
                    S.op('dve', lambda e, t=t: e.tensor_copy(out=tB[:, 64:96], in_=kvr[:, t, 128:160]), R=[bf('kvr')], W=[bf('tB')])
                    rope_tm(tB[:, 64:96].rearrange("p (h d) -> p h d", d=32), rowv[:, 32:64], tt, 1,
                            tmb2[:, 0:32].rearrange("p (h d) -> p h d", d=32))
                    pt = rot('pt', 2)
                    S.op('pe', lambda e, pt=pt: e.transpose(out=PT[pt][0:32, 0:128], in_=tmb2[:, 0:32], identity=ident[:]),
                         R=[bf('ropeo'), bf('ident')], W=[bf('PT%d' % pt)])
                    S.op('dve', lambda e, pt=pt, tt=tt: e.tensor_copy(out=KrT[0:32, tt * 128:(tt + 1) * 128], in_=PT[pt][0:32, 0:128]),
                         R=[bf('PT%d' % pt)], W=[bf('KrT')])

        def l0_pass_b(src, dst, base, slen):
            ng = slen // G
            nt = slen // 128
            ringcfg['slots'] = [0, 1]
            S.op('dve', lambda e: e.memset(QT[:], 0.0), W=[bf('QT')])
            S.op('dve', lambda e: e.memset(ybT[:], 0.0), W=[bf('ybT')])
            for g in range(ng):
                stage_n(src, base + g * G)
                t0c = base + g * G
                S.dma('sp', dst[t0c:t0c + G, :], src[t0c:t0c + G, :], W=[bf('ydram')], sembuf=bf('ycopy'))
                for j in range(4):
                    r = load_chunk('EA%d' % j)
                    for part in range(4):
                        po = rot('po', 2)
                        proj_fm(r, 0, 512, part * 128, 128, po)
                        if part == 0:
                            S.op('act', lambda e, po=po: e.activation(out=b1[:, 0:512], in_=PO[po][:, :], func=AF.Copy),
                                 R=[bf('PO%d' % po)], W=[bf('b1')])
                        elif part == 1:
                            S.op('dve', lambda e, po=po: e.tensor_tensor(out=pcv[:, 1:513], in0=PO[po][:, :], in1=b1[:, 0:512], op=ALU.mult),
                                 R=[bf('PO%d' % po), bf('b1')], W=[bf('pcv')])
                            S.op('dve', lambda e, j=j, g=g: e.tensor_copy(out=pcv[:, 0:1], in_=pedge[:, j, g, 1:2]),
                                 R=[bf('pedge')], W=[bf('pcv')])
                            S.op('dve', lambda e, j=j, g=g: e.tensor_copy(out=pcv[:, 513:514], in_=pedge[:, j, g + 2, 0:1]),
                                 R=[bf('pedge')], W=[bf('pcv')])
                            S.op('dve', lambda e, j=j: e.tensor_scalar(out=tA[:, 0:512], in0=pcv[:, 1:513], scalar1=colv[:, 8 + j * 3 + 1:8 + j * 3 + 2],
                                                                       scalar2=None, op0=ALU.mult), R=[bf('pcv'), bf('colv')], W=[bf('tA')])
                            S.op('dve', lambda e, j=j: e.scalar_tensor_tensor(out=tA[:, 0:512], in0=pcv[:, 0:512], scalar=colv[:, 8 + j * 3:8 + j * 3 + 1],
                                                                              in1=tA[:, 0:512], op0=ALU.mult, op1=ALU.add),
                                 R=[bf('pcv'), bf('colv')], W=[bf('tA')])
                            S.op('dve', lambda e, j=j: e.scalar_tensor_tensor(out=tA[:, 0:512], in0=pcv[:, 2:514], scalar=colv[:, 8 + j * 3 + 2:8 + j * 3 + 3],
                                                                              in1=tA[:, 0:512], op0=ALU.mult, op1=ALU.add),
                                 R=[bf('pcv'), bf('colv')], W=[bf('tA')])
                        elif part == 2:
                            S.op('dve', lambda e, po=po: e.tensor_tensor(out=tA[:, 512:1024], in0=PO[po][:, :], in1=tA[:, 0:512], op=ALU.mult),
                                 R=[bf('PO%d' % po), bf('tA')], W=[bf('tA2')])
                        else:
                            S.op('act', lambda e, po=po: e.activation(out=tB[:, 0:512], in_=PO[po][:, :], func=AF.Sigmoid),
                                 R=[bf('PO%d' % po)], W=[bf('tB')])
                            S.op('dve', lambda e, po=po: e.tensor_tensor(out=tB[:, 512:1024], in0=PO[po][:, :], in1=tB[:, 0:512], op=ALU.mult),
                                 R=[bf('PO%d' % po), bf('tB')], W=[bf('tB2')])
                            S.op('dve', lambda e, j=j: e.tensor_tensor(out=ycA[:, j, :], in0=tA[:, 512:1024], in1=tB[:, 512:1024], op=ALU.mult),
                                 R=[bf('tA2'), bf('tB2')], W=[bf('ycA')])
                r = load_chunk('E_BZ')
                for h in range(8):
                    po = rot('po', 2)
                    proj_fm(r, 0, 512, h * 64, 64, po)
                    S.op('act', lambda e, po=po: e.activation(out=tB[0:64, 0:512], in_=PO[po][0:64, :], func=AF.Sigmoid),
                         R=[bf('PO%d' % po)], W=[bf('tB')])
                    S.op('dve', lambda e, po=po, h=h: e.tensor_tensor(out=gz2[0:64, h, :], in0=PO[po][0:64, :], in1=tB[0:64, 0:512], op=ALU.mult),
                         R=[bf('PO%d' % po), bf('tB')], W=[bf('gz2')])
                r = load_chunk('E_CQ')
                S.op('dve', lambda e: e.memset(stat[:, 0:4], 0.0), W=[bf('stat')])
                pq = []
                for t in range(4):
                    pst = rot('ps', 2)
                    proj_tm(r, 256, 256, t, pst)
                    S.op('dve', lambda e, t=t, pst=pst: e.tensor_copy(out=tC[:, t * 256:(t + 1) * 256], in_=PS[pst][:, 0:256]),
                         R=[*psb(pst)], W=[bf('tCq')])
                    S.op('act', lambda e, t=t: e.activation(out=tA[:, 0:256], in_=tC[:, t * 256:(t + 1) * 256], func=AF.Square,
                                                            accum_out=stat[:, t:t + 1]), R=[bf('tCq')], W=[bf('tA'), bf('stat')])
                rstd_from(stat[:, 0:4], stat[:, 4:8], 4, 256.0)
                for t in range(4):
                    S.op('act', lambda e, t=t: e.activation(out=tmb[:, t * 256:(t + 1) * 256], in_=tC[:, t * 256:(t + 1) * 256], func=AF.Copy,
                                                            scale=stat[:, 4 + t:5 + t]), R=[bf('tCq'), bf('stat')], W=[bf('tmb')])
                for kc in range(2):
                    pt = rot('pt', 2)
                    S.group('pe', [lambda e, t=t, kc=kc, pt=pt: e.transpose(out=PT[pt][:, t * 128:(t + 1) * 128],
                                                                           in_=tmb[:, t * 256 + kc * 128:t * 256 + (kc + 1) * 128], identity=ident[:])
                                   for t in range(4)], R=[bf('tmb'), bf('ident')], W=[bf('PT%d' % pt)])
                    S.op('dve', lambda e, kc=kc, pt=pt: e.tensor_scalar(out=ckT[:, kc, :], in0=PT[pt][:, 0:512], scalar1=colv[:, kc:kc + 1],
                                                                        scalar2=None, op0=ALU.mult), R=[bf('PT%d' % pt), bf('colv')], W=[bf('ckT')])
                r = load_chunk('UQ')
                psts = {}

                def projA(t, r=r, psts=psts):
                    pst = rot('ps', 2)
                    psts[t] = pst
                    fns = []
                    for n0, nn in ((0, 512), (512, 256)):
                        for kc in range(2):
                            fns.append(lambda e, kc=kc, n0=n0, nn=nn, t=t, pst=pst, r=r: e.matmul(
                                PS[pst][:, n0:n0 + nn], lhsT=ckT[:, kc, t * 128:(t + 1) * 128],
                                rhs=ring[r][:, kc * 768 + n0:kc * 768 + n0 + nn], start=(kc == 0), stop=(kc == 1)))
                    S.group('pe', fns, R=[rbuf(r), bf('ckT')], W=[*psb(pst)])
                projA(0)
                for t in range(4):
                    tt = g * 4 + t
                    if t + 1 < 4:
                        projA(t + 1)
                    pst = psts[t]
                    v3 = head_norm(pst, 96, 8, 96, None)
                    rstd_from(stat[:, 16:24], stat[:, 40:48], 8, 96.0)
                    S.op('dve', lambda e, v3=v3: e.tensor_tensor(out=tB[:, 0:768].rearrange("p (h d) -> p h d", d=96), in0=v3,
                                                                 in1=stat[:, 40:48].unsqueeze(2).to_broadcast([128, 8, 96]), op=ALU.mult),
                         R=[*psb(pst), bf('stat')], W=[bf('tB')])
                    q3 = tB[:, 0:768].rearrange("p (h d) -> p h d", d=96)
                    S.op('dve', lambda e, q3=q3: e.tensor_copy(out=tmb[:, 0:512].rearrange("p (h d) -> p h d", d=64), in_=q3[:, :, 0:64]),
                         R=[bf('tB')], W=[bf('tmb')])
                    rope_tm(q3[:, :, 64:96], rowv[:, 0:32], tt, 8, tmb2[:, 0:256].rearrange("p (h d) -> p h d", d=32))
                    pt = rot('pt', 2)
                    S.group('pe', [lambda e, j=j, pt=pt: e.transpose(out=PT[pt][:, j * 128:(j + 1) * 128], in_=tmb[:, j * 128:(j + 1) * 128],
                                                                    identity=ident[:]) for j in range(4)],
                            R=[bf('tmb'), bf('ident')], W=[bf('PT%d' % pt)])
                    S.op('dve', lambda e, pt=pt, t=t: e.tensor_scalar(
                        out=QT[:].rearrange("p (j two) c -> p j two c", two=2)[0:64, :, 0, t * 128:(t + 1) * 128],
                        in0=PT[pt][0:64, 0:512].rearrange("p (j c) -> p j c", c=128), scalar1=colv[0:64, 4:5], scalar2=None, op0=ALU.mult),
                         R=[bf('PT%d' % pt), bf('colv')], W=[bf('QT')])
                    S.op('dve', lambda e, pt=pt, t=t: e.tensor_scalar(
                        out=QT[:].rearrange("p (j two) c -> p j two c", two=2)[64:128, :, 1, t * 128:(t + 1) * 128],
                        in0=PT[pt][64:128, 0:512].rearrange("p (j c) -> p j c", c=128), scalar1=colv[64:128, 4:5], scalar2=None, op0=ALU.mult),
                         R=[bf('PT%d' % pt), bf('colv')], W=[bf('QT')])
                    pt = rot('pt', 2)
                    S.group('pe', [lambda e, h=h, pt=pt: e.transpose(out=PT[pt][0:32, h * 128:(h + 1) * 128], in_=tmb2[:, h * 32:(h + 1) * 32],
                                                                    identity=ident[:]) for h in range(8)],
                            R=[bf('ropeo'), bf('ident')], W=[bf('PT%d' % pt)])
                    S.op('dve', lambda e, pt=pt, t=t: e.tensor_copy(out=QrT[0:32, :, t * 128:(t + 1) * 128],
                                                                    in_=PT[pt][0:32, :].rearrange("p (h c) -> p h c", c=128)),
                         R=[bf('PT%d' % pt)], W=[bf('QrT')])
                items = []
                for h in range(8):
                    hb, j = (h % 2) * 64, h // 2
                    po = rot('po', 2)
                    for kt in range(nt):
                        sl = rot('sc', 4)

                        def front(h=h, hb=hb, j=j, kt=kt, sl=sl):
                            sap, sbuf_ = sc_slot(sl)
                            pap, pbuf_ = p_slot(sl)
                            S.group('pe', [
                                lambda e: e.matmul(sap, lhsT=KT[:, j, kt * 128:(kt + 1) * 128], rhs=QT[:, h, :],
                                                   start=True, stop=False),
                                lambda e: e.matmul(sap, lhsT=KrT[:, kt * 128:(kt + 1) * 128], rhs=QrT[:, h, :], start=False, stop=True)],
                                R=[bf('KT'), bf('KrT'), bf('QT'), bf('QrT')], W=[sbuf_])
                            S.op('act', lambda e: e.activation(out=pap, in_=sap, func=AF.Exp, scale=rsK[:, kt, h:h + 1]),
                                 R=[sbuf_, bf('rsK')], W=[pbuf_])

                        def back(h=h, kt=kt, sl=sl, po=po, nt=nt):
                            pap, pbuf_ = p_slot(sl)
                            S.op('pe', lambda e: e.matmul(PO[po][:, :], lhsT=V[:, kt, h * 65:h * 65 + 128], rhs=pap, start=(kt == 0), stop=(kt == nt - 1)),
                                 R=[bf('V'), pbuf_], W=[bf('PO%d' % po)])
                            if kt == nt - 1:
                                return (lambda: finalize_a(po), lambda: finalize_b(po, h, gz2[0:64, h, :], bf('gz2')))
                            return None
                        items.append((front, back))
                run_pipeline(items, depth=3, delay=max(0, min(6, nt - 3)))
                out_proj2('EO', dst, base + g * G)

        def l1_pass_a(src, base, slen):
            ng = slen // G
            ringcfg['slots'] = [0, 1, 2, 3]
            for g in range(ng):
                stage_n(src, base + g * G)
                rk = load_chunk('O_K')
                rv = load_chunk('O_V')
                psts = {}

                def projA(t, g=g, rv=rv, rk=rk, psts=psts):
                    tt = g * 4 + t
                    pst = rot('ps', 2)
                    proj_tm(rv, 512, 512, t, pst)
                    S.op('dve', lambda e, tt=tt, pst=pst: e.tensor_copy(out=V[:, tt, 0:520].rearrange("p (h d) -> p h d", d=65)[:, :, 0:64],
                                                                       in_=PS[pst][:, 0:512].rearrange("p (h d) -> p h d", d=64)),
                         R=[*psb(pst)], W=[bf('V')])
                    psts[t] = pst
                    proj_tm(rk, 512, 512, t, pst, o0=512)
                projA(0)
                for t in range(4):
                    tt = g * 4 + t
                    if t + 1 < 4:
                        projA(t + 1)
                    pst = psts[t]
                    v3 = head_norm(pst, 64, 8, 64, None, o0=512)
                    rstd_from(stat[:, 16:24], stat[:, 40:48], 8, 64.0)
                    S.op('dve', lambda e, v3=v3: e.tensor_tensor(out=tmb[:, 0:512].rearrange("p (h d) -> p h d", d=64), in0=v3,
                                                                 in1=stat[:, 40:48].unsqueeze(2).to_broadcast([128, 8, 64]), op=ALU.mult),
                         R=[*psb(pst), bf('stat')], W=[bf('tmb')])
                    pt = rot('pt', 2)
                    S.group('pe', [lambda e, j=j, pt=pt: e.transpose(out=PT[pt][:, j * 128:(j + 1) * 128], in_=tmb[:, j * 128:(j + 1) * 128],
                                                                    identity=ident[:]) for j in range(4)],
                            R=[bf('tmb'), bf('ident')], W=[bf('PT%d' % pt)])
                    S.op('dve', lambda e, pt=pt, tt=tt: e.tensor_scalar(out=KT[:, :, tt * 128:(tt + 1) * 128],
                                                                        in0=PT[pt][:, 0:512].rearrange("p (j c) -> p j c", c=128),
                                                                        scalar1=colv[:, 5:6], scalar2=None, op0=ALU.mult),
                         R=[bf('PT%d' % pt), bf('colv')], W=[bf('KT')])

        def gelu_parts(po, dst_f32, tmp):
            S.op('act', lambda e: e.activation(out=tmp, in_=PO[po][:, :], func=AF.Square), R=[bf('PO%d' % po)], W=[bf('gl1')])
            S.op('dve', lambda e: e.tensor_scalar(out=tmp, in0=tmp, scalar1=0.044715 * 1.5957691216, scalar2=1.5957691216, op0=ALU.mult,
                                                  op1=ALU.add), R=[bf('gl1')], W=[bf('gl1')])
            S.op('dve', lambda e: e.tensor_tensor(out=tmp, in0=tmp, in1=PO[po][:, :], op=ALU.mult), R=[bf('gl1'), bf('PO%d' % po)], W=[bf('gl1')])
            S.op('act', lambda e: e.activation(out=dst_f32, in_=tmp, func=AF.Sigmoid), R=[bf('gl1')], W=[bf('gl2')])

        def l1_pass_b(src, dst, base, slen):
            ng = slen // G
            nt = slen // 128
            for g in range(ng):
                stage_n(src, base + g * G)
                S.fence([(e_[0], e_[1], 'dma') for bb in (bf('ystore0'), bf('ystore1')) for e_ in bb.sems.values()])
                r_cu = load_chunk('O_CU')
                r_cz = load_chunk('O_CZ')
                r_cv = load_chunk('O_CV')
                csets = [(tA[:, 0:512], tA[:, 512:1024], tmb[:, 0:512], stat[:, 0:8]),
                         (tB[:, 0:512], tB[:, 512:1024], tmb[:, 512:1024], stat[:, 8:16]),
                         (tC[:, 0:512], tC[:, 512:1024], tmb2[:, 0:512], stat[:, 24:32])]
                GA, GB = 0.044715 * 1.5957691216, 1.5957691216

                def skew(chains, lag):
                    nst = max(len(c_) + jj * lag for jj, c_ in enumerate(chains))
                    for step in range(nst):
                        for jj, c_ in enumerate(chains):
                            si_ = step - jj * lag
                            if 0 <= si_ < len(c_):
                                c_[si_]()

                def cu_chain(j, k, r_cu=r_cu):
                    X, Y, _, _ = csets[k]
                    bX, bY = bf('cX%d' % k), bf('cY%d' % k)
                    box = {}

                    def s0():
                        box['sap'], box['sb'] = sc_slot(rot('sc', 4))
                        sap = box['sap']
                        S.group('pe', [lambda e, kc=kc: e.matmul(sap, lhsT=ring[r_cu][:, kc * 512 + j * 128:kc * 512 + (j + 1) * 128],
                                                                 rhs=hT[:, kc, :], start=(kc == 0), stop=(kc == 7)) for kc in range(8)],
                                R=[rbuf(r_cu), bf('hT')], W=[box['sb']])
                    return [
                        s0,
                        lambda: S.op('act', lambda e: e.activation(out=X, in_=box['sap'], func=AF.Square), R=[box['sb']], W=[bX]),
                        lambda: S.op('dve', lambda e: e.tensor_scalar(out=X, in0=X, scalar1=GA, scalar2=GB, op0=ALU.mult, op1=ALU.add), W=[bX]),
                        lambda: S.op('dve', lambda e: e.tensor_tensor(out=X, in0=X, in1=box['sap'], op=ALU.mult), R=[box['sb']], W=[bX]),
                        lambda: S.op('act', lambda e: e.activation(out=Y, in_=X, func=AF.Sigmoid), R=[bX], W=[bY]),
                        lambda: S.op('dve', lambda e: e.tensor_tensor(out=gz[:, j, :], in0=box['sap'], in1=Y, op=ALU.mult),
                                     R=[box['sb'], bY], W=[bf('gz')]),
                    ]

                def cz_chain(j, k, r_cz=r_cz):
                    X, Y, _, _ = csets[k]
                    bX, bY = bf('cX%d' % k), bf('cY%d' % k)
                    box = {}

                    def s0():
                        box['sap'], box['sb'] = sc_slot(rot('sc', 4))
                        sap = box['sap']
                        S.group('pe', [lambda e, kc=kc: e.matmul(sap, lhsT=ring[r_cz][:, kc * 512 + j * 128:kc * 512 + (j + 1) * 128],
                                                                 rhs=hT[:, kc, :], start=(kc == 0), stop=(kc == 7)) for kc in range(8)],
                                R=[rbuf(r_cz), bf('hT')], W=[box['sb']])
                    return [
                        s0,
                        lambda: S.op('act', lambda e: e.activation(out=Y, in_=box['sap'], func=AF.Sigmoid), R=[box['sb']], W=[bY]),
                        lambda: S.op('dve', lambda e: e.tensor_tensor(out=X, in0=box['sap'], in1=Y, op=ALU.mult), R=[box['sb'], bY], W=[bX]),
                        lambda: S.op('dve', lambda e: e.tensor_tensor(out=gz[:, j, :], in0=gz[:, j, :], in1=X, op=ALU.mult), R=[bX], W=[bf('gz')]),
                    ]

                def cv_chain(t, k, r_cv=r_cv):
                    X, Y, VV, st = csets[k]
                    bX, bY, bV, bS = bf('cX%d' % k), bf('cY%d' % k), bf('cV%d' % k), bf('cS%d' % k)
                    box = {}

                    def s0():
                        box['sap'], box['sb'] = sc_slot(rot('sc', 4))
                        sap = box['sap']
                        S.group('pe', [lambda e, kc=kc: e.matmul(sap, lhsT=hT[:, kc, t * 128:(t + 1) * 128],
                                                                 rhs=ring[r_cv][:, kc * 512:(kc + 1) * 512], start=(kc == 0), stop=(kc == 7))
                                       for kc in range(8)], R=[rbuf(r_cv), bf('hT')], W=[box['sb']])

                    def s6():
                        S.op('dve', lambda e: e.memset(st[:, 1:2], 0.0), W=[bS])
                        S.op('dve', lambda e: e.tensor_reduce(out=st[:, 0:1], in_=X, axis=AX.X, op=ALU.add), R=[bX], W=[bS])

                    def spatial(gi):
                        def f():
                            po = rot('po', 2)
                            S.group('pe', [
                                lambda e: e.matmul(PO[po][:, 0:128], lhsT=VV[:, gi * 128:(gi + 1) * 128], rhs=wsT[:, gi * 128:(gi + 1) * 128],
                                                   start=True, stop=False),
                                lambda e: e.matmul(PO[po][:, 0:128], lhsT=onesf[0:1, :], rhs=cbs[0:1, gi * 128:(gi + 1) * 128],
                                                   start=False, stop=True)],
                                R=[bV, bf('wsT'), bf('onesf'), bf('cbs')], W=[bf('PO%d' % po)])
                            S.op('dve', lambda e: e.tensor_tensor(out=ycA[:, gi, t * 128:(t + 1) * 128], in0=PO[po][:, 0:128],
                                                                  in1=gz[:, gi, t * 128:(t + 1) * 128], op=ALU.mult),
                                 R=[bf('PO%d' % po), bf('gz')], W=[bf('ycA')])
                        return f
                    return [
                        s0,
                        lambda: S.op('act', lambda e: e.activation(out=X, in_=box['sap'], func=AF.Square), R=[box['sb']], W=[bX]),
                        lambda: S.op('dve', lambda e: e.tensor_scalar(out=X, in0=X, scalar1=GA, scalar2=GB, op0=ALU.mult, op1=ALU.add), W=[bX]),
                        lambda: S.op('dve', lambda e: e.tensor_tensor(out=X, in0=X, in1=box['sap'], op=ALU.mult), R=[box['sb']], W=[bX]),
                        lambda: S.op('act', lambda e: e.activation(out=Y, in_=X, func=AF.Sigmoid), R=[bX], W=[bY]),
                        lambda: S.op('dve', lambda e: e.tensor_tensor(out=X, in0=box['sap'], in1=Y, op=ALU.mult), R=[box['sb'], bY], W=[bX]),
                        s6,
                        lambda: S.op('dve', lambda e: e.tensor_scalar(out=st[:, 2:3], in0=st[:, 0:1], scalar1=-1.0 / 512, scalar2=None, op0=ALU.mult),
                                     W=[bS]),
                        lambda: S.op('dve', lambda e: e.tensor_scalar(out=X, in0=X, scalar1=st[:, 2:3], scalar2=None, op0=ALU.add), R=[bS], W=[bX]),
                        lambda: S.op('act', lambda e: e.activation(out=Y, in_=X, func=AF.Square, accum_out=st[:, 1:2]), R=[bX], W=[bY, bS]),
                        lambda: S.op('act', lambda e: e.activation(out=st[:, 4:5], in_=st[:, 1:2], func=AF.Ln, scale=1.0 / 512, bias=EPS), W=[bS]),
                        lambda: S.op('act', lambda e: e.activation(out=st[:, 3:4], in_=st[:, 4:5], func=AF.Exp, scale=-0.5), W=[bS]),
                        lambda: S.op('dve', lambda e: e.scalar_tensor_tensor(out=X, in0=X, scalar=st[:, 3:4], in1=rowv[:, 64:576],
                                                                             op0=ALU.mult, op1=ALU.mult), R=[bS, bf('rowv')], W=[bX]),
                        lambda: S.op('dve', lambda e: e.tensor_tensor(out=VV, in0=X, in1=rowv[:, 576:1088], op=ALU.add), R=[bX, bf('rowv')], W=[bV]),
                        spatial(0), spatial(1), spatial(2), spatial(3),
                    ]

                skew([cu_chain(j, j % 3) for j in range(4)], 2)
                skew([cz_chain(j, (j + 1) % 3) for j in range(4)], 2)
                skew([cv_chain(t, (t + 2) % 3) for t in range(4)], 6)
                S.fence()
                r = load_chunk('O_DZ')
                for h in range(8):
                    po = rot('po', 2)
                    proj_fm(r, 0, 512, h * 64, 64, po)
                    S.op('act', lambda e, po=po: e.activation(out=tB[0:64, 0:512], in_=PO[po][0:64, :], func=AF.Sigmoid),
                         R=[bf('PO%d' % po)], W=[bf('tB')])
                    S.op('dve', lambda e, po=po, h=h: e.tensor_tensor(out=gz2[0:64, h, :], in0=PO[po][0:64, :], in1=tB[0:64, 0:512], op=ALU.mult),
                         R=[bf('PO%d' % po), bf('tB')], W=[bf('gz2')])
                r = load_chunk('O_DQ')
                psts = {}

                def projA(t, r=r, psts=psts):
                    pst = rot('ps', 2)
                    psts[t] = pst
                    proj_tm(r, 512, 512, t, pst)
                projA(0)
                for t in range(4):
                    if t + 1 < 4:
                        projA(t + 1)
                    pst = psts[t]
                    v3 = head_norm(pst, 64, 8, 64, None)
                    rstd_from(stat[:, 16:24], stat[:, 40:48], 8, 64.0)
                    S.op('dve', lambda e, v3=v3: e.tensor_tensor(out=tmb[:, 0:512].rearrange("p (h d) -> p h d", d=64), in0=v3,
                                                                 in1=stat[:, 40:48].unsqueeze(2).to_broadcast([128, 8, 64]), op=ALU.mult),
                         R=[*psb(pst), bf('stat')], W=[bf('tmb')])
                    pt = rot('pt', 2)
                    S.group('pe', [lambda e, j=j, pt=pt: e.transpose(out=PT[pt][:, j * 128:(j + 1) * 128], in_=tmb[:, j * 128:(j + 1) * 128],
                                                                    identity=ident[:]) for j in range(4)],
                            R=[bf('tmb'), bf('ident')], W=[bf('PT%d' % pt)])
                    S.op('dve', lambda e, pt=pt, t=t: e.tensor_scalar(
                        out=QT[:].rearrange("p (j two) c -> p j two c", two=2)[0:64, :, 0, t * 128:(t + 1) * 128],
                        in0=PT[pt][0:64, 0:512].rearrange("p (j c) -> p j c", c=128), scalar1=colv[0:64, 6:7], scalar2=None, op0=ALU.mult),
                         R=[bf('PT%d' % pt), bf('colv')], W=[bf('QT')])
                    S.op('dve', lambda e, pt=pt, t=t: e.tensor_scalar(
                        out=QT[:].rearrange("p (j two) c -> p j two c", two=2)[64:128, :, 1, t * 128:(t + 1) * 128],
                        in0=PT[pt][64:128, 0:512].rearrange("p (j c) -> p j c", c=128), scalar1=colv[64:128, 6:7], scalar2=None, op0=ALU.mult),
                         R=[bf('PT%d' % pt), bf('colv')], W=[bf('QT')])
                items = []
                for h in range(8):
                    hb, j = (h % 2) * 64, h // 2
                    po = rot('po', 2)
                    kts = [kt for kt in range(4 * g - 8, 4 * g + 12) if 0 <= kt < nt]
                    rtbox = {}
                    for ki, kt in enumerate(kts):
                        sl = rot('sc', 4)
                        c0 = (4 * g - kt) * 128 + C0

                        def front(h=h, hb=hb, j=j, kt=kt, ki=ki, sl=sl, c0=c0, rtbox=rtbox):
                            if ki == 0:
                                rtbox['rt'] = load_chunk('T%d' % h)
                            rt = rtbox['rt']
                            sap, sbuf_ = sc_slot(sl)
                            pap, pbuf_ = p_slot(sl)
                            S.op('pe', lambda e: e.matmul(sap, lhsT=KT[:, j, kt * 128:(kt + 1) * 128], rhs=QT[:, h, :],
                                                          start=True, stop=True), R=[bf('KT'), bf('QT')], W=[sbuf_])
                            S.op('act', lambda e: e.activation(out=pap, in_=sap, func=AF.Exp, scale=0.125), R=[sbuf_], W=[pbuf_])
                            S.op('dve',
                                 lambda e: e.tensor_tensor(out=pap, in0=pap, in1=ring[rt][:, c0:c0 + 512], op=ALU.mult),
                                 R=[rbuf(rt)], W=[pbuf_])

                        def back(h=h, kt=kt, ki=ki, sl=sl, po=po, n=len(kts)):
                            pap, pbuf_ = p_slot(sl)
                            S.op('pe', lambda e: e.matmul(PO[po][:, :], lhsT=V[:, kt, h * 65:h * 65 + 128], rhs=pap, start=(ki == 0), stop=(ki == n - 1)),
                                 R=[bf('V'), pbuf_], W=[bf('PO%d' % po)])
                            if ki == n - 1:
                                return (lambda: finalize_a(po, True), lambda: finalize_b(po, h, gz2[0:64, h, :], bf('gz2')))
                            return None
                        items.append((front, back))
                run_pipeline(items, depth=3, delay=max(0, min(6, min(12, nt) - 3)))
                out_proj2('OO', dst, base + g * G)

        if nlayers == 2:
            b0 = 0
            for slen in seqs:
                for ps_i, (isy, pf_first) in enumerate(((False, True), (False, True), (True, False), (True, True))):
                    for g in range(slen // G):
                        pref['plan'].append((isy, b0 + g * G, True if g > 0 else pf_first))
                b0 += slen
        base = 0
        last = None
        for si, slen in enumerate(seqs):
            if nlayers == 0:
                break
            seq_layer_setup(0, si)
            if nlayers == -3:
                break
            if nlayers == -2:
                stage_n(x, base)
                break
            l0_pass_a(x, base, slen)
            if nlayers < 0:
                break
            last = l0_pass_b(x, y, base, slen)
            if nlayers > 1:
                seq_layer_setup(1, si)
                l1_pass_a(y, base, slen)
                l1_pass_b(y, y, base, slen)
            base += slen

        S.sbuf_left = nc.sbuf_bytes_remaining
        block = es.enter_context(nc.Block())
        fin = [(e_[0], e_[1], 'dma') for b in B.values() for e_ in b.sems.values()]
        S.run(block, fin)
    return nc, S


_CACHE = {}


def _prep_shared(inp):
    hc = _host_consts()
    f = lambda a: np.ascontiguousarray(np.asarray(a, dtype=np.float32))
    colv = np.zeros((128, 24), np.float32)
    colv[:, 0:2] = f(inp['mla_q_norm'])[0].reshape(2, 128).T
    colv[:, 2] = f(inp['mla_kv_norm'])[0]
    colv[:, 3] = np.tile(f(inp['mla_k_gain'])[0][0:64], 2)
    colv[:, 4] = np.tile(f(inp['mla_q_gain'])[0][0:64], 2)
    colv[:, 5] = np.tile(f(inp['d_k_gain'])[0], 2)
    colv[:, 6] = np.tile(f(inp['d_q_gain'])[0], 2)
    ac = f(inp['a_conv'])[0]
    for j in range(4):
        colv[:, 8 + j * 3:8 + j * 3 + 3] = ac[:, j * 128:(j + 1) * 128].T
    rowv = np.zeros((1, 1088), np.float32)
    rowv[0, 0:32] = f(inp['mla_q_gain'])[0][64:96]
    rowv[0, 32:64] = f(inp['mla_k_gain'])[0][64:96]
    rowv[0, 64:576] = f(inp['c_vnorm_g'])[0]
    rowv[0, 576:1088] = f(inp['c_vnorm_b'])[0]
    cws = f(inp['c_ws'])[0]
    c_wsT = np.ascontiguousarray(cws.transpose(2, 0, 1).reshape(128, 512))
    shared = dict(
        norm_gT=np.ascontiguousarray(f(inp['norm_g']).reshape(2, 8, 128).transpose(0, 2, 1)),
        w_mod=f(inp['w_mod']), b_mod=f(inp['b_mod']), rel_bias=f(inp['rel_bias']),
        w_in_e=f(inp['w_in_e'])[0], w_uq=f(inp['mla_w_uq'])[0], w_ukv=f(inp['mla_w_ukv'])[0], w_out_e=f(inp['w_out_e'])[0],
        w_in_o=f(inp['w_in_o'])[0], w_out_o=f(inp['w_out_o'])[0], colv=colv, rowv=rowv,
        c_bs=np.ascontiguousarray(f(inp['c_bs'])[0].reshape(1, 512)), c_wsT=c_wsT,
        cos=hc['cos'], sin=hc['sin'], oh=hc['oh'], mult=hc['mult'], ident=hc['ident'])
    return shared


def kernel(**inputs):
    xp = np.asarray(inputs['x_prompt'], dtype=np.float32)
    xsm = np.asarray(inputs['x_sample'], dtype=np.float32)
    cp = np.asarray(inputs['c_prompt'], dtype=np.float32)
    cs = np.asarray(inputs['c_sample'], dtype=np.float32)
    ncore = 8
    seqs = [xp.shape[1]] + [xsm.shape[1]] * 4
    key = tuple(seqs)
    if key not in _CACHE:
        _CACHE[key] = build(seqs)[0]
    nc = _CACHE[key]
    shared = _prep_shared(inputs)
    in_maps = []
    for c in range(ncore):
        xc = np.concatenate([xp[c]] + [xsm[4 * c + i] for i in range(4)], axis=0)
        cc = np.concatenate([cp[c:c + 1], cs[4 * c:4 * c + 4]], axis=0)
        m = dict(shared)
        m['x'] = np.ascontiguousarray(xc)
        m['cT'] = np.ascontiguousarray(cc.T)
        in_maps.append(m)
    res = run_bass_kernel_spmd(nc, in_maps, core_ids=list(range(ncore)))
    yp = np.empty_like(xp)
    ys = np.empty_like(xsm)
    L = xp.shape[1]
    Ls = xsm.shape[1]
    for c in range(ncore):
        yc = np.asarray(res.results[c]['y'])
        yp[c] = yc[0:L]
        for i in range(4):
            ys[4 * c + i] = yc[L + i * Ls:L + (i + 1) * Ls]
    return (yp, ys)
```

```python
import numpy as np
import concourse.bass as bass
import concourse.mybir as mybir
from concourse.bass_utils import run_bass_kernel_spmd
from contextlib import ExitStack

F32 = mybir.dt.float32
BF16 = mybir.dt.bfloat16
AF = mybir.ActivationFunctionType
ALU = mybir.AluOpType
AX = mybir.AxisListType
EPS = 1e-6
D = 1024
G = 512
FAST_RECIP = False


def RECIP(e, out, in_):
    if FAST_RECIP:
        return e.reciprocal_approx_fast(out=out, in_=in_)
    return e.reciprocal(out=out, in_=in_)


C0 = 1408
TW = 2944
WL = 3072


class Buf:
    def __init__(self, name):
        self.name = name
        self.excl = name.startswith('PS') or name.startswith('PO') or name.startswith('PT')
        self.lw = None
        self.rd = []
        self.sems = {}


class Sched:
    ENG = ('pe', 'act', 'dve', 'pool', 'sp')

    def __init__(self, nc, es):
        self.nc, self.es = nc, es
        self.q = {e: [] for e in self.ENG}
        self.cur = {e: None for e in self.ENG}
        self.waited = {e: {} for e in self.ENG}
        self.nsem = 0
        self.ninst = 0

    def newsem(self):
        self.nsem += 1
        return self.es.enter_context(self.nc.semaphore("s%d" % self.nsem))

    def _deps(self, R, W, extra):
        d = list(extra)
        for b in R:
            d.append(b.lw)
            if b.excl:
                d.extend(b.rd)
        for b in W:
            d.append(b.lw)
            d.extend(b.rd)
        return d

    def _wait(self, eng, deps):
        best = {}
        for t in deps:
            if t is None:
                continue
            sem, val, te = t
            if eng == 'pe' and te == 'pe':
                continue
            k = id(sem)
            if k not in best or best[k][1] < val:
                best[k] = (sem, val)
        w = self.waited[eng]
        for k, (sem, val) in best.items():
            if w.get(k, 0) >= val:
                continue
            w[k] = val
            self.q[eng].append(lambda e, sem=sem, val=val: e.wait_ge(sem, val))

    def _mark(self, tk, R, W):
        for b in R:
            b.rd.append(tk)
        for b in W:
            b.lw = tk
            b.rd = []

    def op(self, eng, fn, R=(), W=(), deps=()):
        self._wait(eng, self._deps(R, W, deps))
        c = self.cur[eng]
        if c is None or c[1] >= 30000:
            c = self.cur[eng] = [self.newsem(), 0]
        c[1] += 1
        sem, val = c[0], c[1]
        self.q[eng].append(lambda e, fn=fn, sem=sem: fn(e).then_inc(sem, 1))
        self.ninst += 1
        tk = (sem, val, eng)
        self._mark(tk, R, W)
        return tk

    def group(self, eng, fns, R=(), W=(), deps=()):
        self._wait(eng, self._deps(R, W, deps))
        for fn in fns[:-1]:
            self.q[eng].append(lambda e, fn=fn: fn(e))
            self.ninst += 1
        return self.op(eng, fns[-1], R, W, deps=())

    def dma(self, queue, out, in_, R=(), W=(), sembuf=None, deps=(), **kw):
        self._wait(queue, self._deps(R, W, deps))
        sb = sembuf if sembuf is not None else (W[0] if W else R[0])
        if queue not in sb.sems:
            sb.sems[queue] = [self.newsem(), 0]
        ent = sb.sems[queue]
        ent[1] += 16
        sem, val = ent[0], ent[1]
        self.q[queue].append(lambda e, sem=sem: e.dma_start(out=out, in_=in_, **kw).then_inc(sem, 16))
        self.ninst += 1
        tk = (sem, val, 'dma')
        self._mark(tk, R, W)
        return tk

    def fence(self, extra=()):
        tks = [(c[0], c[1], e) for e, c in self.cur.items() if c is not None and e != 'sp']
        tks += list(extra)
        for e in ('pe', 'act', 'dve', 'pool'):
            self._wait(e, [t for t in tks if t[2] != e or e != 'pe'])

    def run(self, block, final_deps):
        for e in self.ENG:
            self._wait(e, final_deps)
        q = self.q

        @block.tensor
        def _(t):
            for f in q['pe']:
                f(t)

        @block.scalar
        def _(a):
            for f in q['act']:
                f(a)

        @block.vector
        def _(v):
            for f in q['dve']:
                f(v)

        @block.gpsimd
        def _(g):
            for f in q['pool']:
                f(g)

        @block.sync
        def _(s):
            for f in q['sp']:
                f(s)


def _t5_bucket(rel):
    half, max_exact = 16, 8
    n = np.abs(rel)
    large = max_exact + (np.log(np.maximum(n, 1) / max_exact) / np.log(1024 / max_exact)
                         * (half - max_exact)).astype(np.int32)
    large = np.minimum(large, half - 1)
    return ((rel > 0).astype(np.int32) * half + np.where(n < max_exact, n, large)).astype(np.int32)


def _host_consts():
    o = 1536 - np.arange(WL)
    mult = np.zeros(WL, np.float32)
    for w, d in ((128, 1), (512, 4), (2048, 16)):
        offs = d * np.arange(-(w // (2 * d)), w // (2 * d) + 1)
        mult += np.isin(o, offs).astype(np.float32)
    oh = np.zeros((32, WL), np.float32)
    bk = _t5_bucket(o)
    oh[bk, np.arange(WL)] = (mult > 0).astype(np.float32)
    half = 16
    inv = 10000.0 ** (-np.arange(half, dtype=np.float32) * 2.0 / 32)
    ang = np.arange(4096, dtype=np.float32)[:, None] * inv[None, :]
    return dict(oh=oh, mult=mult[None, :].copy(), cos=np.cos(ang).astype(np.float32),
                sin=np.sin(ang).astype(np.float32), ident=np.eye(128, dtype=np.float32))


CH = {}
_names = (['E_KV', 'UKV', 'E_AC', 'E_AX'] + ['EA%d' % j for j in range(4)] + ['E_BZ', 'E_CQ', 'UQ'] +
          ['EO%d' % j for j in range(4)] + ['O_K', 'O_V', 'O_CU', 'O_CV', 'O_CZ', 'O_DQ', 'O_DZ'] +
          ['OO%d' % j for j in range(4)] + ['T%d' % h for h in range(8)])
for _i, _n in enumerate(_names):
    CH[_n] = _i
NCH = len(_names)


def build(seqs, nlayers=2):
    nseq = len(seqs)
    ntok = sum(seqs)
    smax = max(seqs)
    ntmax = smax // 128
    ngmax = smax // G
    nc = bass.Bass("TRN2", target_bir_lowering=False)

    def din(name, shape, dt=F32):
        return nc.dram_tensor(name, list(shape), dt, kind="ExternalInput").ap()

    x = din("x", [ntok, D])
    cT = din("cT", [D, nseq])
    norm_gT = din("norm_gT", [2, 128, 8])
    w_mod = din("w_mod", [2, D, 3 * D])
    b_mod = din("b_mod", [2, 3 * D])
    rel_bias = din("rel_bias", [32, 8])
    w_in_e = din("w_in_e", [D, 2976])
    w_uq = din("w_uq", [256, 768])
    w_ukv = din("w_ukv", [128, 1024])
    w_out_e = din("w_out_e", [D, D])
    w_in_o = din("w_in_o", [D, 3584])
    w_out_o = din("w_out_o", [D, D])
    colv_d = din("colv", [128, 24])
    rowv_d = din("rowv", [1, 1088])
    c_bs_d = din("c_bs", [1, 512])
    c_wsT_d = din("c_wsT", [128, 512])
    cos_d = din("cos", [4096, 16])
    sin_d = din("sin", [4096, 16])
    oh_d = din("oh", [32, WL])
    mult_d = din("mult", [1, WL])
    ident_d = din("ident", [128, 128])
    y = nc.dram_tensor("y", [ntok, D], F32, kind="ExternalOutput").ap()
    wsc = nc.dram_tensor("wsc", [NCH, 128, 4096], BF16, kind="Internal").ap()
    modsc = nc.dram_tensor("modsc", [2, nseq, 3 * D], F32, kind="Internal").ap()
    wvec = nc.dram_tensor("wvec", [8, WL], BF16, kind="Internal").ap()

    es = ExitStack()
    with es:
        S = Sched(nc, es)

        def sb(name, shape, dt):
            return es.enter_context(nc.sbuf_tensor("sb_" + name, list(shape), dt))

        def ps(name, shape, dt):
            return es.enter_context(nc.psum_tensor("ps_" + name, list(shape), dt))

        ident = sb("ident", [128, 128], BF16)
        onesf = sb("onesf", [128, 128], F32)
        cosT = sb("cosT", [128, ntmax, 16], F32)
        sinT = sb("sinT", [128, ntmax, 16], F32)
        colv = sb("colv", [128, 24], F32)
        rowv = sb("rowv", [128, 1088], F32)
        cbs = sb("cbs", [1, 512], F32)
        wsT = sb("wsT", [128, 512], BF16)
        KT = sb("KT", [128, 4, smax], BF16)
        KrT = sb("KrT", [128, 4096], BF16)
        rsK = sb("rsK", [128, ntmax, 8], F32)
        V = sb("V", [128, ntmax, 584], BF16)
        pedge = sb("pedge", [128, 4, ngmax + 2, 2], BF16)
        ring = [sb("ring%d" % i, [128, 4096], BF16) for i in range(2)]
        ringcfg = {'slots': [0, 1]}
        gate_bc = sb("gate_bc", [128, D], F32)
        modv = sb("modv", [128, 32], F32)
        xin = sb("xin", [128, 4, D], F32)
        xs = [sb("xs%d" % i, [128, D], BF16) for i in range(2)]
        hT = sb("hT", [128, 8, G], BF16)
        stat = sb("stat", [128, 64], F32)
        tA = sb("tA", [128, 1024], F32)
        tB = sb("tB", [128, 1024], F32)
        tC = sb("tC", [128, 1024], F32)
        b1 = sb("b1", [128, 514], BF16)
        pcv = sb("pcv", [128, 514], BF16)
        gz = sb("gz", [128, 4, G], BF16)
        gz2 = sb("gz2", [64, 8, G], BF16)
        ycA = sb("ycA", [128, 4, G], BF16)
        ybT = sb("ybT", [128, 8, G], BF16)
        QT = sb("QT", [128, 8, G], BF16)
        QrT = sb("QrT", [128, 8, G], BF16)
        ring.append(KrT)
        ring.append(QrT[:].rearrange("p h g -> p (h g)"))
        ring.append(QT[:].rearrange("p h g -> p (h g)"))
        ring.append(ybT[:].rearrange("p h g -> p (h g)"))
        tmb = sb("tmb", [128, 1024], BF16)
        tmb2 = sb("tmb2", [128, 512], BF16)
        ckT = sb("ckT", [128, 2, G], BF16)
        Pb = [sb("P%d" % i, [128, 1024], BF16) for i in range(2)]
        bcs = sb("bcs", [64, G], F32)
        kvr = sb("kvr", [128, 4, 160], F32)
        scT = sb("scT", [128, 8, nseq], BF16)
        relb = sb("relb", [32, 8], BF16)
        onesb = sb("onesb", [1, 128], BF16)
        bmodc = sb("bmodc", [1, 512], BF16)
        PS = [ps("PS%d" % i, [128, 1024], F32) for i in range(2)]
        PO = [ps("PO%d" % i, [128, 512], F32) for i in range(2)]
        PT = [ps("PT%d" % i, [128, 1024], BF16) for i in range(2)]

        B = {}

        def bf(name):
            if name not in B:
                B[name] = Buf(name)
            return B[name]

        cnt = {'ps': 0, 'po': 0, 'pt': 0, 'ring': 0, 'xs': 0, 'P': 0, 'xo': 0, 'sc': 0}

        def rbuf(r):
            return bf(('ring0', 'ring1', 'KrT', 'QrT', 'QT', 'ybT')[r])

        def psb(i):
            return [bf('PS%da' % i), bf('PS%db' % i)]

        def sc_slot(sl):
            return PS[sl // 2][:, (sl % 2) * 512:(sl % 2 + 1) * 512], bf('PS%d%s' % (sl // 2, 'ab'[sl % 2]))

        def p_slot(sl):
            return Pb[sl // 2][:, (sl % 2) * 512:(sl % 2 + 1) * 512], bf('P%d%s' % (sl // 2, 'ab'[sl % 2]))

        def run_pipeline(items, depth=2, delay=6):
            n = len(items)
            pending = []
            i = 0
            while i < n + depth or pending:
                if i < n:
                    items[i][0]()
                nxt = []
                for cd, f in pending:
                    if cd <= 0:
                        f()
                    else:
                        nxt.append((cd - 1, f))
                pending = nxt
                if 0 <= i - depth < n:
                    fin = items[i - depth][1]()
                    if fin is not None:
                        fin[0]()
                        pending.append((delay, fin[1]))
                i += 1

        def rot(kind, n):
            i = cnt[kind] % n
            cnt[kind] += 1
            return i

        S.dma('sp', tA[:, 0:128], ident_d, W=[bf('tA')])
        S.op('dve', lambda e: e.tensor_copy(out=ident[:], in_=tA[:, 0:128]), R=[bf('tA')], W=[bf('ident')])
        S.op('dve', lambda e: e.memset(onesf[:], 1.0), W=[bf('onesf')])
        S.op('dve', lambda e: e.memset(onesb[:], 1.0), W=[bf('onesb')])
        S.op('dve', lambda e: e.memset(V[:], 0.0), W=[bf('V')])
        S.op('dve', lambda e: e.memset(V[:, :, 0:520].rearrange("p t (h d) -> p t h d", d=65)[:, :, :, 64:65], 1.0), W=[bf('V')])
        S.op('dve', lambda e: e.memset(QT[:], 0.0), W=[bf('QT')])
        S.op('dve', lambda e: e.memset(QrT[:], 0.0), W=[bf('QrT')])
        S.op('dve', lambda e: e.memset(KrT[:], 0.0), W=[bf('KrT')])
        S.op('dve', lambda e: e.memset(ybT[:], 0.0), W=[bf('ybT')])
        for i in range(2):
            S.op('dve', lambda e, i=i: e.memset(ring[i][:], 0.0), W=[rbuf(i)])
        S.op('dve', lambda e: e.memset(pedge[:], 0.0), W=[bf('pedge')])
        S.dma('sp', cosT[:], cos_d[0:smax, :].rearrange("(t p) d -> p t d", p=128), W=[bf('cos')])
        S.dma('sp', sinT[:], sin_d[0:smax, :].rearrange("(t p) d -> p t d", p=128), W=[bf('sin')])
        S.dma('sp', colv[:], colv_d, W=[bf('colv')])
        S.dma('sp', rowv[:], rowv_d.partition_broadcast(128), W=[bf('rowv')])
        S.dma('sp', cbs[:], c_bs_d, W=[bf('cbs')])
        S.dma('pool', wsT[:], c_wsT_d, W=[bf('wsT')])

        S.dma('sp', tA[:, 0:8 * nseq].rearrange("p (k s) -> p k s", s=nseq), cT.rearrange("(k p) s -> p k s", p=128),
              W=[bf('tA')], allow_slow_non_contiguous=True)
        S.op('act', lambda e: e.activation(out=tB[:, 0:8 * nseq], in_=tA[:, 0:8 * nseq], func=AF.Sigmoid),
             R=[bf('tA')], W=[bf('tB')])
        S.op('dve', lambda e: e.tensor_tensor(out=scT[:].rearrange("p k s -> p (k s)"), in0=tA[:, 0:8 * nseq],
                                              in1=tB[:, 0:8 * nseq], op=ALU.mult), R=[bf('tA'), bf('tB')], W=[bf('scT')])
        for l in range(nlayers):
            for cc in range(6):
                r = rot('ring', 2)
                S.dma('pool', ring[r][:].rearrange("p (kc j) -> p kc j", j=512),
                      w_mod[l][:, cc * 512:(cc + 1) * 512].rearrange("(kc p) j -> p kc j", p=128), W=[rbuf(r)])
                S.dma('pool', bmodc[:], b_mod[l:l + 1, cc * 512:(cc + 1) * 512], W=[bf('bmodc')])
                po = rot('po', 2)
                fns = [lambda e, po=po, r=r, kc=kc: e.matmul(PO[po][0:nseq, :], lhsT=scT[:, kc, :],
                                                             rhs=ring[r][:, kc * 512:(kc + 1) * 512], start=(kc == 0), stop=False)
                       for kc in range(8)]
                fns.append(lambda e, po=po, l=l, cc=cc: e.matmul(
                    PO[po][0:nseq, :], lhsT=onesb[0:1, 0:nseq],
                    rhs=bmodc[0:1, :], start=False, stop=True))
                S.group('pe', fns, R=[bf('scT'), rbuf(r), bf('onesb'), bf('bmodc')], W=[bf('PO%d' % po)])
                S.op('dve', lambda e, po=po: e.tensor_copy(out=tC[0:nseq, 0:512], in_=PO[po][0:nseq, :]),
                     R=[bf('PO%d' % po)], W=[bf('tC')])
                S.dma('sp', modsc[l][:, cc * 512:(cc + 1) * 512], tC[0:nseq, 0:512], R=[bf('tC')], W=[bf('modsc')])

        S.dma('pool', relb[:], rel_bias, W=[bf('relb')])
        for cc in range(6):
            S.dma('pool', tmb2[0:32, :], oh_d[:, cc * 512:(cc + 1) * 512], W=[bf('tmb2')])
            S.dma('sp', tB[0:8, 0:512], mult_d[:, cc * 512:(cc + 1) * 512].partition_broadcast(8), W=[bf('tB')])
            po = rot('po', 2)
            S.op('pe', lambda e, po=po: e.matmul(PO[po][0:8, :], lhsT=relb[:], rhs=tmb2[0:32, :], start=True, stop=True),
                 R=[bf('relb'), bf('tmb2')], W=[bf('PO%d' % po)])
            S.op('act', lambda e, po=po: e.activation(out=tA[0:8, 0:512], in_=PO[po][0:8, :], func=AF.Exp),
                 R=[bf('PO%d' % po)], W=[bf('tA')])
            S.op('dve', lambda e: e.tensor_tensor(out=tmb[0:8, 0:512], in0=tA[0:8, 0:512], in1=tB[0:8, 0:512], op=ALU.mult),
                 R=[bf('tA'), bf('tB')], W=[bf('tmb')])
            S.dma('sp', wvec[:, cc * 512:(cc + 1) * 512], tmb[0:8, 0:512], R=[bf('tmb')], W=[bf('wvec')])
        S.op('dve', lambda e: e.memset(ring[0][:], 0.0), W=[rbuf(0)])
        def wchunk_k1024(name, w, c0, ncols, width=512, off=0):
            dst = wsc[CH[name]][:, 0:8 * width].rearrange("p (kc j) -> p kc j", j=width)[:, :, off:off + ncols]
            S.dma('pool', dst, w[:, c0:c0 + ncols].rearrange("(kc p) j -> p kc j", p=128), W=[bf('wsc_' + name + str(off))],
                  sembuf=bf('wsc_' + name))

        wchunk_k1024('E_KV', w_in_e, 2304, 160, width=160)
        S.dma('pool', wsc[CH['UKV']][:, 0:1024], w_ukv, W=[bf('wsc_UKV0')], sembuf=bf('wsc_UKV'))
        wchunk_k1024('E_AC', w_in_e, 512, 512)
        wchunk_k1024('E_AX', w_in_e, 1024, 512)
        for j in range(4):
            for pi, c0 in enumerate((512, 1024, 0, 1536)):
                wchunk_k1024('EA%d' % j, w_in_e, c0 + j * 128, 128, off=pi * 128)
        wchunk_k1024('E_BZ', w_in_e, 2464, 512)
        wchunk_k1024('E_CQ', w_in_e, 2048, 256, width=256)
        S.dma('pool', wsc[CH['UQ']][:, 0:1536].rearrange("p (kc j) -> p kc j", j=768),
              w_uq.rearrange("(kc p) j -> p kc j", p=128), W=[bf('wsc_UQ0')], sembuf=bf('wsc_UQ'))
        for nm, wo in (('EO', w_out_e), ('OO', w_out_o)):
            for j in range(4):
                S.dma('sp', wsc[CH['%s%d' % (nm, j)]][64:128, 1024:3072], ring[0][64:128, 1024:3072], R=[bf('ring0')],
                      W=[bf('wsc_%s%dz' % (nm, j))], sembuf=bf('wsc_%s%d' % (nm, j)))
                S.dma('pool', wsc[CH['%s%d' % (nm, j)]][:, 0:1024].rearrange("p (kc j) -> p kc j", j=256),
                      wo[0:512, j * 256:(j + 1) * 256].rearrange("(kc p) j -> p kc j", p=128),
                      W=[bf('wsc_%s%da' % (nm, j))], sembuf=bf('wsc_%s%d' % (nm, j)))
                S.dma('pool', wsc[CH['%s%d' % (nm, j)]][0:64, 1024:3072].rearrange("p (h j) -> p h j", j=256),
                      wo[512:1024, j * 256:(j + 1) * 256].rearrange("(h p) j -> p h j", p=64),
                      W=[bf('wsc_%s%db' % (nm, j))], sembuf=bf('wsc_%s%d' % (nm, j)))
        for nm, c0 in (('O_K', 2048), ('O_V', 2560), ('O_CU', 0), ('O_CV', 512), ('O_CZ', 1024), ('O_DQ', 1536),
                       ('O_DZ', 3072)):
            wchunk_k1024(nm, w_in_o, c0, 512)

        def chunk_tickets(name):
            return [(e_[0], e_[1], 'dma') for k, b in B.items() if k == 'wsc_' + name for e_ in b.sems.values()]

        for h in range(8):
            for kl in range(128):
                S.dma('sp' if kl % 2 else 'pool', wsc[CH['T%d' % h]][kl:kl + 1, 0:TW], wvec[h:h + 1, 128 - kl:128 - kl + TW],
                      R=[bf('wvec')], W=[bf('wsc_T%d_%d' % (h, kl))], sembuf=bf('wsc_T%d' % h))

        def load_chunk(name):
            r = ringcfg['slots'][rot('ring', len(ringcfg['slots']))]
            deps = chunk_tickets(name)
            nc_ = (3072 if name[:2] in ('EO', 'OO') else 2944 if name[0] == 'T' else 1024 if name == 'UKV' else 1536 if name == 'UQ'
                   else 1280 if name == 'E_KV' else 2048 if name == 'E_CQ' else 4096)
            S.dma('sp', ring[r][:, 0:nc_], wsc[CH[name]][:, 0:nc_], W=[rbuf(r)], deps=deps)
            return r

        def rstd_from(ssv, outv, n, dim):
            S.op('act', lambda e: e.activation(out=stat[:, 56:56 + n], in_=ssv, func=AF.Ln, scale=1.0 / dim, bias=EPS),
                 R=[bf('stat')], W=[bf('stat2')])
            S.op('act', lambda e: e.activation(out=outv, in_=stat[:, 56:56 + n], func=AF.Exp, scale=-0.5),
                 R=[bf('stat2')], W=[bf('stat')])

        def seq_layer_setup(l, si):
            S.dma('sp', modv[:, 0:16].rearrange("p (a j) -> p a j", j=8),
                  modsc[l, si, 0:2048].rearrange("(a j p) -> p a j", p=128, j=8), R=[bf('modsc')], W=[bf('modv')],
                  allow_slow_non_contiguous=True)
            S.dma('sp', modv[:, 24:32], norm_gT[l], W=[bf('modvg')])
            S.dma('sp', gate_bc[:], modsc[l, si:si + 1, 2048:3072].partition_broadcast(128), R=[bf('modsc')], W=[bf('gate_bc')])
            S.op('dve', lambda e: e.scalar_tensor_tensor(out=modv[:, 16:24], in0=modv[:, 8:16], scalar=1.0, in1=modv[:, 24:32],
                                                         op0=ALU.add, op1=ALU.mult), R=[bf('modv'), bf('modvg')], W=[bf('modv2')])

        pref = {'key': None, 'plan': [], 'idx': 0}

        def x_load(src, t0):
            rdep = [bf('ydram')] if src is y else []
            S.dma('sp', xin[:], src[t0:t0 + G, :].rearrange("(t p) f -> p t f", p=128), R=rdep, W=[bf('xin')], sembuf=bf('xin'))

        def stage_n(src, t0):
            if pref['key'] != (src is y, t0):
                x_load(src, t0)
            pref['key'] = None
            stage_n_compute()
            plan = pref['plan']
            if plan:
                i = pref['idx']
                assert plan[i][0:2] == (src is y, t0), (plan[i], src is y, t0)
                pref['idx'] = i + 1
                if i + 1 < len(plan) and plan[i + 1][2]:
                    nsrc = y if plan[i + 1][0] else x
                    x_load(nsrc, plan[i + 1][1])
                    pref['key'] = (plan[i + 1][0], plan[i + 1][1])

        def stage_n_compute():
            S.op('dve', lambda e: e.memset(stat[:, 0:4], 0.0), W=[bf('stat')])
            for t in range(4):
                S.op('act', lambda e, t=t: e.activation(out=tA[:], in_=xin[:, t, :], func=AF.Square, accum_out=stat[:, t:t + 1]),
                     R=[bf('xin')], W=[bf('tA'), bf('stat')])
            rstd_from(stat[:, 0:4], stat[:, 4:8], 4, float(D))
            for t in range(4):
                xi = rot('xs', 2)
                S.op('act', lambda e, t=t, xi=xi: e.activation(out=xs[xi][:], in_=xin[:, t, :], func=AF.Copy, scale=stat[:, 4 + t:5 + t]),
                     R=[bf('xin'), bf('stat')], W=[bf('xs%d' % xi)])
                pt = rot('pt', 2)
                S.group('pe', [lambda e, kc=kc, xi=xi, pt=pt: e.transpose(out=PT[pt][:, kc * 128:(kc + 1) * 128],
                                                                            in_=xs[xi][:, kc * 128:(kc + 1) * 128], identity=ident[:])
                               for kc in range(8)], R=[bf('xs%d' % xi), bf('ident')], W=[bf('PT%d' % pt)])
                S.op('dve', lambda e, pt=pt: e.tensor_tensor(out=tmb[:].rearrange("p (k j) -> p k j", j=128),
                                                             in0=PT[pt][:].rearrange("p (k j) -> p k j", j=128),
                                                             in1=modv[:, 16:24].unsqueeze(2).to_broadcast([128, 8, 128]), op=ALU.mult),
                     R=[bf('PT%d' % pt), bf('modv2')], W=[bf('tmb')])
                S.op('dve', lambda e, t=t: e.tensor_tensor(out=hT[:, :, t * 128:(t + 1) * 128],
                                                           in0=tmb[:].rearrange("p (k j) -> p k j", j=128),
                                                           in1=modv[:, 0:8].unsqueeze(2).to_broadcast([128, 8, 128]), op=ALU.add),
                     R=[bf('tmb'), bf('modv')], W=[bf('hT')])

        def proj_fm(r, c0, width, m0, msz, po):
            S.group('pe', [lambda e, kc=kc: e.matmul(PO[po][0:msz, :], lhsT=ring[r][:, kc * width + c0 + m0:kc * width + c0 + m0 + msz],
                                                     rhs=hT[:, kc, :], start=(kc == 0), stop=(kc == 7)) for kc in range(8)],
                    R=[rbuf(r), bf('hT')], W=[bf('PO%d' % po)])

        def proj_tm(r, width, ncols, t, pst, c0=0, o0=0):
            fns = []
            for n0 in range(0, ncols, 512):
                nn = min(512, ncols - n0)
                for kc in range(8):
                    fns.append(lambda e, kc=kc, n0=n0, nn=nn: e.matmul(
                        PS[pst][:, o0 + n0:o0 + n0 + nn], lhsT=hT[:, kc, t * 128:(t + 1) * 128],
                        rhs=ring[r][:, kc * width + c0 + n0:kc * width + c0 + n0 + nn], start=(kc == 0), stop=(kc == 7)))
            S.group('pe', fns, R=[rbuf(r), bf('hT')], W=[*psb(pst)])

        def rope_tm(src3, gain_bc, t_abs, nh, dst):
            cosb = cosT[:, t_abs, :].unsqueeze(1).to_broadcast([128, nh, 16])
            sinb = sinT[:, t_abs, :].unsqueeze(1).to_broadcast([128, nh, 16])
            g3 = tC[:, 0:nh * 32].rearrange("p (h d) -> p h d", d=32)
            w1 = tC[:, 256:256 + nh * 16].rearrange("p (h d) -> p h d", d=16)
            w2 = tC[:, 512:512 + nh * 16].rearrange("p (h d) -> p h d", d=16)
            S.op('dve', lambda e: e.tensor_tensor(out=g3, in0=src3, in1=gain_bc.unsqueeze(1).to_broadcast([128, nh, 32]), op=ALU.mult),
                 R=[bf('tB'), bf('rowv')], W=[bf('tC')])
            S.op('dve', lambda e: e.tensor_tensor(out=w1, in0=g3[:, :, 0:16], in1=cosb, op=ALU.mult), R=[bf('tC'), bf('cos')], W=[bf('tCw1')])
            S.op('dve', lambda e: e.tensor_tensor(out=w2, in0=g3[:, :, 16:32], in1=sinb, op=ALU.mult), R=[bf('tC'), bf('sin')], W=[bf('tCw2')])
            S.op('dve', lambda e: e.tensor_tensor(out=dst[:, :, 0:16], in0=w1, in1=w2, op=ALU.subtract), R=[bf('tCw1'), bf('tCw2')], W=[bf('ropeo')])
            S.op('dve', lambda e: e.tensor_tensor(out=w1, in0=g3[:, :, 0:16], in1=sinb, op=ALU.mult), R=[bf('tC'), bf('sin'), bf('ropeo')], W=[bf('tCw1')])
            S.op('dve', lambda e: e.tensor_tensor(out=w2, in0=g3[:, :, 16:32], in1=cosb, op=ALU.mult), R=[bf('tC'), bf('cos'), bf('ropeo')], W=[bf('tCw2')])
            S.op('dve', lambda e: e.tensor_tensor(out=dst[:, :, 16:32], in0=w1, in1=w2, op=ALU.add), R=[bf('tCw1'), bf('tCw2')], W=[bf('ropeo')])

        def head_norm(pst, ncol_h, nh, dim, dst_scaled, o0=0):
            v3 = PS[pst][:, o0:o0 + nh * ncol_h].rearrange("p (h d) -> p h d", d=ncol_h)
            a3 = tA[:, 0:nh * ncol_h].rearrange("p (h d) -> p h d", d=ncol_h)
            S.op('act', lambda e: e.activation(out=tA[:, 0:nh * ncol_h], in_=PS[pst][:, o0:o0 + nh * ncol_h], func=AF.Square),
                 W=[bf('tA'), *psb(pst)])
            if nlayers == -12:
                return v3
            S.op('dve', lambda e: e.tensor_reduce(out=stat[:, 16:16 + nh], in_=a3[:, :, 0:dim], axis=AX.X, op=ALU.add),
                 R=[bf('tA')], W=[bf('stat')])
            return v3

        def finalize_a(po, on_act=False):
            if on_act:
                S.op('act', lambda e: e.activation(out=tC[64:65, 0:512], in_=PO[po][64:65, :], func=AF.Ln), R=[bf('PO%d' % po)], W=[bf('rden')])
                S.op('act', lambda e: e.activation(out=tC[64:65, 0:512], in_=tC[64:65, 0:512], func=AF.Exp, scale=-1.0), W=[bf('rden')])
            else:
                S.op('dve', lambda e: RECIP(e, tC[64:65, 0:512], PO[po][64:65, :]), R=[bf('PO%d' % po)], W=[bf('rden')])

        def finalize_b(po, h, gate_ap, gate_buf):
            sap, sbuf_ = sc_slot(rot('sc', 4))
            S.op('pe', lambda e: e.matmul(sap[0:64, :], lhsT=onesf[64:65, 0:64], rhs=tC[64:65, 0:512], start=True, stop=True),
                 R=[bf('onesf'), bf('rden')], W=[sbuf_])
            S.op('dve', lambda e: e.tensor_tensor(out=bcs[:], in0=sap[0:64, :], in1=gate_ap, op=ALU.mult),
                 R=[sbuf_, gate_buf], W=[bf('bcs')])
            S.op('dve', lambda e: e.tensor_tensor(out=ybT[0:64, h, :], in0=PO[po][0:64, :], in1=bcs[:], op=ALU.mult),
                 R=[bf('PO%d' % po), bf('bcs')], W=[bf('ybT')])

        def out_proj2(prefix, dst, t0):
            tk = None
            for j in range(4):
                r = load_chunk('%s%d' % (prefix, j))
                tX, tXn = (tC, 'tC') if j % 2 == 0 else (tB, 'tB')
                for t in range(4):
                    pst = rot('ps', 2)
                    fns = [lambda e, kc=kc, t=t, pst=pst, r=r: e.matmul(PS[pst][:, 0:256], lhsT=ycA[:, kc, t * 128:(t + 1) * 128],
                                                                       rhs=ring[r][:, kc * 256:(kc + 1) * 256], start=(kc == 0), stop=False)
                           for kc in range(4)]
                    fns += [lambda e, h=h, t=t, pst=pst, r=r: e.matmul(PS[pst][:, 0:256], lhsT=ybT[:, h, t * 128:(t + 1) * 128],
                                                                      rhs=ring[r][:, 1024 + h * 256:1024 + (h + 1) * 256],
                                                                      start=False, stop=(h == 7)) for h in range(8)]
                    S.group('pe', fns, R=[rbuf(r), bf('ycA'), bf('ybT')], W=[*psb(pst)])
                    S.op('dve', lambda e, t=t, pst=pst, j=j, tX=tX: e.tensor_tensor(out=tX[:, t * 256:(t + 1) * 256], in0=PS[pst][:, 0:256],
                                                                                    in1=gate_bc[:, j * 256:(j + 1) * 256], op=ALU.mult),
                         R=[*psb(pst), bf('gate_bc')], W=[bf(tXn)])
                tk = S.dma('pool', dst[t0:t0 + G, j * 256:(j + 1) * 256].rearrange("(t p) f -> p t f", p=128),
                           tX[:].rearrange("p (t f) -> p t f", f=256), R=[bf(tXn)], W=[bf('ydram')], sembuf=bf('ystore%d' % (j % 2)),
                           accum_op=ALU.add)
            return tk

        def l0_pass_a(src, base, slen):
            ng = slen // G
            ringcfg['slots'] = [0, 1, 4, 5]
            S.op('dve', lambda e: e.memset(KrT[:], 0.0), W=[bf('KrT')])
            S.op('dve', lambda e: e.memset(QrT[:], 0.0), W=[bf('QrT')])
            S.op('dve', lambda e: e.memset(pedge[:], 0.0), W=[bf('pedge')])
            for g in range(ng):
                stage_n(src, base + g * G)
                r = load_chunk('E_KV')
                for t in range(4):
                    pst = rot('ps', 2)
                    proj_tm(r, 160, 160, t, pst)
                    S.op('dve', lambda e, t=t, pst=pst: e.tensor_copy(out=kvr[:, t, :], in_=PS[pst][:, 0:160]),
                         R=[*psb(pst)], W=[bf('kvr')])
                if nlayers == -5:
                    return
                rc = load_chunk('E_AC')
                rx = load_chunk('E_AX')
                for j in range(4 if nlayers != -4 else 0):
                    for which, rr in ((0, rc), (1, rx)):
                        po = rot('po', 2)
                        fns = []
                        for col in range(2):
                            cidx = col * (G - 1)
                            for kc in range(8):
                                fns.append(lambda e, kc=kc, cidx=cidx, col=col, rr=rr, po=po, j=j: e.matmul(
                                    PO[po][:, col:col + 1], lhsT=ring[rr][:, kc * 512 + j * 128:kc * 512 + (j + 1) * 128],
                                    rhs=hT[:, kc, cidx:cidx + 1], start=(kc == 0), stop=(kc == 7)))
                        S.group('pe', fns, R=[rbuf(rr), bf('hT')], W=[bf('PO%d' % po)])
                        if which == 0:
                            S.op('dve', lambda e, po=po: e.tensor_copy(out=stat[:, 32:34], in_=PO[po][:, 0:2]),
                                 R=[bf('PO%d' % po)], W=[bf('stat3')])
                        else:
                            S.op('dve', lambda e, po=po, j=j, g=g: e.tensor_tensor(out=pedge[:, j, g + 1, :], in0=PO[po][:, 0:2],
                                                                                   in1=stat[:, 32:34], op=ALU.mult),
                                 R=[bf('PO%d' % po), bf('stat3')], W=[bf('pedge')])
                S.op('dve', lambda e: e.memset(stat[:, 0:4], 0.0), W=[bf('stat')])
                for t in range(4):
                    S.op('act', lambda e, t=t: e.activation(out=tA[:, 0:128], in_=kvr[:, t, 0:128], func=AF.Square,
                                                            accum_out=stat[:, t:t + 1]), R=[bf('kvr')], W=[bf('tA'), bf('stat')])
                rstd_from(stat[:, 0:4], stat[:, 4:8], 4, 128.0)
                pt = rot('pt', 2)
                for t in range(4):
                    S.op('act', lambda e, t=t: e.activation(out=tmb2[:, t * 128:(t + 1) * 128], in_=kvr[:, t, 0:128], func=AF.Copy,
                                                            scale=stat[:, 4 + t:5 + t]), R=[bf('kvr'), bf('stat')], W=[bf('tmb2')])
                S.group('pe', [lambda e, t=t, pt=pt: e.transpose(out=PT[pt][:, t * 128:(t + 1) * 128], in_=tmb2[:, t * 128:(t + 1) * 128],
                                                                identity=ident[:]) for t in range(4)],
                        R=[bf('tmb2'), bf('ident')], W=[bf('PT%d' % pt)])
                S.op('dve', lambda e, pt=pt: e.tensor_scalar(out=ckT[:, 0, :], in0=PT[pt][:, 0:512], scalar1=colv[:, 2:3], scalar2=None,
                                                             op0=ALU.mult), R=[bf('PT%d' % pt), bf('colv')], W=[bf('ckT')])
                if nlayers == -6:
                    return
                ru = load_chunk('UKV')
                psts = {}

                def projA(t, ru=ru, psts=psts):
                    pst = rot('ps', 2)
                    psts[t] = pst
                    S.group('pe', [lambda e, n0=n0, t=t, pst=pst, ru=ru: e.matmul(PS[pst][:, n0:n0 + 512], lhsT=ckT[:, 0, t * 128:(t + 1) * 128],
                                                                          rhs=ring[ru][:, n0:n0 + 512], start=True, stop=True)
                                   for n0 in (0, 512)], R=[rbuf(ru), bf('ckT')], W=[*psb(pst)])
                projA(0)
                for t in range(4):
                    tt = (base - base) + g * 4 + t
                    if t + 1 < 4:
                        projA(t + 1)
                    pst = psts[t]
                    v3 = PS[pst][:].rearrange("p (h d) -> p h d", d=128)
                    S.op('dve', lambda e, tt=tt, v3=v3: e.tensor_copy(out=V[:, tt, 0:520].rearrange("p (h d) -> p h d", d=65)[:, :, 0:64], in_=v3[:, :, 64:128]),
                         R=[*psb(pst)], W=[bf('V')])
                    if nlayers == -8:
                        continue
                    head_norm(pst, 128, 8, 64, None)
                    if nlayers in (-11, -12):
                        continue
                    S.op('dve', lambda e: e.memset(stat[:, 24:25], 0.0), W=[bf('stat4')])
                    S.op('act', lambda e, t=t: e.activation(out=tB[:, 0:32], in_=kvr[:, t, 128:160], func=AF.Square,
                                                            accum_out=stat[:, 24:25]), R=[bf('kvr')], W=[bf('tB'), bf('stat4')])
                    S.op('dve', lambda e: e.tensor_scalar(out=stat[:, 16:24], in0=stat[:, 16:24], scalar1=stat[:, 24:25], scalar2=None,
                                                          op0=ALU.add), R=[bf('stat4')], W=[bf('stat')])
                    rstd_from(stat[:, 16:24], stat[:, 40:48], 8, 96.0)
                    S.op('dve', lambda e, tt=tt: e.tensor_scalar(out=rsK[:, tt, :], in0=stat[:, 40:48], scalar1=96.0 ** -0.5, scalar2=None,
                                                                 op0=ALU.mult), R=[bf('stat')], W=[bf('rsK')])
                    if nlayers == -9:
                        continue
                    S.op('dve', lambda e, v3=v3: e.tensor_copy(out=tmb[:, 0:512].rearrange("p (h d) -> p h d", d=64), in_=v3[:, :, 0:64]),
                         R=[*psb(pst)], W=[bf('tmb')])
                    pt = rot('pt', 2)
                    S.group('pe', [lambda e, j=j, pt=pt: e.transpose(out=PT[pt][:, j * 128:(j + 1) * 128], in_=tmb[:, j * 128:(j + 1) * 128],
                                                                    identity=ident[:]) for j in range(4)],
                            R=[bf('tmb'), bf('ident')], W=[bf('PT%d' % pt)])
                    if nlayers == -10:
                        continue
                    S.op('dve', lambda e, pt=pt, tt=tt: e.tensor_scalar(out=KT[:, :, tt * 128:(tt + 1) * 128],
                                                                        in0=PT[pt][:, 0:512].rearrange("p (j c) -> p j c", c=128),
                                                                        scalar1=colv[:, 3:4], scalar2=None, op0=ALU.mult),
                         R=[bf('PT%d' % pt), bf('colv')], W=[bf('KT')])
                    if nlayers == -7:
                        continue
                    S.op('dve', lambda e, t=t: e.tensor_copy(out=tB[:, 64:96], in_=kvr[:, t, 128:160]), R=[bf('kvr')], W=[bf('tB')])
                    rope_tm(tB[:, 64:96].rearrange("p (h d) -> p h d", d=32), rowv[:, 32:64], tt, 1,
                            tmb2[:, 0:32].rearrange("p (h d) -> p h d", d=32))
                    pt = rot('pt', 2)
                    S.op('pe', lambda e, pt=pt: e.transpose(out=PT[pt][0:32, 0:128], in_=tmb2[:, 0:32], identity=ident[:]),
                         R=[bf('ropeo'), bf('ident')], W=[bf('PT%d' % pt)])
                    S.op('dve', lambda e, pt=pt, tt=tt: e.tensor_copy(out=KrT[0:32, tt * 128:(tt + 1) * 128], in_=PT[pt][0:32, 0:128]),
                         R=[bf('PT%d' % pt)], W=[bf('KrT')])

        def l0_pass_b(src, dst, base, slen):
            ng = slen // G
            nt = slen // 128
            ringcfg['slots'] = [0, 1]
            S.op('dve', lambda e: e.memset(QT[:], 0.0), W=[bf('QT')])
            S.op('dve', lambda e: e.memset(ybT[:], 0.0), W=[bf('ybT')])
            for g in range(ng):
                stage_n(src, base + g * G)
                t0c = base + g * G
                S.dma('sp', dst[t0c:t0c + G, :], src[t0c:t0c + G, :], W=[bf('ydram')], sembuf=bf('ycopy'))
                for j in range(4):
                    r = load_chunk('EA%d' % j)
                    for part in range(4):
                        po = rot('po', 2)
                        proj_fm(r, 0, 512, part * 128, 128, po)
                        if part == 0:
                            S.op('act', lambda e, po=po: e.activation(out=b1[:, 0:512], in_=PO[po][:, :], func=AF.Copy),
                                 R=[bf('PO%d' % po)], W=[bf('b1')])
                        elif part == 1:
                            S.op('dve', lambda e, po=po: e.tensor_tensor(out=pcv[:, 1:513], in0=PO[po][:, :], in1=b1[:, 0:512], op=ALU.mult),
                                 R=[bf('PO%d' % po), bf('b1')], W=[bf('pcv')])
                            S.op('dve', lambda e, j=j, g=g: e.tensor_copy(out=pcv[:, 0:1], in_=pedge[:, j, g, 1:2]),
                                 R=[bf('pedge')], W=[bf('pcv')])
                            S.op('dve', lambda e, j=j, g=g: e.tensor_copy(out=pcv[:, 513:514], in_=pedge[:, j, g + 2, 0:1]),
                                 R=[bf('pedge')], W=[bf('pcv')])
                            S.op('dve', lambda e, j=j: e.tensor_scalar(out=tA[:, 0:512], in0=pcv[:, 1:513], scalar1=colv[:, 8 + j * 3 + 1:8 + j * 3 + 2],
                                                                       scalar2=None, op0=ALU.mult), R=[bf('pcv'), bf('colv')], W=[bf('tA')])
                            S.op('dve', lambda e, j=j: e.scalar_tensor_tensor(out=tA[:, 0:512], in0=pcv[:, 0:512], scalar=colv[:, 8 + j * 3:8 + j * 3 + 1],
                                                                              in1=tA[:, 0:512], op0=ALU.mult, op1=ALU.add),
                                 R=[bf('pcv'), bf('colv')], W=[bf('tA')])
                            S.op('dve', lambda e, j=j: e.scalar_tensor_tensor(out=tA[:, 0:512], in0=pcv[:, 2:514], scalar=colv[:, 8 + j * 3 + 2:8 + j * 3 + 3],
                                                                              in1=tA[:, 0:512], op0=ALU.mult, op1=ALU.add),
                                 R=[bf('pcv'), bf('colv')], W=[bf('tA')])
                        elif part == 2:
                            S.op('dve', lambda e, po=po: e.tensor_tensor(out=tA[:, 512:1024], in0=PO[po][:, :], in1=tA[:, 0:512], op=ALU.mult),
                                 R=[bf('PO%d' % po), bf('tA')], W=[bf('tA2')])
                        else:
                            S.op('act', lambda e, po=po: e.activation(out=tB[:, 0:512], in_=PO[po][:, :], func=AF.Sigmoid),
                                 R=[bf('PO%d' % po)], W=[bf('tB')])
                            S.op('dve', lambda e, po=po: e.tensor_tensor(out=tB[:, 512:1024], in0=PO[po][:, :], in1=tB[:, 0:512], op=ALU.mult),
                                 R=[bf('PO%d' % po), bf('tB')], W=[bf('tB2')])
                            S.op('dve', lambda e, j=j: e.tensor_tensor(out=ycA[:, j, :], in0=tA[:, 512:1024], in1=tB[:, 512:1024], op=ALU.mult),
                                 R=[bf('tA2'), bf('tB2')], W=[bf('ycA')])
                r = load_chunk('E_BZ')
                for h in range(8):
                    po = rot('po', 2)
                    proj_fm(r, 0, 512, h * 64, 64, po)
                    S.op('act', lambda e, po=po: e.activation(out=tB[0:64, 0:512], in_=PO[po][0:64, :], func=AF.Sigmoid),
                         R=[bf('PO%d' % po)], W=[bf('tB')])
                    S.op('dve', lambda e, po=po, h=h: e.tensor_tensor(out=gz2[0:64, h, :], in0=PO[po][0:64, :], in1=tB[0:64, 0:512], op=ALU.mult),
                         R=[bf('PO%d' % po), bf('tB')], W=[bf('gz2')])
                r = load_chunk('E_CQ')
                S.op('dve', lambda e: e.memset(stat[:, 0:4], 0.0), W=[bf('stat')])
                pq = []
                for t in range(4):
                    pst = rot('ps', 2)
                    proj_tm(r, 256, 256, t, pst)
                    S.op('dve', lambda e, t=t, pst=pst: e.tensor_copy(out=tC[:, t * 256:(t + 1) * 256], in_=PS[pst][:, 0:256]),
                         R=[*psb(pst)], W=[bf('tCq')])
                    S.op('act', lambda e, t=t: e.activation(out=tA[:, 0:256], in_=tC[:, t * 256:(t + 1) * 256], func=AF.Square,
                                                            accum_out=stat[:, t:t + 1]), R=[bf('tCq')], W=[bf('tA'), bf('stat')])
                rstd_from(stat[:, 0:4], stat[:, 4:8], 4, 256.0)
                for t in range(4):
                    S.op('act', lambda e, t=t: e.activation(out=tmb[:, t * 256:(t + 1) * 256], in_=tC[:, t * 256:(t + 1) * 256], func=AF.Copy,
                                                            scale=stat[:, 4 + t:5 + t]), R=[bf('tCq'), bf('stat')], W=[bf('tmb')])
                for kc in range(2):
                    pt = rot('pt', 2)
                    S.group('pe', [lambda e, t=t, kc=kc, pt=pt: e.transpose(out=PT[pt][:, t * 128:(t + 1) * 128],
                                                                           in_=tmb[:, t * 256 + kc * 128:t * 256 + (kc + 1) * 128], identity=ident[:])
                                   for t in range(4)], R=[bf('tmb'), bf('ident')], W=[bf('PT%d' % pt)])
                    S.op('dve', lambda e, kc=kc, pt=pt: e.tensor_scalar(out=ckT[:, kc, :], in0=PT[pt][:, 0:512], scalar1=colv[:, kc:kc + 1],
                                                                        scalar2=None, op0=ALU.mult), R=[bf('PT%d' % pt), bf('colv')], W=[bf('ckT')])
                r = load_chunk('UQ')
                psts = {}

                def projA(t, r=r, psts=psts):
                    pst = rot('ps', 2)
                    psts[t] = pst
                    fns = []
                    for n0, nn in ((0, 512), (512, 256)):
                        for kc in range(2):
                            fns.append(lambda e, kc=kc, n0=n0, nn=nn, t=t, pst=pst, r=r: e.matmul(
                                PS[pst][:, n0:n0 + nn], lhsT=ckT[:, kc, t * 128:(t + 1) * 128],
                                rhs=ring[r][:, kc * 768 + n0:kc * 768 + n0 + nn], start=(kc == 0), stop=(kc == 1)))
                    S.group('pe', fns, R=[rbuf(r), bf('ckT')], W=[*psb(pst)])
                projA(0)
                for t in range(4):
                    tt = g * 4 + t
                    if t + 1 < 4:
                        projA(t + 1)
                    pst = psts[t]
                    v3 = head_norm(pst, 96, 8, 96, None)
                    rstd_from(stat[:, 16:24], stat[:, 40:48], 8, 96.0)
                    S.op('dve', lambda e, v3=v3: e.tensor_tensor(out=tB[:, 0:768].rearrange("p (h d) -> p h d", d=96), in0=v3,
                                                                 in1=stat[:, 40:48].unsqueeze(2).to_broadcast([128, 8, 96]), op=ALU.mult),
                         R=[*psb(pst), bf('stat')], W=[bf('tB')])
                    q3 = tB[:, 0:768].rearrange("p (h d) -> p h d", d=96)
                    S.op('dve', lambda e, q3=q3: e.tensor_copy(out=tmb[:, 0:512].rearrange("p (h d) -> p h d", d=64), in_=q3[:, :, 0:64]),
                         R=[bf('tB')], W=[bf('tmb')])
                    rope_tm(q3[:, :, 64:96], rowv[:, 0:32], tt, 8, tmb2[:, 0:256].rearrange("p (h d) -> p h d", d=32))
                    pt = rot('pt', 2)
                    S.group('pe', [lambda e, j=j, pt=pt: e.transpose(out=PT[pt][:, j * 128:(j + 1) * 128], in_=tmb[:, j * 128:(j + 1) * 128],
                                                                    identity=ident[:]) for j in range(4)],
                            R=[bf('tmb'), bf('ident')], W=[bf('PT%d' % pt)])
                    S.op('dve', lambda e, pt=pt, t=t: e.tensor_scalar(
                        out=QT[:].rearrange("p (j two) c -> p j two c", two=2)[0:64, :, 0, t * 128:(t + 1) * 128],
                        in0=PT[pt][0:64, 0:512].rearrange("p (j c) -> p j c", c=128), scalar1=colv[0:64, 4:5], scalar2=None, op0=ALU.mult),
                         R=[bf('PT%d' % pt), bf('colv')], W=[bf('QT')])
                    S.op('dve', lambda e, pt=pt, t=t: e.tensor_scalar(
                        out=QT[:].rearrange("p (j two) c -> p j two c", two=2)[64:128, :, 1, t * 128:(t + 1) * 128],
                        in0=PT[pt][64:128, 0:512].rearrange("p (j c) -> p j c", c=128), scalar1=colv[64:128, 4:5], scalar2=None, op0=ALU.mult),
                         R=[bf('PT%d' % pt), bf('colv')], W=[bf('QT')])
                    pt = rot('pt', 2)
                    S.group('pe', [lambda e, h=h, pt=pt: e.transpose(out=PT[pt][0:32, h * 128:(h + 1) * 128], in_=tmb2[:, h * 32:(h + 1) * 32],
                                                                    identity=ident[:]) for h in range(8)],
                            R=[bf('ropeo'), bf('ident')], W=[bf('PT%d' % pt)])
                    S.op('dve', lambda e, pt=pt, t=t: e.tensor_copy(out=QrT[0:32, :, t * 128:(t + 1) * 128],
                                                                    in_=PT[pt][0:32, :].rearrange("p (h c) -> p h c", c=128)),
                         R=[bf('PT%d' % pt)], W=[bf('QrT')])
                items = []
                for h in range(8):
                    hb, j = (h % 2) * 64, h // 2
                    po = rot('po', 2)
                    for kt in range(nt):
                        sl = rot('sc', 4)

                        def front(h=h, hb=hb, j=j, kt=kt, sl=sl):
                            sap, sbuf_ = sc_slot(sl)
                            pap, pbuf_ = p_slot(sl)
                            S.group('pe', [
                                lambda e: e.matmul(sap, lhsT=KT[:, j, kt * 128:(kt + 1) * 128], rhs=QT[:, h, :],
                                                   start=True, stop=False),
                                lambda e: e.matmul(sap, lhsT=KrT[:, kt * 128:(kt + 1) * 128], rhs=QrT[:, h, :], start=False, stop=True)],
                                R=[bf('KT'), bf('KrT'), bf('QT'), bf('QrT')], W=[sbuf_])
                            S.op('act', lambda e: e.activation(out=pap, in_=sap, func=AF.Exp, scale=rsK[:, kt, h:h + 1]),
                                 R=[sbuf_, bf('rsK')], W=[pbuf_])

                        def back(h=h, kt=kt, sl=sl, po=po, nt=nt):
                            pap, pbuf_ = p_slot(sl)
                            S.op('pe', lambda e: e.matmul(PO[po][:, :], lhsT=V[:, kt, h * 65:h * 65 + 128], rhs=pap, start=(kt == 0), stop=(kt == nt - 1)),
                                 R=[bf('V'), pbuf_], W=[bf('PO%d' % po)])
                            if kt == nt - 1:
                                return (lambda: finalize_a(po), lambda: finalize_b(po, h, gz2[0:64, h, :], bf('gz2')))
                            return None
                        items.append((front, back))
                run_pipeline(items, depth=3, delay=max(0, min(6, nt - 3)))
                out_proj2('EO', dst, base + g * G)

        def l1_pass_a(src, base, slen):
            ng = slen // G
            ringcfg['slots'] = [0, 1, 2, 3]
            for g in range(ng):
                stage_n(src, base + g * G)
                rk = load_chunk('O_K')
                rv = load_chunk('O_V')
                psts = {}

                def projA(t, g=g, rv=rv, rk=rk, psts=psts):
                    tt = g * 4 + t
                    pst = rot('ps', 2)
                    proj_tm(rv, 512, 512, t, pst)
                    S.op('dve', lambda e, tt=tt, pst=pst: e.tensor_copy(out=V[:, tt, 0:520].rearrange("p (h d) -> p h d", d=65)[:, :, 0:64],
                                                                       in_=PS[pst][:, 0:512].rearrange("p (h d) -> p h d", d=64)),
                         R=[*psb(pst)], W=[bf('V')])
                    psts[t] = pst
                    proj_tm(rk, 512, 512, t, pst, o0=512)
                projA(0)
                for t in range(4):
                    tt = g * 4 + t
                    if t + 1 < 4:
                        projA(t + 1)
                    pst = psts[t]
                    v3 = head_norm(pst, 64, 8, 64, None, o0=512)
                    rstd_from(stat[:, 16:24], stat[:, 40:48], 8, 64.0)
                    S.op('dve', lambda e, v3=v3: e.tensor_tensor(out=tmb[:, 0:512].rearrange("p (h d) -> p h d", d=64), in0=v3,
                                                                 in1=stat[:, 40:48].unsqueeze(2).to_broadcast([128, 8, 64]), op=ALU.mult),
                         R=[*psb(pst), bf('stat')], W=[bf('tmb')])
                    pt = rot('pt', 2)
                    S.group('pe', [lambda e, j=j, pt=pt: e.transpose(out=PT[pt][:, j * 128:(j + 1) * 128], in_=tmb[:, j * 128:(j + 1) * 128],
                                                                    identity=ident[:]) for j in range(4)],
                            R=[bf('tmb'), bf('ident')], W=[bf('PT%d' % pt)])
                    S.op('dve', lambda e, pt=pt, tt=tt: e.tensor_scalar(out=KT[:, :, tt * 128:(tt + 1) * 128],
                                                                        in0=PT[pt][:, 0:512].rearrange("p (j c) -> p j c", c=128),
                                                                        scalar1=colv[:, 5:6], scalar2=None, op0=ALU.mult),
                         R=[bf('PT%d' % pt), bf('colv')], W=[bf('KT')])

        def gelu_parts(po, dst_f32, tmp):
            S.op('act', lambda e: e.activation(out=tmp, in_=PO[po][:, :], func=AF.Square), R=[bf('PO%d' % po)], W=[bf('gl1')])
            S.op('dve', lambda e: e.tensor_scalar(out=tmp, in0=tmp, scalar1=0.044715 * 1.5957691216, scalar2=1.5957691216, op0=ALU.mult,
                                                  op1=ALU.add), R=[bf('gl1')], W=[bf('gl1')])
            S.op('dve', lambda e: e.tensor_tensor(out=tmp, in0=tmp, in1=PO[po][:, :], op=ALU.mult), R=[bf('gl1'), bf('PO%d' % po)], W=[bf('gl1')])
            S.op('act', lambda e: e.activation(out=dst_f32, in_=tmp, func=AF.Sigmoid), R=[bf('gl1')], W=[bf('gl2')])

        def l1_pass_b(src, dst, base, slen):
            ng = slen // G
            nt = slen // 128
            for g in range(ng):
                stage_n(src, base + g * G)
                S.fence([(e_[0], e_[1], 'dma') for bb in (bf('ystore0'), bf('ystore1')) for e_ in bb.sems.values()])
                r_cu = load_chunk('O_CU')
                r_cz = load_chunk('O_CZ')
                r_cv = load_chunk('O_CV')
                csets = [(tA[:, 0:512], tA[:, 512:1024], tmb[:, 0:512], stat[:, 0:8]),
                         (tB[:, 0:512], tB[:, 512:1024], tmb[:, 512:1024], stat[:, 8:16]),
                         (tC[:, 0:512], tC[:, 512:1024], tmb2[:, 0:512], stat[:, 24:32])]
                GA, GB = 0.044715 * 1.5957691216, 1.5957691216

                def skew(chains, lag):
                    nst = max(len(c_) + jj * lag for jj, c_ in enumerate(chains))
                    for step in range(nst):
                        for jj, c_ in enumerate(chains):
                            si_ = step - jj * lag
                            if 0 <= si_ < len(c_):
                                c_[si_]()

                def cu_chain(j, k, r_cu=r_cu):
                    X, Y, _, _ = csets[k]
                    bX, bY = bf('cX%d' % k), bf('cY%d' % k)
                    box = {}

                    def s0():
                        box['sap'], box['sb'] = sc_slot(rot('sc', 4))
                        sap = box['sap']
                        S.group('pe', [lambda e, kc=kc: e.matmul(sap, lhsT=ring[r_cu][:, kc * 512 + j * 128:kc * 512 + (j + 1) * 128],
                                                                 rhs=hT[:, kc, :], start=(kc == 0), stop=(kc == 7)) for kc in range(8)],
                                R=[rbuf(r_cu), bf('hT')], W=[box['sb']])
                    return [
                        s0,
                        lambda: S.op('act', lambda e: e.activation(out=X, in_=box['sap'], func=AF.Square), R=[box['sb']], W=[bX]),
                        lambda: S.op('dve', lambda e: e.tensor_scalar(out=X, in0=X, scalar1=GA, scalar2=GB, op0=ALU.mult, op1=ALU.add), W=[bX]),
                        lambda: S.op('dve', lambda e: e.tensor_tensor(out=X, in0=X, in1=box['sap'], op=ALU.mult), R=[box['sb']], W=[bX]),
                        lambda: S.op('act', lambda e: e.activation(out=Y, in_=X, func=AF.Sigmoid), R=[bX], W=[bY]),
                        lambda: S.op('dve', lambda e: e.tensor_tensor(out=gz[:, j, :], in0=box['sap'], in1=Y, op=ALU.mult),
                                     R=[box['sb'], bY], W=[bf('gz')]),
                    ]

                def cz_chain(j, k, r_cz=r_cz):
                    X, Y, _, _ = csets[k]
                    bX, bY = bf('cX%d' % k), bf('cY%d' % k)
                    box = {}

                    def s0():
                        box['sap'], box['sb'] = sc_slot(rot('sc', 4))
                        sap = box['sap']
                        S.group('pe', [lambda e, kc=kc: e.matmul(sap, lhsT=ring[r_cz][:, kc * 512 + j * 128:kc * 512 + (j + 1) * 128],
                                                                 rhs=hT[:, kc, :], start=(kc == 0), stop=(kc == 7)) for kc in range(8)],
                                R=[rbuf(r_cz), bf('hT')], W=[box['sb']])
                    return [
                        s0,
                        lambda: S.op('act', lambda e: e.activation(out=Y, in_=box['sap'], func=AF.Sigmoid), R=[box['sb']], W=[bY]),
                        lambda: S.op('dve', lambda e: e.tensor_tensor(out=X, in0=box['sap'], in1=Y, op=ALU.mult), R=[box['sb'], bY], W=[bX]),
                        lambda: S.op('dve', lambda e: e.tensor_tensor(out=gz[:, j, :], in0=gz[:, j, :], in1=X, op=ALU.mult), R=[bX], W=[bf('gz')]),
                    ]

                def cv_chain(t, k, r_cv=r_cv):
                    X, Y, VV, st = csets[k]
                    bX, bY, bV, bS = bf('cX%d' % k), bf('cY%d' % k), bf('cV%d' % k), bf('cS%d' % k)
                    box = {}

                    def s0():
                        box['sap'], box['sb'] = sc_slot(rot('sc', 4))
                        sap = box['sap']
                        S.group('pe', [lambda e, kc=kc: e.matmul(sap, lhsT=hT[:, kc, t * 128:(t + 1) * 128],
                                                                 rhs=ring[r_cv][:, kc * 512:(kc + 1) * 512], start=(kc == 0), stop=(kc == 7))
                                       for kc in range(8)], R=[rbuf(r_cv), bf('hT')], W=[box['sb']])

                    def s6():
                        S.op('dve', lambda e: e.memset(st[:, 1:2], 0.0), W=[bS])
                        S.op('dve', lambda e: e.tensor_reduce(out=st[:, 0:1], in_=X, axis=AX.X, op=ALU.add), R=[bX], W=[bS])

                    def spatial(gi):
                        def f():
                            po = rot('po', 2)
                            S.group('pe', [
                                lambda e: e.matmul(PO[po][:, 0:128], lhsT=VV[:, gi * 128:(gi + 1) * 128], rhs=wsT[:, gi * 128:(gi + 1) * 128],
                                                   start=True, stop=False),
                                lambda e: e.matmul(PO[po][:, 0:128], lhsT=onesf[0:1, :], rhs=cbs[0:1, gi * 128:(gi + 1) * 128],
                                                   start=False, stop=True)],
                                R=[bV, bf('wsT'), bf('onesf'), bf('cbs')], W=[bf('PO%d' % po)])
                            S.op('dve', lambda e: e.tensor_tensor(out=ycA[:, gi, t * 128:(t + 1) * 128], in0=PO[po][:, 0:128],
                                                                  in1=gz[:, gi, t * 128:(t + 1) * 128], op=ALU.mult),
                                 R=[bf('PO%d' % po), bf('gz')], W=[bf('ycA')])
                        return f
                    return [
                        s0,
                        lambda: S.op('act', lambda e: e.activation(out=X, in_=box['sap'], func=AF.Square), R=[box['sb']], W=[bX]),
                        lambda: S.op('dve', lambda e: e.tensor_scalar(out=X, in0=X, scalar1=GA, scalar2=GB, op0=ALU.mult, op1=ALU.add), W=[bX]),
                        lambda: S.op('dve', lambda e: e.tensor_tensor(out=X, in0=X, in1=box['sap'], op=ALU.mult), R=[box['sb']], W=[bX]),
                        lambda: S.op('act', lambda e: e.activation(out=Y, in_=X, func=AF.Sigmoid), R=[bX], W=[bY]),
                        lambda: S.op('dve', lambda e: e.tensor_tensor(out=X, in0=box['sap'], in1=Y, op=ALU.mult), R=[box['sb'], bY], W=[bX]),
                        s6,
                        lambda: S.op('dve', lambda e: e.tensor_scalar(out=st[:, 2:3], in0=st[:, 0:1], scalar1=-1.0 / 512, scalar2=None, op0=ALU.mult),
                                     W=[bS]),
                        lambda: S.op('dve', lambda e: e.tensor_scalar(out=X, in0=X, scalar1=st[:, 2:3], scalar2=None, op0=ALU.add), R=[bS], W=[bX]),
                        lambda: S.op('act', lambda e: e.activation(out=Y, in_=X, func=AF.Square, accum_out=st[:, 1:2]), R=[bX], W=[bY, bS]),
                        lambda: S.op('act', lambda e: e.activation(out=st[:, 4:5], in_=st[:, 1:2], func=AF.Ln, scale=1.0 / 512, bias=EPS), W=[bS]),
                        lambda: S.op('act', lambda e: e.activation(out=st[:, 3:4], in_=st[:, 4:5], func=AF.Exp, scale=-0.5), W=[bS]),
                        lambda: S.op('dve', lambda e: e.scalar_tensor_tensor(out=X, in0=X, scalar=st[:, 3:4], in1=rowv[:, 64:576],
                                                                             op0=ALU.mult, op1=ALU.mult), R=[bS, bf('rowv')], W=[bX]),
                        lambda: S.op('dve', lambda e: e.tensor_tensor(out=VV, in0=X, in1=rowv[:, 576:1088], op=ALU.add), R=[bX, bf('rowv')], W=[bV]),
                        spatial(0), spatial(1), spatial(2), spatial(3),
                    ]

                skew([cu_chain(j, j % 3) for j in range(4)], 2)
                skew([cz_chain(j, (j + 1) % 3) for j in range(4)], 2)
                skew([cv_chain(t, (t + 2) % 3) for t in range(4)], 6)
                S.fence()
                r = load_chunk('O_DZ')
                for h in range(8):
                    po = rot('po', 2)
                    proj_fm(r, 0, 512, h * 64, 64, po)
                    S.op('act', lambda e, po=po: e.activation(out=tB[0:64, 0:512], in_=PO[po][0:64, :], func=AF.Sigmoid),
                         R=[bf('PO%d' % po)], W=[bf('tB')])
                    S.op('dve', lambda e, po=po, h=h: e.tensor_tensor(out=gz2[0:64, h, :], in0=PO[po][0:64, :], in1=tB[0:64, 0:512], op=ALU.mult),
                         R=[bf('PO%d' % po), bf('tB')], W=[bf('gz2')])
                r = load_chunk('O_DQ')
                psts = {}

                def projA(t, r=r, psts=psts):
                    pst = rot('ps', 2)
                    psts[t] = pst
                    proj_tm(r, 512, 512, t, pst)
                projA(0)
                for t in range(4):
                    if t + 1 < 4:
                        projA(t + 1)
                    pst = psts[t]
                    v3 = head_norm(pst, 64, 8, 64, None)
                    rstd_from(stat[:, 16:24], stat[:, 40:48], 8, 64.0)
                    S.op('dve', lambda e, v3=v3: e.tensor_tensor(out=tmb[:, 0:512].rearrange("p (h d) -> p h d", d=64), in0=v3,
                                                                 in1=stat[:, 40:48].unsqueeze(2).to_broadcast([128, 8, 64]), op=ALU.mult),
                         R=[*psb(pst), bf('stat')], W=[bf('tmb')])
                    pt = rot('pt', 2)
                    S.group('pe', [lambda e, j=j, pt=pt: e.transpose(out=PT[pt][:, j * 128:(j + 1) * 128], in_=tmb[:, j * 128:(j + 1) * 128],
                                                                    identity=ident[:]) for j in range(4)],
                            R=[bf('tmb'), bf('ident')], W=[bf('PT%d' % pt)])
                    S.op('dve', lambda e, pt=pt, t=t: e.tensor_scalar(
                        out=QT[:].rearrange("p (j two) c -> p j two c", two=2)[0:64, :, 0, t * 128:(t + 1) * 128],
                        in0=PT[pt][0:64, 0:512].rearrange("p (j c) -> p j c", c=128), scalar1=colv[0:64, 6:7], scalar2=None, op0=ALU.mult),
                         R=[bf('PT%d' % pt), bf('colv')], W=[bf('QT')])
                    S.op('dve', lambda e, pt=pt, t=t: e.tensor_scalar(
                        out=QT[:].rearrange("p (j two) c -> p j two c", two=2)[64:128, :, 1, t * 128:(t + 1) * 128],
                        in0=PT[pt][64:128, 0:512].rearrange("p (j c) -> p j c", c=128), scalar1=colv[64:128, 6:7], scalar2=None, op0=ALU.mult),
                         R=[bf('PT%d' % pt), bf('colv')], W=[bf('QT')])
                items = []
                for h in range(8):
                    hb, j = (h % 2) * 64, h // 2
                    po = rot('po', 2)
                    kts = [kt for kt in range(4 * g - 8, 4 * g + 12) if 0 <= kt < nt]
                    rtbox = {}
                    for ki, kt in enumerate(kts):
                        sl = rot('sc', 4)
                        c0 = (4 * g - kt) * 128 + C0

                        def front(h=h, hb=hb, j=j, kt=kt, ki=ki, sl=sl, c0=c0, rtbox=rtbox):
                            if ki == 0:
                                rtbox['rt'] = load_chunk('T%d' % h)
                            rt = rtbox['rt']
                            sap, sbuf_ = sc_slot(sl)
                            pap, pbuf_ = p_slot(sl)
                            S.op('pe', lambda e: e.matmul(sap, lhsT=KT[:, j, kt * 128:(kt + 1) * 128], rhs=QT[:, h, :],
                                                          start=True, stop=True), R=[bf('KT'), bf('QT')], W=[sbuf_])
                            S.op('act', lambda e: e.activation(out=pap, in_=sap, func=AF.Exp, scale=0.125), R=[sbuf_], W=[pbuf_])
                            S.op('dve',
                                 lambda e: e.tensor_tensor(out=pap, in0=pap, in1=ring[rt][:, c0:c0 + 512], op=ALU.mult),
                                 R=[rbuf(rt)], W=[pbuf_])

                        def back(h=h, kt=kt, ki=ki, sl=sl, po=po, n=len(kts)):
                            pap, pbuf_ = p_slot(sl)
                            S.op('pe', lambda e: e.matmul(PO[po][:, :], lhsT=V[:, kt, h * 65:h * 65 + 128], rhs=pap, start=(ki == 0), stop=(ki == n - 1)),
                                 R=[bf('V'), pbuf_], W=[bf('PO%d' % po)])
                            if ki == n - 1:
                                return (lambda: finalize_a(po, True), lambda: finalize_b(po, h, gz2[0:64, h, :], bf('gz2')))
                            return None
                        items.append((front, back))
                run_pipeline(items, depth=3, delay=max(0, min(6, min(12, nt) - 3)))
                out_proj2('OO', dst, base + g * G)

        if nlayers == 2:
            b0 = 0
            for slen in seqs:
                for ps_i, (isy, pf_first) in enumerate(((False, True), (False, True), (True, False), (True, True))):
                    for g in range(slen // G):
                        pref['plan'].append((isy, b0 + g * G, True if g > 0 else pf_first))
                b0 += slen
        base = 0
        last = None
        for si, slen in enumerate(seqs):
            if nlayers == 0:
                break
            seq_layer_setup(0, si)
            if nlayers == -3:
                break
            if nlayers == -2:
                stage_n(x, base)
                break
            l0_pass_a(x, base, slen)
            if nlayers < 0:
                break
            last = l0_pass_b(x, y, base, slen)
            if nlayers > 1:
                seq_layer_setup(1, si)
                l1_pass_a(y, base, slen)
                l1_pass_b(y, y, base, slen)
            base += slen

        S.sbuf_left = nc.sbuf_bytes_remaining
        block = es.enter_context(nc.Block())
        fin = [(e_[0], e_[1], 'dma') for b in B.values() for e_ in b.sems.values()]
        S.run(block, fin)
    return nc, S


_CACHE = {}


def _prep_shared(inp):
    hc = _host_consts()
    f = lambda a: np.ascontiguousarray(np.asarray(a, dtype=np.float32))
    colv = np.zeros((128, 24), np.float32)
    colv[:, 0:2] = f(inp['mla_q_norm'])[0].reshape(2, 128).T
    colv[:, 2] = f(inp['mla_kv_norm'])[0]
    colv[:, 3] = np.tile(f(inp['mla_k_gain'])[0][0:64], 2)
    colv[:, 4] = np.tile(f(inp['mla_q_gain'])[0][0:64], 2)
    colv[:, 5] = np.tile(f(inp['d_k_gain'])[0], 2)
    colv[:, 6] = np.tile(f(inp['d_q_gain'])[0], 2)
    ac = f(inp['a_conv'])[0]
    for j in range(4):
        colv[:, 8 + j * 3:8 + j * 3 + 3] = ac[:, j * 128:(j + 1) * 128].T
    rowv = np.zeros((1, 1088), np.float32)
    rowv[0, 0:32] = f(inp['mla_q_gain'])[0][64:96]
    rowv[0, 32:64] = f(inp['mla_k_gain'])[0][64:96]
    rowv[0, 64:576] = f(inp['c_vnorm_g'])[0]
    rowv[0, 576:1088] = f(inp['c_vnorm_b'])[0]
    cws = f(inp['c_ws'])[0]
    c_wsT = np.ascontiguousarray(cws.transpose(2, 0, 1).reshape(128, 512))
    shared = dict(
        norm_gT=np.ascontiguousarray(f(inp['norm_g']).reshape(2, 8, 128).transpose(0, 2, 1)),
        w_mod=f(inp['w_mod']), b_mod=f(inp['b_mod']), rel_bias=f(inp['rel_bias']),
        w_in_e=f(inp['w_in_e'])[0], w_uq=f(inp['mla_w_uq'])[0], w_ukv=f(inp['mla_w_ukv'])[0], w_out_e=f(inp['w_out_e'])[0],
        w_in_o=f(inp['w_in_o'])[0], w_out_o=f(inp['w_out_o'])[0], colv=colv, rowv=rowv,
        c_bs=np.ascontiguousarray(f(inp['c_bs'])[0].reshape(1, 512)), c_wsT=c_wsT,
        cos=hc['cos'], sin=hc['sin'], oh=hc['oh'], mult=hc['mult'], ident=hc['ident'])
    return shared


def kernel(**inputs):
    xp = np.asarray(inputs['x_prompt'], dtype=np.float32)
    xsm = np.asarray(inputs['x_sample'], dtype=np.float32)
    cp = np.asarray(inputs['c_prompt'], dtype=np.float32)
    cs = np.asarray(inputs['c_sample'], dtype=np.float32)
    ncore = 8
    seqs = [xp.shape[1]] + [xsm.shape[1]] * 4
    key = tuple(seqs)
    if key not in _CACHE:
        _CACHE[key] = build(seqs)[0]
    nc = _CACHE[key]
    shared = _prep_shared(inputs)
    in_maps = []
    for c in range(ncore):
        xc = np.concatenate([xp[c]] + [xsm[4 * c + i] for i in range(4)], axis=0)
        cc = np.concatenate([cp[c:c + 1], cs[4 * c:4 * c + 4]], axis=0)
        m = dict(shared)
        m['x'] = np.ascontiguousarray(xc)
        m['cT'] = np.ascontiguousarray(cc.T)
        in_maps.append(m)
    res = run_bass_kernel_spmd(nc, in_maps, core_ids=list(range(ncore)))
    yp = np.empty_like(xp)
    ys = np.empty_like(xsm)
    L = xp.shape[1]
    Ls = xsm.shape[1]
    for c in range(ncore):
        yc = np.asarray(res.results[c]['y'])
        yp[c] = yc[0:L]
        for i in range(4):
            ys[4 * c + i] = yc[L + i * Ls:L + (i + 1) * Ls]
    return (yp, ys)
```

```python
import numpy as np
import concourse.bass as bass
import concourse.mybir as mybir
from concourse.bass_utils import run_bass_kernel_spmd
from contextlib import ExitStack

F32 = mybir.dt.float32
BF16 = mybir.dt.bfloat16
AF = mybir.ActivationFunctionType
ALU = mybir.AluOpType
AX = mybir.AxisListType
EPS = 1e-6
D = 1024
G = 512
FAST_RECIP = False


def RECIP(e, out, in_):
    if FAST_RECIP:
        return e.reciprocal_approx_fast(out=out, in_=in_)
    return e.reciprocal(out=out, in_=in_)


C0 = 1408
TW = 2944
WL = 3072


class Buf:
    def __init__(self, name):
        self.name = name
        self.excl = name.startswith('PS') or name.startswith('PO') or name.startswith('PT')
        self.lw = None
        self.rd = []
        self.sems = {}


class Sched:
    ENG = ('pe', 'act', 'dve', 'pool', 'sp')

    def __init__(self, nc, es):
        self.nc, self.es = nc, es
        self.q = {e: [] for e in self.ENG}
        self.cur = {e: None for e in self.ENG}
        self.waited = {e: {} for e in self.ENG}
        self.nsem = 0
        self.ninst = 0

    def newsem(self):
        self.nsem += 1
        return self.es.enter_context(self.nc.semaphore("s%d" % self.nsem))

    def _deps(self, R, W, extra):
        d = list(extra)
        for b in R:
            d.append(b.lw)
            if b.excl:
                d.extend(b.rd)
        for b in W:
            d.append(b.lw)
            d.extend(b.rd)
        return d

    def _wait(self, eng, deps):
        best = {}
        for t in deps:
            if t is None:
                continue
            sem, val, te = t
            if eng == 'pe' and te == 'pe':
                continue
            k = id(sem)
            if k not in best or best[k][1] < val:
                best[k] = (sem, val)
        w = self.waited[eng]
        for k, (sem, val) in best.items():
            if w.get(k, 0) >= val:
                continue
            w[k] = val
            self.q[eng].append(lambda e, sem=sem, val=val: e.wait_ge(sem, val))

    def _mark(self, tk, R, W):
        for b in R:
            b.rd.append(tk)
        for b in W:
            b.lw = tk
            b.rd = []

    def op(self, eng, fn, R=(), W=(), deps=()):
        self._wait(eng, self._deps(R, W, deps))
        c = self.cur[eng]
        if c is None or c[1] >= 30000:
            c = self.cur[eng] = [self.newsem(), 0]
        c[1] += 1
        sem, val = c[0], c[1]
        self.q[eng].append(lambda e, fn=fn, sem=sem: fn(e).then_inc(sem, 1))
        self.ninst += 1
        tk = (sem, val, eng)
        self._mark(tk, R, W)
        return tk

    def group(self, eng, fns, R=(), W=(), deps=()):
        self._wait(eng, self._deps(R, W, deps))
        for fn in fns[:-1]:
            self.q[eng].append(lambda e, fn=fn: fn(e))
            self.ninst += 1
        return self.op(eng, fns[-1], R, W, deps=())

    def dma(self, queue, out, in_, R=(), W=(), sembuf=None, deps=(), **kw):
        self._wait(queue, self._deps(R, W, deps))
        sb = sembuf if sembuf is not None else (W[0] if W else R[0])
        if queue not in sb.sems:
            sb.sems[queue] = [self.newsem(), 0]
        ent = sb.sems[queue]
        ent[1] += 16
        sem, val = ent[0], ent[1]
        self.q[queue].append(lambda e, sem=sem: e.dma_start(out=out, in_=in_, **kw).then_inc(sem, 16))
        self.ninst += 1
        tk = (sem, val, 'dma')
        self._mark(tk, R, W)
        return tk

    def fence(self, extra=()):
        tks = [(c[0], c[1], e) for e, c in self.cur.items() if c is not None and e != 'sp']
        tks += list(extra)
        for e in ('pe', 'act', 'dve', 'pool'):
            self._wait(e, [t for t in tks if t[2] != e or e != 'pe'])

    def run(self, block, final_deps):
        for e in self.ENG:
            self._wait(e, final_deps)
        q = self.q

        @block.tensor
        def _(t):
            for f in q['pe']:
                f(t)

        @block.scalar
        def _(a):
            for f in q['act']:
                f(a)

        @block.vector
        def _(v):
            for f in q['dve']:
                f(v)

        @block.gpsimd
        def _(g):
            for f in q['pool']:
                f(g)

        @block.sync
        def _(s):
            for f in q['sp']:
                f(s)


def _t5_bucket(rel):
    half, max_exact = 16, 8
    n = np.abs(rel)
    large = max_exact + (np.log(np.maximum(n, 1) / max_exact) / np.log(1024 / max_exact)
                         * (half - max_exact)).astype(np.int32)
    large = np.minimum(large, half - 1)
    return ((rel > 0).astype(np.int32) * half + np.where(n < max_exact, n, large)).astype(np.int32)


def _host_consts():
    o = 1536 - np.arange(WL)
    mult = np.zeros(WL, np.float32)
    for w, d in ((128, 1), (512, 4), (2048, 16)):
        offs = d * np.arange(-(w // (2 * d)), w // (2 * d) + 1)
        mult += np.isin(o, offs).astype(np.float32)
    oh = np.zeros((32, WL), np.float32)
    bk = _t5_bucket(o)
    oh[bk, np.arange(WL)] = (mult > 0).astype(np.float32)
    half = 16
    inv = 10000.0 ** (-np.arange(half, dtype=np.float32) * 2.0 / 32)
    ang = np.arange(4096, dtype=np.float32)[:, None] * inv[None, :]
    return dict(oh=oh, mult=mult[None, :].copy(), cos=np.cos(ang).astype(np.float32),
                sin=np.sin(ang).astype(np.float32), ident=np.eye(128, dtype=np.float32))


CH = {}
_names = (['E_KV', 'UKV', 'E_AC', 'E_AX'] + ['EA%d' % j for j in range(4)] + ['E_BZ', 'E_CQ', 'UQ'] +
          ['EO%d' % j for j in range(4)] + ['O_K', 'O_V', 'O_CU', 'O_CV', 'O_CZ', 'O_DQ', 'O_DZ'] +
          ['OO%d' % j for j in range(4)] + ['T%d' % h for h in range(8)])
for _i, _n in enumerate(_names):
    CH[_n] = _i
NCH = len(_names)


def build(seqs, nlayers=2):
    nseq = len(seqs)
    ntok = sum(seqs)
    smax = max(seqs)
    ntmax = smax // 128
    ngmax = smax // G
    nc = bass.Bass("TRN2", target_bir_lowering=False)

    def din(name, shape, dt=F32):
        return nc.dram_tensor(name, list(shape), dt, kind="ExternalInput").ap()

    x = din("x", [ntok, D])
    cT = din("cT", [D, nseq])
    norm_gT = din("norm_gT", [2, 128, 8])
    w_mod = din("w_mod", [2, D, 3 * D])
    b_mod = din("b_mod", [2, 3 * D])
    rel_bias = din("rel_bias", [32, 8])
    w_in_e = din("w_in_e", [D, 2976])
    w_uq = din("w_uq", [256, 768])
    w_ukv = din("w_ukv", [128, 1024])
    w_out_e = din("w_out_e", [D, D])
    w_in_o = din("w_in_o", [D, 3584])
    w_out_o = din("w_out_o", [D, D])
    colv_d = din("colv", [128, 24])
    rowv_d = din("rowv", [1, 1088])
    c_bs_d = din("c_bs", [1, 512])
    c_wsT_d = din("c_wsT", [128, 512])
    cos_d = din("cos", [4096, 16])
    sin_d = din("sin", [4096, 16])
    oh_d = din("oh", [32, WL])
    mult_d = din("mult", [1, WL])
    ident_d = din("ident", [128, 128])
    y = nc.dram_tensor("y", [ntok, D], F32, kind="ExternalOutput").ap()
    wsc = nc.dram_tensor("wsc", [NCH, 128, 4096], BF16, kind="Internal").ap()
    modsc = nc.dram_tensor("modsc", [2, nseq, 3 * D], F32, kind="Internal").ap()
    wvec = nc.dram_tensor("wvec", [8, WL], BF16, kind="Internal").ap()

    es = ExitStack()
    with es:
        S = Sched(nc, es)

        def sb(name, shape, dt):
            return es.enter_context(nc.sbuf_tensor("sb_" + name, list(shape), dt))

        def ps(name, shape, dt):
            return es.enter_context(nc.psum_tensor("ps_" + name, list(shape), dt))

        ident = sb("ident", [128, 128], BF16)
        onesf = sb("onesf", [128, 128], F32)
        cosT = sb("cosT", [128, ntmax, 16], F32)
        sinT = sb("sinT", [128, ntmax, 16], F32)
        colv = sb("colv", [128, 24], F32)
        rowv = sb("rowv", [128, 1088], F32)
        cbs = sb("cbs", [1, 512], F32)
        wsT = sb("wsT", [128, 512], BF16)
        KT = sb("KT", [128, 4, smax], BF16)
        KrT = sb("KrT", [128, 4096], BF16)
        rsK = sb("rsK", [128, ntmax, 8], F32)
        V = sb("V", [128, ntmax, 584], BF16)
        pedge = sb("pedge", [128, 4, ngmax + 2, 2], BF16)
        ring = [sb("ring%d" % i, [128, 4096], BF16) for i in range(2)]
        ringcfg = {'slots': [0, 1]}
        gate_bc = sb("gate_bc", [128, D], F32)
        modv = sb("modv", [128, 32], F32)
        xin = sb("xin", [128, 4, D], F32)
        xs = [sb("xs%d" % i, [128, D], BF16) for i in range(2)]
        hT = sb("hT", [128, 8, G], BF16)
        stat = sb("stat", [128, 64], F32)
        tA = sb("tA", [128, 1024], F32)
        tB = sb("tB", [128, 1024], F32)
        tC = sb("tC", [128, 1024], F32)
        b1 = sb("b1", [128, 514], BF16)
        pcv = sb("pcv", [128, 514], BF16)
        gz = sb("gz", [128, 4, G], BF16)
        gz2 = sb("gz2", [64, 8, G], BF16)
        ycA = sb("ycA", [128, 4, G], BF16)
        ybT = sb("ybT", [128, 8, G], BF16)
        QT = sb("QT", [128, 8, G], BF16)
        QrT = sb("QrT", [128, 8, G], BF16)
        ring.append(KrT)
        ring.append(QrT[:].rearrange("p h g -> p (h g)"))
        ring.append(QT[:].rearrange("p h g -> p (h g)"))
        ring.append(ybT[:].rearrange("p h g -> p (h g)"))
        tmb = sb("tmb", [128, 1024], BF16)
        tmb2 = sb("tmb2", [128, 512], BF16)
        ckT = sb("ckT", [128, 2, G], BF16)
        Pb = [sb("P%d" % i, [128, 1024], BF16) for i in range(2)]
        bcs = sb("bcs", [64, G], F32)
        kvr = sb("kvr", [128, 4, 160], F32)
        scT = sb("scT", [128, 8, nseq], BF16)
        relb = sb("relb", [32, 8], BF16)
        onesb = sb("onesb", [1, 128], BF16)
        bmodc = sb("bmodc", [1, 512], BF16)
        PS = [ps("PS%d" % i, [128, 1024], F32) for i in range(2)]
        PO = [ps("PO%d" % i, [128, 512], F32) for i in range(2)]
        PT = [ps("PT%d" % i, [128, 1024], BF16) for i in range(2)]

        B = {}

        def bf(name):
            if name not in B:
                B[name] = Buf(name)
            return B[name]

        cnt = {'ps': 0, 'po': 0, 'pt': 0, 'ring': 0, 'xs': 0, 'P': 0, 'xo': 0, 'sc': 0}

        def rbuf(r):
            return bf(('ring0', 'ring1', 'KrT', 'QrT', 'QT', 'ybT')[r])

        def psb(i):
            return [bf('PS%da' % i), bf('PS%db' % i)]

        def sc_slot(sl):
            return PS[sl // 2][:, (sl % 2) * 512:(sl % 2 + 1) * 512], bf('PS%d%s' % (sl // 2, 'ab'[sl % 2]))

        def p_slot(sl):
            return Pb[sl // 2][:, (sl % 2) * 512:(sl % 2 + 1) * 512], bf('P%d%s' % (sl // 2, 'ab'[sl % 2]))

        def run_pipeline(items, depth=2, delay=6):
            n = len(items)
            pending = []
            i = 0
            while i < n + depth or pending:
                if i < n:
                    items[i][0]()
                nxt = []
                for cd, f in pending:
                    if cd <= 0:
                        f()
                    else:
                        nxt.append((cd - 1, f))
                pending = nxt
                if 0 <= i - depth < n:
                    fin = items[i - depth][1]()
                    if fin is not None:
                        fin[0]()
                        pending.append((delay, fin[1]))
                i += 1

        def rot(kind, n):
            i = cnt[kind] % n
            cnt[kind] += 1
            return i

        S.dma('sp', tA[:, 0:128], ident_d, W=[bf('tA')])
        S.op('dve', lambda e: e.tensor_copy(out=ident[:], in_=tA[:, 0:128]), R=[bf('tA')], W=[bf('ident')])
        S.op('dve', lambda e: e.memset(onesf[:], 1.0), W=[bf('onesf')])
        S.op('dve', lambda e: e.memset(onesb[:], 1.0), W=[bf('onesb')])
        S.op('dve', lambda e: e.memset(V[:], 0.0), W=[bf('V')])
        S.op('dve', lambda e: e.memset(V[:, :, 0:520].rearrange("p t (h d) -> p t h d", d=65)[:, :, :, 64:65], 1.0), W=[bf('V')])
        S.op('dve', lambda e: e.memset(QT[:], 0.0), W=[bf('QT')])
        S.op('dve', lambda e: e.memset(QrT[:], 0.0), W=[bf('QrT')])
        S.op('dve', lambda e: e.memset(KrT[:], 0.0), W=[bf('KrT')])
        S.op('dve', lambda e: e.memset(ybT[:], 0.0), W=[bf('ybT')])
        for i in range(2):
            S.op('dve', lambda e, i=i: e.memset(ring[i][:], 0.0), W=[rbuf(i)])
        S.op('dve', lambda e: e.memset(pedge[:], 0.0), W=[bf('pedge')])
        S.dma('sp', cosT[:], cos_d[0:smax, :].rearrange("(t p) d -> p t d", p=128), W=[bf('cos')])
        S.dma('sp', sinT[:], sin_d[0:smax, :].rearrange("(t p) d -> p t d", p=128), W=[bf('sin')])
        S.dma('sp', colv[:], colv_d, W=[bf('colv')])
        S.dma('sp', rowv[:], rowv_d.partition_broadcast(128), W=[bf('rowv')])
        S.dma('sp', cbs[:], c_bs_d, W=[bf('cbs')])
        S.dma('pool', wsT[:], c_wsT_d, W=[bf('wsT')])

        S.dma('sp', tA[:, 0:8 * nseq].rearrange("p (k s) -> p k s", s=nseq), cT.rearrange("(k p) s -> p k s", p=128),
              W=[bf('tA')], allow_slow_non_contiguous=True)
        S.op('act', lambda e: e.activation(out=tB[:, 0:8 * nseq], in_=tA[:, 0:8 * nseq], func=AF.Sigmoid),
             R=[bf('tA')], W=[bf('tB')])
        S.op('dve', lambda e: e.tensor_tensor(out=scT[:].rearrange("p k s -> p (k s)"), in0=tA[:, 0:8 * nseq],
                                              in1=tB[:, 0:8 * nseq], op=ALU.mult), R=[bf('tA'), bf('tB')], W=[bf('scT')])
        for l in range(nlayers):
            for cc in range(6):
                r = rot('ring', 2)
                S.dma('pool', ring[r][:].rearrange("p (kc j) -> p kc j", j=512),
                      w_mod[l][:, cc * 512:(cc + 1) * 512].rearrange("(kc p) j -> p kc j", p=128), W=[rbuf(r)])
                S.dma('pool', bmodc[:], b_mod[l:l + 1, cc * 512:(cc + 1) * 512], W=[bf('bmodc')])
                po = rot('po', 2)
                fns = [lambda e, po=po, r=r, kc=kc: e.matmul(PO[po][0:nseq, :], lhsT=scT[:, kc, :],
                                                             rhs=ring[r][:, kc * 512:(kc + 1) * 512], start=(kc == 0), stop=False)
                       for kc in range(8)]
                fns.append(lambda e, po=po, l=l, cc=cc: e.matmul(
                    PO[po][0:nseq, :], lhsT=onesb[0:1, 0:nseq],
                    rhs=bmodc[0:1, :], start=False, stop=True))
                S.group('pe', fns, R=[bf('scT'), rbuf(r), bf('onesb'), bf('bmodc')], W=[bf('PO%d' % po)])
                S.op('dve', lambda e, po=po: e.tensor_copy(out=tC[0:nseq, 0:512], in_=PO[po][0:nseq, :]),
                     R=[bf('PO%d' % po)], W=[bf('tC')])
                S.dma('sp', modsc[l][:, cc * 512:(cc + 1) * 512], tC[0:nseq, 0:512], R=[bf('tC')], W=[bf('modsc')])

        S.dma('pool', relb[:], rel_bias, W=[bf('relb')])
        for cc in range(6):
            S.dma('pool', tmb2[0:32, :], oh_d[:, cc * 512:(cc + 1) * 512], W=[bf('tmb2')])
            S.dma('sp', tB[0:8, 0:512], mult_d[:, cc * 512:(cc + 1) * 512].partition_broadcast(8), W=[bf('tB')])
            po = rot('po', 2)
            S.op('pe', lambda e, po=po: e.matmul(PO[po][0:8, :], lhsT=relb[:], rhs=tmb2[0:32, :], start=True, stop=True),
                 R=[bf('relb'), bf('tmb2')], W=[bf('PO%d' % po)])
            S.op('act', lambda e, po=po: e.activation(out=tA[0:8, 0:512], in_=PO[po][0:8, :], func=AF.Exp),
                 R=[bf('PO%d' % po)], W=[bf('tA')])
            S.op('dve', lambda e: e.tensor_tensor(out=tmb[0:8, 0:512], in0=tA[0:8, 0:512], in1=tB[0:8, 0:512], op=ALU.mult),
                 R=[bf('tA'), bf('tB')], W=[bf('tmb')])
            S.dma('sp', wvec[:, cc * 512:(cc + 1) * 512], tmb[0:8, 0:512], R=[bf('tmb')], W=[bf('wvec')])
        S.op('dve', lambda e: e.memset(ring[0][:], 0.0), W=[rbuf(0)])
        def wchunk_k1024(name, w, c0, ncols, width=512, off=0):
            dst = wsc[CH[name]][:, 0:8 * width].rearrange("p (kc j) -> p kc j", j=width)[:, :, off:off + ncols]
            S.dma('pool', dst, w[:, c0:c0 + ncols].rearrange("(kc p) j -> p kc j", p=128), W=[bf('wsc_' + name + str(off))],
                  sembuf=bf('wsc_' + name))

        wchunk_k1024('E_KV', w_in_e, 2304, 160, width=160)
        S.dma('pool', wsc[CH['UKV']][:, 0:1024], w_ukv, W=[bf('wsc_UKV0')], sembuf=bf('wsc_UKV'))
        wchunk_k1024('E_AC', w_in_e, 512, 512)
        wchunk_k1024('E_AX', w_in_e, 1024, 512)
        for j in range(4):
            for pi, c0 in enumerate((512, 1024, 0, 1536)):
                wchunk_k1024('EA%d' % j, w_in_e, c0 + j * 128, 128, off=pi * 128)
        wchunk_k1024('E_BZ', w_in_e, 2464, 512)
        wchunk_k1024('E_CQ', w_in_e, 2048, 256, width=256)
        S.dma('pool', wsc[CH['UQ']][:, 0:1536].rearrange("p (kc j) -> p kc j", j=768),
              w_uq.rearrange("(kc p) j -> p kc j", p=128), W=[bf('wsc_UQ0')], sembuf=bf('wsc_UQ'))
        for nm, wo in (('EO', w_out_e), ('OO', w_out_o)):
            for j in range(4):
                S.dma('sp', wsc[CH['%s%d' % (nm, j)]][64:128, 1024:3072], ring[0][64:128, 1024:3072], R=[bf('ring0')],
                      W=[bf('wsc_%s%dz' % (nm, j))], sembuf=bf('wsc_%s%d' % (nm, j)))
                S.dma('pool', wsc[CH['%s%d' % (nm, j)]][:, 0:1024].rearrange("p (kc j) -> p kc j", j=256),
                      wo[0:512, j * 256:(j + 1) * 256].rearrange("(kc p) j -> p kc j", p=128),
                      W=[bf('wsc_%s%da' % (nm, j))], sembuf=bf('wsc_%s%d' % (nm, j)))
                S.dma('pool', wsc[CH['%s%d' % (nm, j)]][0:64, 1024:3072].rearrange("p (h j) -> p h j", j=256),
                      wo[512:1024, j * 256:(j + 1) * 256].rearrange("(h p) j -> p h j", p=64),
                      W=[bf('wsc_%s%db' % (nm, j))], sembuf=bf('wsc_%s%d' % (nm, j)))
        for nm, c0 in (('O_K', 2048), ('O_V', 2560), ('O_CU', 0), ('O_CV', 512), ('O_CZ', 1024), ('O_DQ', 1536),
                       ('O_DZ', 3072)):
            wchunk_k1024(nm, w_in_o, c0, 512)

        def chunk_tickets(name):
            return [(e_[0], e_[1], 'dma') for k, b in B.items() if k == 'wsc_' + name for e_ in b.sems.values()]

        toe_pending = list(range(8))

        def emit_toe_rows(h):
            for kl in range(128):
                S.dma('pool', wsc[CH['T%d' % h]][kl:kl + 1, 0:TW], wvec[h:h + 1, 128 - kl:128 - kl + TW],
                      R=[bf('wvec')], W=[bf('wsc_T%d_%d' % (h, kl))], sembuf=bf('wsc_T%d' % h))

        def load_chunk(name):
            r = ringcfg['slots'][rot('ring', len(ringcfg['slots']))]
            deps = chunk_tickets(name)
            nc_ = (3072 if name[:2] in ('EO', 'OO') else 2944 if name[0] == 'T' else 1024 if name == 'UKV' else 1536 if name == 'UQ'
                   else 1280 if name == 'E_KV' else 2048 if name == 'E_CQ' else 4096)
            S.dma('sp', ring[r][:, 0:nc_], wsc[CH[name]][:, 0:nc_], W=[rbuf(r)], deps=deps)
            return r

        def rstd_from(ssv, outv, n, dim):
            S.op('act', lambda e: e.activation(out=stat[:, 56:56 + n], in_=ssv, func=AF.Ln, scale=1.0 / dim, bias=EPS),
                 R=[bf('stat')], W=[bf('stat2')])
            S.op('act', lambda e: e.activation(out=outv, in_=stat[:, 56:56 + n], func=AF.Exp, scale=-0.5),
                 R=[bf('stat2')], W=[bf('stat')])

        def seq_layer_setup(l, si):
            S.dma('sp', modv[:, 0:16].rearrange("p (a j) -> p a j", j=8),
                  modsc[l, si, 0:2048].rearrange("(a j p) -> p a j", p=128, j=8), R=[bf('modsc')], W=[bf('modv')],
                  allow_slow_non_contiguous=True)
            S.dma('sp', modv[:, 24:32], norm_gT[l], W=[bf('modvg')])
            S.dma('sp', gate_bc[:], modsc[l, si:si + 1, 2048:3072].partition_broadcast(128), R=[bf('modsc')], W=[bf('gate_bc')])
            S.op('dve', lambda e: e.scalar_tensor_tensor(out=modv[:, 16:24], in0=modv[:, 8:16], scalar=1.0, in1=modv[:, 24:32],
                                                         op0=ALU.add, op1=ALU.mult), R=[bf('modv'), bf('modvg')], W=[bf('modv2')])

        pref = {'key': None, 'plan': [], 'idx': 0}

        def x_load(src, t0):
            rdep = [bf('ydram')] if src is y else []
            S.dma('sp', xin[:], src[t0:t0 + G, :].rearrange("(t p) f -> p t f", p=128), R=rdep, W=[bf('xin')], sembuf=bf('xin'))

        def stage_n(src, t0):
            if pref['key'] != (src is y, t0):
                x_load(src, t0)
            pref['key'] = None
            stage_n_compute()
            plan = pref['plan']
            if plan:
                i = pref['idx']
                assert plan[i][0:2] == (src is y, t0), (plan[i], src is y, t0)
                pref['idx'] = i + 1
                if i + 1 < len(plan) and plan[i + 1][2]:
                    nsrc = y if plan[i + 1][0] else x
                    x_load(nsrc, plan[i + 1][1])
                    pref['key'] = (plan[i + 1][0], plan[i + 1][1])

        def stage_n_compute():
            S.op('dve', lambda e: e.memset(stat[:, 0:4], 0.0), W=[bf('stat')])
            for t in range(4):
                S.op('act', lambda e, t=t: e.activation(out=tA[:], in_=xin[:, t, :], func=AF.Square, accum_out=stat[:, t:t + 1]),
                     R=[bf('xin')], W=[bf('tA'), bf('stat')])
            rstd_from(stat[:, 0:4], stat[:, 4:8], 4, float(D))
            for t in range(4):
                xi = rot('xs', 2)
                S.op('act', lambda e, t=t, xi=xi: e.activation(out=xs[xi][:], in_=xin[:, t, :], func=AF.Copy, scale=stat[:, 4 + t:5 + t]),
                     R=[bf('xin'), bf('stat')], W=[bf('xs%d' % xi)])
                pt = rot('pt', 2)
                S.group('pe', [lambda e, kc=kc, xi=xi, pt=pt: e.transpose(out=PT[pt][:, kc * 128:(kc + 1) * 128],
                                                                            in_=xs[xi][:, kc * 128:(kc + 1) * 128], identity=ident[:])
                               for kc in range(8)], R=[bf('xs%d' % xi), bf('ident')], W=[bf('PT%d' % pt)])
                S.op('dve', lambda e, pt=pt: e.tensor_tensor(out=tmb[:].rearrange("p (k j) -> p k j", j=128),
                                                             in0=PT[pt][:].rearrange("p (k j) -> p k j", j=128),
                                                             in1=modv[:, 16:24].unsqueeze(2).to_broadcast([128, 8, 128]), op=ALU.mult),
                     R=[bf('PT%d' % pt), bf('modv2')], W=[bf('tmb')])
                S.op('dve', lambda e, t=t: e.tensor_tensor(out=hT[:, :, t * 128:(t + 1) * 128],
                                                           in0=tmb[:].rearrange("p (k j) -> p k j", j=128),
                                                           in1=modv[:, 0:8].unsqueeze(2).to_broadcast([128, 8, 128]), op=ALU.add),
                     R=[bf('tmb'), bf('modv')], W=[bf('hT')])

        def proj_fm(r, c0, width, m0, msz, po):
            S.group('pe', [lambda e, kc=kc: e.matmul(PO[po][0:msz, :], lhsT=ring[r][:, kc * width + c0 + m0:kc * width + c0 + m0 + msz],
                                                     rhs=hT[:, kc, :], start=(kc == 0), stop=(kc == 7)) for kc in range(8)],
                    R=[rbuf(r), bf('hT')], W=[bf('PO%d' % po)])

        def proj_tm(r, width, ncols, t, pst, c0=0, o0=0):
            fns = []
            for n0 in range(0, ncols, 512):
                nn = min(512, ncols - n0)
                for kc in range(8):
                    fns.append(lambda e, kc=kc, n0=n0, nn=nn: e.matmul(
                        PS[pst][:, o0 + n0:o0 + n0 + nn], lhsT=hT[:, kc, t * 128:(t + 1) * 128],
                        rhs=ring[r][:, kc * width + c0 + n0:kc * width + c0 + n0 + nn], start=(kc == 0), stop=(kc == 7)))
            S.group('pe', fns, R=[rbuf(r), bf('hT')], W=[*psb(pst)])

        def rope_tm(src3, gain_bc, t_abs, nh, dst):
            cosb = cosT[:, t_abs, :].unsqueeze(1).to_broadcast([128, nh, 16])
            sinb = sinT[:, t_abs, :].unsqueeze(1).to_broadcast([128, nh, 16])
            g3 = tC[:, 0:nh * 32].rearrange("p (h d) -> p h d", d=32)
            w1 = tC[:, 256:256 + nh * 16].rearrange("p (h d) -> p h d", d=16)
            w2 = tC[:, 512:512 + nh * 16].rearrange("p (h d) -> p h d", d=16)
            S.op('dve', lambda e: e.tensor_tensor(out=g3, in0=src3, in1=gain_bc.unsqueeze(1).to_broadcast([128, nh, 32]), op=ALU.mult),
                 R=[bf('tB'), bf('rowv')], W=[bf('tC')])
            S.op('dve', lambda e: e.tensor_tensor(out=w1, in0=g3[:, :, 0:16], in1=cosb, op=ALU.mult), R=[bf('tC'), bf('cos')], W=[bf('tCw1')])
            S.op('dve', lambda e: e.tensor_tensor(out=w2, in0=g3[:, :, 16:32], in1=sinb, op=ALU.mult), R=[bf('tC'), bf('sin')], W=[bf('tCw2')])
            S.op('dve', lambda e: e.tensor_tensor(out=dst[:, :, 0:16], in0=w1, in1=w2, op=ALU.subtract), R=[bf('tCw1'), bf('tCw2')], W=[bf('ropeo')])
            S.op('dve', lambda e: e.tensor_tensor(out=w1, in0=g3[:, :, 0:16], in1=sinb, op=ALU.mult), R=[bf('tC'), bf('sin'), bf('ropeo')], W=[bf('tCw1')])
            S.op('dve', lambda e: e.tensor_tensor(out=w2, in0=g3[:, :, 16:32], in1=cosb, op=ALU.mult), R=[bf('tC'), bf('cos'), bf('ropeo')], W=[bf('tCw2')])
            S.op('dve', lambda e: e.tensor_tensor(out=dst[:, :, 16:32], in0=w1, in1=w2, op=ALU.add), R=[bf('tCw1'), bf('tCw2')], W=[bf('ropeo')])

        def head_norm(pst, ncol_h, nh, dim, dst_scaled, o0=0):
            v3 = PS[pst][:, o0:o0 + nh * ncol_h].rearrange("p (h d) -> p h d", d=ncol_h)
            a3 = tA[:, 0:nh * ncol_h].rearrange("p (h d) -> p h d", d=ncol_h)
            S.op('act', lambda e: e.activation(out=tA[:, 0:nh * ncol_h], in_=PS[pst][:, o0:o0 + nh * ncol_h], func=AF.Square),
                 W=[bf('tA'), *psb(pst)])
            if nlayers == -12:
                return v3
            S.op('dve', lambda e: e.tensor_reduce(out=stat[:, 16:16 + nh], in_=a3[:, :, 0:dim], axis=AX.X, op=ALU.add),
                 R=[bf('tA')], W=[bf('stat')])
            return v3

        def finalize_a(po, on_act=False):
            if on_act:
                S.op('act', lambda e: e.activation(out=tC[64:65, 0:512], in_=PO[po][64:65, :], func=AF.Ln), R=[bf('PO%d' % po)], W=[bf('rden')])
                S.op('act', lambda e: e.activation(out=tC[64:65, 0:512], in_=tC[64:65, 0:512], func=AF.Exp, scale=-1.0), W=[bf('rden')])
            else:
                S.op('dve', lambda e: RECIP(e, tC[64:65, 0:512], PO[po][64:65, :]), R=[bf('PO%d' % po)], W=[bf('rden')])

        def finalize_b(po, h, gate_ap, gate_buf):
            sap, sbuf_ = sc_slot(rot('sc', 4))
            S.op('pe', lambda e: e.matmul(sap[0:64, :], lhsT=onesf[64:65, 0:64], rhs=tC[64:65, 0:512], start=True, stop=True),
                 R=[bf('onesf'), bf('rden')], W=[sbuf_])
            S.op('dve', lambda e: e.tensor_tensor(out=bcs[:], in0=sap[0:64, :], in1=gate_ap, op=ALU.mult),
                 R=[sbuf_, gate_buf], W=[bf('bcs')])
            S.op('dve', lambda e: e.tensor_tensor(out=ybT[0:64, h, :], in0=PO[po][0:64, :], in1=bcs[:], op=ALU.mult),
                 R=[bf('PO%d' % po), bf('bcs')], W=[bf('ybT')])

        def out_proj2(prefix, dst, t0):
            tk = None
            for j in range(4):
                r = load_chunk('%s%d' % (prefix, j))
                tX, tXn = (tC, 'tC') if j % 2 == 0 else (tB, 'tB')
                for t in range(4):
                    pst = rot('ps', 2)
                    fns = [lambda e, kc=kc, t=t, pst=pst, r=r: e.matmul(PS[pst][:, 0:256], lhsT=ycA[:, kc, t * 128:(t + 1) * 128],
                                                                       rhs=ring[r][:, kc * 256:(kc + 1) * 256], start=(kc == 0), stop=False)
                           for kc in range(4)]
                    fns += [lambda e, h=h, t=t, pst=pst, r=r: e.matmul(PS[pst][:, 0:256], lhsT=ybT[:, h, t * 128:(t + 1) * 128],
                                                                      rhs=ring[r][:, 1024 + h * 256:1024 + (h + 1) * 256],
                                                                      start=False, stop=(h == 7)) for h in range(8)]
                    S.group('pe', fns, R=[rbuf(r), bf('ycA'), bf('ybT')], W=[*psb(pst)])
                    S.op('dve', lambda e, t=t, pst=pst, j=j, tX=tX: e.tensor_tensor(out=tX[:, t * 256:(t + 1) * 256], in0=PS[pst][:, 0:256],
                                                                                    in1=gate_bc[:, j * 256:(j + 1) * 256], op=ALU.mult),
                         R=[*psb(pst), bf('gate_bc')], W=[bf(tXn)])
                tk = S.dma('pool', dst[t0:t0 + G, j * 256:(j + 1) * 256].rearrange("(t p) f -> p t f", p=128),
                           tX[:].rearrange("p (t f) -> p t f", f=256), R=[bf(tXn)], W=[bf('ydram')], sembuf=bf('ystore%d' % (j % 2)),
                           accum_op=ALU.add)
            return tk

        def l0_pass_a(src, base, slen):
            ng = slen // G
            ringcfg['slots'] = [0, 1, 4, 5]
            S.op('dve', lambda e: e.memset(KrT[:], 0.0), W=[bf('KrT')])
            S.op('dve', lambda e: e.memset(QrT[:], 0.0), W=[bf('QrT')])
            S.op('dve', lambda e: e.memset(pedge[:], 0.0), W=[bf('pedge')])
            for g in range(ng):
                stage_n(src, base + g * G)
                r = load_chunk('E_KV')
                for t in range(4):
                    pst = rot('ps', 2)
                    proj_tm(r, 160, 160, t, pst)
                    S.op('dve', lambda e, t=t, pst=pst: e.tensor_copy(out=kvr[:, t, :], in_=PS[pst][:, 0:160]),
                         R=[*psb(pst)], W=[bf('kvr')])
                if nlayers == -5:
                    return
                rc = load_chunk('E_AC')
                rx = load_chunk('E_AX')
                for j in range(4 if nlayers != -4 else 0):
                    for which, rr in ((0, rc), (1, rx)):
                        po = rot('po', 2)
                        fns = []
                        for col in range(2):
                            cidx = col * (G - 1)
                            for kc in range(8):
                                fns.append(lambda e, kc=kc, cidx=cidx, col=col, rr=rr, po=po, j=j: e.matmul(
                                    PO[po][:, col:col + 1], lhsT=ring[rr][:, kc * 512 + j * 128:kc * 512 + (j + 1) * 128],
                                    rhs=hT[:, kc, cidx:cidx + 1], start=(kc == 0), stop=(kc == 7)))
                        S.group('pe', fns, R=[rbuf(rr), bf('hT')], W=[bf('PO%d' % po)])
                        if which == 0:
                            S.op('dve', lambda e, po=po: e.tensor_copy(out=stat[:, 32:34], in_=PO[po][:, 0:2]),
                                 R=[bf('PO%d' % po)], W=[bf('stat3')])
                        else:
                            S.op('dve', lambda e, po=po, j=j, g=g: e.tensor_tensor(out=pedge[:, j, g + 1, :], in0=PO[po][:, 0:2],
                                                                                   in1=stat[:, 32:34], op=ALU.mult),
                                 R=[bf('PO%d' % po), bf('stat3')], W=[bf('pedge')])
                S.op('dve', lambda e: e.memset(stat[:, 0:4], 0.0), W=[bf('stat')])
                for t in range(4):
                    S.op('act', lambda e, t=t: e.activation(out=tA[:, 0:128], in_=kvr[:, t, 0:128], func=AF.Square,
                                                            accum_out=stat[:, t:t + 1]), R=[bf('kvr')], W=[bf('tA'), bf('stat')])
                rstd_from(stat[:, 0:4], stat[:, 4:8], 4, 128.0)
                pt = rot('pt', 2)
                for t in range(4):
                    S.op('act', lambda e, t=t: e.activation(out=tmb2[:, t * 128:(t + 1) * 128], in_=kvr[:, t, 0:128], func=AF.Copy,
                                                            scale=stat[:, 4 + t:5 + t]), R=[bf('kvr'), bf('stat')], W=[bf('tmb2')])
                S.group('pe', [lambda e, t=t, pt=pt: e.transpose(out=PT[pt][:, t * 128:(t + 1) * 128], in_=tmb2[:, t * 128:(t + 1) * 128],
                                                                identity=ident[:]) for t in range(4)],
                        R=[bf('tmb2'), bf('ident')], W=[bf('PT%d' % pt)])
                S.op('dve', lambda e, pt=pt: e.tensor_scalar(out=ckT[:, 0, :], in0=PT[pt][:, 0:512], scalar1=colv[:, 2:3], scalar2=None,
                                                             op0=ALU.mult), R=[bf('PT%d' % pt), bf('colv')], W=[bf('ckT')])
                if nlayers == -6:
                    return
                ru = load_chunk('UKV')
                psts = {}

                def projA(t, ru=ru, psts=psts):
                    pst = rot('ps', 2)
                    psts[t] = pst
                    S.group('pe', [lambda e, n0=n0, t=t, pst=pst, ru=ru: e.matmul(PS[pst][:, n0:n0 + 512], lhsT=ckT[:, 0, t * 128:(t + 1) * 128],
                                                                          rhs=ring[ru][:, n0:n0 + 512], start=True, stop=True)
                                   for n0 in (0, 512)], R=[rbuf(ru), bf('ckT')], W=[*psb(pst)])
                projA(0)
                for t in range(4):
                    tt = (base - base) + g * 4 + t
                    if t + 1 < 4:
                        projA(t + 1)
                    pst = psts[t]
                    v3 = PS[pst][:].rearrange("p (h d) -> p h d", d=128)
                    S.op('dve', lambda e, tt=tt, v3=v3: e.tensor_copy(out=V[:, tt, 0:520].rearrange("p (h d) -> p h d", d=65)[:, :, 0:64], in_=v3[:, :, 64:128]),
                         R=[*psb(pst)], W=[bf('V')])
                    if nlayers == -8:
                        continue
                    head_norm(pst, 128, 8, 64, None)
                    if nlayers in (-11, -12):
                        continue
                    S.op('dve', lambda e: e.memset(stat[:, 24:25], 0.0), W=[bf('stat4')])
                    S.op('act', lambda e, t=t: e.activation(out=tB[:, 0:32], in_=kvr[:, t, 128:160], func=AF.Square,
                                                            accum_out=stat[:, 24:25]), R=[bf('kvr')], W=[bf('tB'), bf('stat4')])
                    S.op('dve', lambda e: e.tensor_scalar(out=stat[:, 16:24], in0=stat[:, 16:24], scalar1=stat[:, 24:25], scalar2=None,
                                                          op0=ALU.add), R=[bf('stat4')], W=[bf('stat')])
                    rstd_from(stat[:, 16:24], stat[:, 40:48], 8, 96.0)
                    S.op('dve', lambda e, tt=tt: e.tensor_scalar(out=rsK[:, tt, :], in0=stat[:, 40:48], scalar1=96.0 ** -0.5, scalar2=None,
                                                                 op0=ALU.mult), R=[bf('stat')], W=[bf('rsK')])
                    if nlayers == -9:
                        continue
                    S.op('dve', lambda e, v3=v3: e.tensor_copy(out=tmb[:, 0:512].rearrange("p (h d) -> p h d", d=64), in_=v3[:, :, 0:64]),
                         R=[*psb(pst)], W=[bf('tmb')])
                    pt = rot('pt', 2)
                    S.group('pe', [lambda e, j=j, pt=pt: e.transpose(out=PT[pt][:, j * 128:(j + 1) * 128], in_=tmb[:, j * 128:(j + 1) * 128],
                                                                    identity=ident[:]) for j in range(4)],
                            R=[bf('tmb'), bf('ident')], W=[bf('PT%d' % pt)])
                    if nlayers == -10:
                        continue
                    S.op('dve', lambda e, pt=pt, tt=tt: e.tensor_scalar(out=KT[:, :, tt * 128:(tt + 1) * 128],
                                                                        in0=PT[pt][:, 0:512].rearrange("p (j c) -> p j c", c=128),
                                                                        scalar1=colv[:, 3:4], scalar2=None, op0=ALU.mult),
                         R=[bf('PT%d' % pt), bf('colv')], W=[bf('KT')])
                    if nlayers == -7:
                        continue
                    S.op('dve', lambda e, t=t: e.tensor_copy(out=tB[:, 64:96], in_=kvr[:, t, 128:160]), R=[bf('kvr')], W=[bf('tB')])
                    rope_tm(tB[:, 64:96].rearrange("p (h d) -> p h d", d=32), rowv[:, 32:64], tt, 1,
                            tmb2[:, 0:32].rearrange("p (h d) -> p h d", d=32))
                    pt = rot('pt', 2)
                    S.op('pe', lambda e, pt=pt: e.transpose(out=PT[pt][0:32, 0:128], in_=tmb2[:, 0:32], identity=ident[:]),
                         R=[bf('ropeo'), bf('ident')], W=[bf('PT%d' % pt)])
                    S.op('dve', lambda e, pt=pt, tt=tt: e.tensor_copy(out=KrT[0:32, tt * 128:(tt + 1) * 128], in_=PT[pt][0:32, 0:128]),
                         R=[bf('PT%d' % pt)], W=[bf('KrT')])

        def l0_pass_b(src, dst, base, slen):
            ng = slen // G
            nt = slen // 128
            ringcfg['slots'] = [0, 1]
            S.op('dve', lambda e: e.memset(QT[:], 0.0), W=[bf('QT')])
            S.op('dve', lambda e: e.memset(ybT[:], 0.0), W=[bf('ybT')])
            for g in range(ng):
                stage_n(src, base + g * G)
                t0c = base + g * G
                S.dma('sp', dst[t0c:t0c + G, :], src[t0c:t0c + G, :], W=[bf('ydram')], sembuf=bf('ycopy'))
                for j in range(4):
                    r = load_chunk('EA%d' % j)
                    for part in range(4):
                        po = rot('po', 2)
                        proj_fm(r, 0, 512, part * 128, 128, po)
                        if part == 0:
                            S.op('act', lambda e, po=po: e.activation(out=b1[:, 0:512], in_=PO[po][:, :], func=AF.Copy),
                                 R=[bf('PO%d' % po)], W=[bf('b1')])
                        elif part == 1:
                            S.op('dve', lambda e, po=po: e.tensor_tensor(out=pcv[:, 1:513], in0=PO[po][:, :], in1=b1[:, 0:512], op=ALU.mult),
                                 R=[bf('PO%d' % po), bf('b1')], W=[bf('pcv')])
                            S.op('dve', lambda e, j=j, g=g: e.tensor_copy(out=pcv[:, 0:1], in_=pedge[:, j, g, 1:2]),
                                 R=[bf('pedge')], W=[bf('pcv')])
                            S.op('dve', lambda e, j=j, g=g: e.tensor_copy(out=pcv[:, 513:514], in_=pedge[:, j, g + 2, 0:1]),
                                 R=[bf('pedge')], W=[bf('pcv')])
                            S.op('dve', lambda e, j=j: e.tensor_scalar(out=tA[:, 0:512], in0=pcv[:, 1:513], scalar1=colv[:, 8 + j * 3 + 1:8 + j * 3 + 2],
                                                                       scalar2=None, op0=ALU.mult), R=[bf('pcv'), bf('colv')], W=[bf('tA')])
                            S.op('dve', lambda e, j=j: e.scalar_tensor_tensor(out=tA[:, 0:512], in0=pcv[:, 0:512], scalar=colv[:, 8 + j * 3:8 + j * 3 + 1],
                                                                              in1=tA[:, 0:512], op0=ALU.mult, op1=ALU.add),
                                 R=[bf('pcv'), bf('colv')], W=[bf('tA')])
                            S.op('dve', lambda e, j=j: e.scalar_tensor_tensor(out=tA[:, 0:512], in0=pcv[:, 2:514], scalar=colv[:, 8 + j * 3 + 2:8 + j * 3 + 3],
                                                                              in1=tA[:, 0:512], op0=ALU.mult, op1=ALU.add),
                                 R=[bf('pcv'), bf('colv')], W=[bf('tA')])
                        elif part == 2:
                            S.op('dve', lambda e, po=po: e.tensor_tensor(out=tA[:, 512:1024], in0=PO[po][:, :], in1=tA[:, 0:512], op=ALU.mult),
                                 R=[bf('PO%d' % po), bf('tA')], W=[bf('tA2')])
                        else:
                            S.op('act', lambda e, po=po: e.activation(out=tB[:, 0:512], in_=PO[po][:, :], func=AF.Sigmoid),
                                 R=[bf('PO%d' % po)], W=[bf('tB')])
                            S.op('dve', lambda e, po=po: e.tensor_tensor(out=tB[:, 512:1024], in0=PO[po][:, :], in1=tB[:, 0:512], op=ALU.mult),
                                 R=[bf('PO%d' % po), bf('tB')], W=[bf('tB2')])
                            S.op('dve', lambda e, j=j: e.tensor_tensor(out=ycA[:, j, :], in0=tA[:, 512:1024], in1=tB[:, 512:1024], op=ALU.mult),
                                 R=[bf('tA2'), bf('tB2')], W=[bf('ycA')])
                r = load_chunk('E_BZ')
                for h in range(8):
                    po = rot('po', 2)
                    proj_fm(r, 0, 512, h * 64, 64, po)
                    S.op('act', lambda e, po=po: e.activation(out=tB[0:64, 0:512], in_=PO[po][0:64, :], func=AF.Sigmoid),
                         R=[bf('PO%d' % po)], W=[bf('tB')])
                    S.op('dve', lambda e, po=po, h=h: e.tensor_tensor(out=gz2[0:64, h, :], in0=PO[po][0:64, :], in1=tB[0:64, 0:512], op=ALU.mult),
                         R=[bf('PO%d' % po), bf('tB')], W=[bf('gz2')])
                r = load_chunk('E_CQ')
                S.op('dve', lambda e: e.memset(stat[:, 0:4], 0.0), W=[bf('stat')])
                pq = []
                for t in range(4):
                    pst = rot('ps', 2)
                    proj_tm(r, 256, 256, t, pst)
                    S.op('dve', lambda e, t=t, pst=pst: e.tensor_copy(out=tC[:, t * 256:(t + 1) * 256], in_=PS[pst][:, 0:256]),
                         R=[*psb(pst)], W=[bf('tCq')])
                    S.op('act', lambda e, t=t: e.activation(out=tA[:, 0:256], in_=tC[:, t * 256:(t + 1) * 256], func=AF.Square,
                                                            accum_out=stat[:, t:t + 1]), R=[bf('tCq')], W=[bf('tA'), bf('stat')])
                rstd_from(stat[:, 0:4], stat[:, 4:8], 4, 256.0)
                for t in range(4):
                    S.op('act', lambda e, t=t: e.activation(out=tmb[:, t * 256:(t + 1) * 256], in_=tC[:, t * 256:(t + 1) * 256], func=AF.Copy,
                                                            scale=stat[:, 4 + t:5 + t]), R=[bf('tCq'), bf('stat')], W=[bf('tmb')])
                for kc in range(2):
                    pt = rot('pt', 2)
                    S.group('pe', [lambda e, t=t, kc=kc, pt=pt: e.transpose(out=PT[pt][:, t * 128:(t + 1) * 128],
                                                                           in_=tmb[:, t * 256 + kc * 128:t * 256 + (kc + 1) * 128], identity=ident[:])
                                   for t in range(4)], R=[bf('tmb'), bf('ident')], W=[bf('PT%d' % pt)])
                    S.op('dve', lambda e, kc=kc, pt=pt: e.tensor_scalar(out=ckT[:, kc, :], in0=PT[pt][:, 0:512], scalar1=colv[:, kc:kc + 1],
                                                                        scalar2=None, op0=ALU.mult), R=[bf('PT%d' % pt), bf('colv')], W=[bf('ckT')])
                r = load_chunk('UQ')
                psts = {}

                def projA(t, r=r, psts=psts):
                    pst = rot('ps', 2)
                    psts[t] = pst
                    fns = []
                    for n0, nn in ((0, 512), (512, 256)):
                        for kc in range(2):
                            fns.append(lambda e, kc=kc, n0=n0, nn=nn, t=t, pst=pst, r=r: e.matmul(
                                PS[pst][:, n0:n0 + nn], lhsT=ckT[:, kc, t * 128:(t + 1) * 128],
                                rhs=ring[r][:, kc * 768 + n0:kc * 768 + n0 + nn], start=(kc == 0), stop=(kc == 1)))
                    S.group('pe', fns, R=[rbuf(r), bf('ckT')], W=[*psb(pst)])
                projA(0)
                for t in range(4):
                    tt = g * 4 + t
                    if t + 1 < 4:
                        projA(t + 1)
                    pst = psts[t]
                    v3 = head_norm(pst, 96, 8, 96, None)
                    rstd_from(stat[:, 16:24], stat[:, 40:48], 8, 96.0)
                    S.op('dve', lambda e, v3=v3: e.tensor_tensor(out=tB[:, 0:768].rearrange("p (h d) -> p h d", d=96), in0=v3,
                                                                 in1=stat[:, 40:48].unsqueeze(2).to_broadcast([128, 8, 96]), op=ALU.mult),
                         R=[*psb(pst), bf('stat')], W=[bf('tB')])
                    q3 = tB[:, 0:768].rearrange("p (h d) -> p h d", d=96)
                    S.op('dve', lambda e, q3=q3: e.tensor_copy(out=tmb[:, 0:512].rearrange("p (h d) -> p h d", d=64), in_=q3[:, :, 0:64]),
                         R=[bf('tB')], W=[bf('tmb')])
                    rope_tm(q3[:, :, 64:96], rowv[:, 0:32], tt, 8, tmb2[:, 0:256].rearrange("p (h d) -> p h d", d=32))
                    pt = rot('pt', 2)
                    S.group('pe', [lambda e, j=j, pt=pt: e.transpose(out=PT[pt][:, j * 128:(j + 1) * 128], in_=tmb[:, j * 128:(j + 1) * 128],
                                                                    identity=ident[:]) for j in range(4)],
                            R=[bf('tmb'), bf('ident')], W=[bf('PT%d' % pt)])
                    S.op('dve', lambda e, pt=pt, t=t: e.tensor_scalar(
                        out=QT[:].rearrange("p (j two) c -> p j two c", two=2)[0:64, :, 0, t * 128:(t + 1) * 128],
                        in0=PT[pt][0:64, 0:512].rearrange("p (j c) -> p j c", c=128), scalar1=colv[0:64, 4:5], scalar2=None, op0=ALU.mult),
                         R=[bf('PT%d' % pt), bf('colv')], W=[bf('QT')])
                    S.op('dve', lambda e, pt=pt, t=t: e.tensor_scalar(
                        out=QT[:].rearrange("p (j two) c -> p j two c", two=2)[64:128, :, 1, t * 128:(t + 1) * 128],
                        in0=PT[pt][64:128, 0:512].rearrange("p (j c) -> p j c", c=128), scalar1=colv[64:128, 4:5], scalar2=None, op0=ALU.mult),
                         R=[bf('PT%d' % pt), bf('colv')], W=[bf('QT')])
                    pt = rot('pt', 2)
                    S.group('pe', [lambda e, h=h, pt=pt: e.transpose(out=PT[pt][0:32, h * 128:(h + 1) * 128], in_=tmb2[:, h * 32:(h + 1) * 32],
                                                                    identity=ident[:]) for h in range(8)],
                            R=[bf('ropeo'), bf('ident')], W=[bf('PT%d' % pt)])
                    S.op('dve', lambda e, pt=pt, t=t: e.tensor_copy(out=QrT[0:32, :, t * 128:(t + 1) * 128],
                                                                    in_=PT[pt][0:32, :].rearrange("p (h c) -> p h c", c=128)),
                         R=[bf('PT%d' % pt)], W=[bf('QrT')])
                items = []
                for h in range(8):
                    hb, j = (h % 2) * 64, h // 2
                    po = rot('po', 2)
                    for kt in range(nt):
                        sl = rot('sc', 4)

                        def front(h=h, hb=hb, j=j, kt=kt, sl=sl):
                            sap, sbuf_ = sc_slot(sl)
                            pap, pbuf_ = p_slot(sl)
                            S.group('pe', [
                                lambda e: e.matmul(sap, lhsT=KT[:, j, kt * 128:(kt + 1) * 128], rhs=QT[:, h, :],
                                                   start=True, stop=False),
                                lambda e: e.matmul(sap, lhsT=KrT[:, kt * 128:(kt + 1) * 128], rhs=QrT[:, h, :], start=False, stop=True)],
                                R=[bf('KT'), bf('KrT'), bf('QT'), bf('QrT')], W=[sbuf_])
                            S.op('act', lambda e: e.activation(out=pap, in_=sap, func=AF.Exp, scale=rsK[:, kt, h:h + 1]),
                                 R=[sbuf_, bf('rsK')], W=[pbuf_])

                        def back(h=h, kt=kt, sl=sl, po=po, nt=nt):
                            pap, pbuf_ = p_slot(sl)
                            S.op('pe', lambda e: e.matmul(PO[po][:, :], lhsT=V[:, kt, h * 65:h * 65 + 128], rhs=pap, start=(kt == 0), stop=(kt == nt - 1)),
                                 R=[bf('V'), pbuf_], W=[bf('PO%d' % po)])
                            if kt == nt - 1:
                                return (lambda: finalize_a(po), lambda: finalize_b(po, h, gz2[0:64, h, :], bf('gz2')))
                            return None
                        items.append((front, back))
                run_pipeline(items, depth=3, delay=max(0, min(6, nt - 3)))
                out_proj2('EO', dst, base + g * G)
                if toe_pending:
                    emit_toe_rows(toe_pending.pop(0))

        def l1_pass_a(src, base, slen):
            ng = slen // G
            while toe_pending:
                emit_toe_rows(toe_pending.pop(0))
            ringcfg['slots'] = [0, 1, 2, 3]
            for g in range(ng):
                stage_n(src, base + g * G)
                rk = load_chunk('O_K')
                rv = load_chunk('O_V')
                psts = {}

                def projA(t, g=g, rv=rv, rk=rk, psts=psts):
                    tt = g * 4 + t
                    pst = rot('ps', 2)
                    proj_tm(rv, 512, 512, t, pst)
                    S.op('dve', lambda e, tt=tt, pst=pst: e.tensor_copy(out=V[:, tt, 0:520].rearrange("p (h d) -> p h d", d=65)[:, :, 0:64],
                                                                       in_=PS[pst][:, 0:512].rearrange("p (h d) -> p h d", d=64)),
                         R=[*psb(pst)], W=[bf('V')])
                    psts[t] = pst
                    proj_tm(rk, 512, 512, t, pst, o0=512)
                projA(0)
                for t in range(4):
                    tt = g * 4 + t
                    if t + 1 < 4:
                        projA(t + 1)
                    pst = psts[t]
                    v3 = head_norm(pst, 64, 8, 64, None, o0=512)
                    rstd_from(stat[:, 16:24], stat[:, 40:48], 8, 64.0)
                    S.op('dve', lambda e, v3=v3: e.tensor_tensor(out=tmb[:, 0:512].rearrange("p (h d) -> p h d", d=64), in0=v3,
                                                                 in1=stat[:, 40:48].unsqueeze(2).to_broadcast([128, 8, 64]), op=ALU.mult),
                         R=[*psb(pst), bf('stat')], W=[bf('tmb')])
                    pt = rot('pt', 2)
                    S.group('pe', [lambda e, j=j, pt=pt: e.transpose(out=PT[pt][:, j * 128:(j + 1) * 128], in_=tmb[:, j * 128:(j + 1) * 128],
                                                                    identity=ident[:]) for j in range(4)],
                            R=[bf('tmb'), bf('ident')], W=[bf('PT%d' % pt)])
                    S.op('dve', lambda e, pt=pt, tt=tt: e.tensor_scalar(out=KT[:, :, tt * 128:(tt + 1) * 128],
                                                                        in0=PT[pt][:, 0:512].rearrange("p (j c) -> p j c", c=128),
                                                                        scalar1=colv[:, 5:6], scalar2=None, op0=ALU.mult),
                         R=[bf('PT%d' % pt), bf('colv')], W=[bf('KT')])

        def gelu_parts(po, dst_f32, tmp):
            S.op('act', lambda e: e.activation(out=tmp, in_=PO[po][:, :], func=AF.Square), R=[bf('PO%d' % po)], W=[bf('gl1')])
            S.op('dve', lambda e: e.tensor_scalar(out=tmp, in0=tmp, scalar1=0.044715 * 1.5957691216, scalar2=1.5957691216, op0=ALU.mult,
                                                  op1=ALU.add), R=[bf('gl1')], W=[bf('gl1')])
            S.op('dve', lambda e: e.tensor_tensor(out=tmp, in0=tmp, in1=PO[po][:, :], op=ALU.mult), R=[bf('gl1'), bf('PO%d' % po)], W=[bf('gl1')])
            S.op('act', lambda e: e.activation(out=dst_f32, in_=tmp, func=AF.Sigmoid), R=[bf('gl1')], W=[bf('gl2')])

        def l1_pass_b(src, dst, base, slen):
            ng = slen // G
            nt = slen // 128
            for g in range(ng):
                stage_n(src, base + g * G)
                S.fence([(e_[0], e_[1], 'dma') for bb in (bf('ystore0'), bf('ystore1')) for e_ in bb.sems.values()])
                r_cu = load_chunk('O_CU')
                r_cz = load_chunk('O_CZ')
                r_cv = load_chunk('O_CV')
                csets = [(tA[:, 0:512], tA[:, 512:1024], tmb[:, 0:512], stat[:, 0:8]),
                         (tB[:, 0:512], tB[:, 512:1024], tmb[:, 512:1024], stat[:, 8:16]),
                         (tC[:, 0:512], tC[:, 512:1024], tmb2[:, 0:512], stat[:, 24:32])]
                GA, GB = 0.044715 * 1.5957691216, 1.5957691216

                def skew(chains, lag):
                    nst = max(len(c_) + jj * lag for jj, c_ in enumerate(chains))
                    for step in range(nst):
                        for jj, c_ in enumerate(chains):
                            si_ = step - jj * lag
                            if 0 <= si_ < len(c_):
                                c_[si_]()

                def cu_chain(j, k, r_cu=r_cu):
                    X, Y, _, _ = csets[k]
                    bX, bY = bf('cX%d' % k), bf('cY%d' % k)
                    box = {}

                    def s0():
                        box['sap'], box['sb'] = sc_slot(rot('sc', 4))
                        sap = box['sap']
                        S.group('pe', [lambda e, kc=kc: e.matmul(sap, lhsT=ring[r_cu][:, kc * 512 + j * 128:kc * 512 + (j + 1) * 128],
                                                                 rhs=hT[:, kc, :], start=(kc == 0), stop=(kc == 7)) for kc in range(8)],
                                R=[rbuf(r_cu), bf('hT')], W=[box['sb']])
                    return [
                        s0,
                        lambda: S.op('act', lambda e: e.activation(out=X, in_=box['sap'], func=AF.Square), R=[box['sb']], W=[bX]),
                        lambda: S.op('dve', lambda e: e.tensor_scalar(out=X, in0=X, scalar1=GA, scalar2=GB, op0=ALU.mult, op1=ALU.add), W=[bX]),
                        lambda: S.op('dve', lambda e: e.tensor_tensor(out=X, in0=X, in1=box['sap'], op=ALU.mult), R=[box['sb']], W=[bX]),
                        lambda: S.op('act', lambda e: e.activation(out=Y, in_=X, func=AF.Sigmoid), R=[bX], W=[bY]),
                        lambda: S.op('dve', lambda e: e.tensor_tensor(out=gz[:, j, :], in0=box['sap'], in1=Y, op=ALU.mult),
                                     R=[box['sb'], bY], W=[bf('gz')]),
                    ]

                def cz_chain(j, k, r_cz=r_cz):
                    X, Y, _, _ = csets[k]
                    bX, bY = bf('cX%d' % k), bf('cY%d' % k)
                    box = {}

                    def s0():
                        box['sap'], box['sb'] = sc_slot(rot('sc', 4))
                        sap = box['sap']
                        S.group('pe', [lambda e, kc=kc: e.matmul(sap, lhsT=ring[r_cz][:, kc * 512 + j * 128:kc * 512 + (j + 1) * 128],
                                                                 rhs=hT[:, kc, :], start=(kc == 0), stop=(kc == 7)) for kc in range(8)],
                                R=[rbuf(r_cz), bf('hT')], W=[box['sb']])
                    return [
                        s0,
                        lambda: S.op('act', lambda e: e.activation(out=Y, in_=box['sap'], func=AF.Sigmoid), R=[box['sb']], W=[bY]),
                        lambda: S.op('dve', lambda e: e.tensor_tensor(out=X, in0=box['sap'], in1=Y, op=ALU.mult), R=[box['sb'], bY], W=[bX]),
                        lambda: S.op('dve', lambda e: e.tensor_tensor(out=gz[:, j, :], in0=gz[:, j, :], in1=X, op=ALU.mult), R=[bX], W=[bf('gz')]),
                    ]

                def cv_chain(t, k, r_cv=r_cv):
                    X, Y, VV, st = csets[k]
                    bX, bY, bV, bS = bf('cX%d' % k), bf('cY%d' % k), bf('cV%d' % k), bf('cS%d' % k)
                    box = {}

                    def s0():
                        box['sap'], box['sb'] = sc_slot(rot('sc', 4))
                        sap = box['sap']
                        S.group('pe', [lambda e, kc=kc: e.matmul(sap, lhsT=hT[:, kc, t * 128:(t + 1) * 128],
                                                                 rhs=ring[r_cv][:, kc * 512:(kc + 1) * 512], start=(kc == 0), stop=(kc == 7))
                                       for kc in range(8)], R=[rbuf(r_cv), bf('hT')], W=[box['sb']])

                    def s6():
                        S.op('dve', lambda e: e.memset(st[:, 1:2], 0.0), W=[bS])
                        S.op('dve', lambda e: e.tensor_reduce(out=st[:, 0:1], in_=X, axis=AX.X, op=ALU.add), R=[bX], W=[bS])

                    def spatial(gi):
                        def f():
                            po = rot('po', 2)
                            S.group('pe', [
                                lambda e: e.matmul(PO[po][:, 0:128], lhsT=VV[:, gi * 128:(gi + 1) * 128], rhs=wsT[:, gi * 128:(gi + 1) * 128],
                                                   start=True, stop=False),
                                lambda e: e.matmul(PO[po][:, 0:128], lhsT=onesf[0:1, :], rhs=cbs[0:1, gi * 128:(gi + 1) * 128],
                                                   start=False, stop=True)],
                                R=[bV, bf('wsT'), bf('onesf'), bf('cbs')], W=[bf('PO%d' % po)])
                            S.op('dve', lambda e: e.tensor_tensor(out=ycA[:, gi, t * 128:(t + 1) * 128], in0=PO[po][:, 0:128],
                                                                  in1=gz[:, gi, t * 128:(t + 1) * 128], op=ALU.mult),
                                 R=[bf('PO%d' % po), bf('gz')], W=[bf('ycA')])
                        return f
                    return [
                        s0,
                        lambda: S.op('act', lambda e: e.activation(out=X, in_=box['sap'], func=AF.Square), R=[box['sb']], W=[bX]),
                        lambda: S.op('dve', lambda e: e.tensor_scalar(out=X, in0=X, scalar1=GA, scalar2=GB, op0=ALU.mult, op1=ALU.add), W=[bX]),
                        lambda: S.op('dve', lambda e: e.tensor_tensor(out=X, in0=X, in1=box['sap'], op=ALU.mult), R=[box['sb']], W=[bX]),
                        lambda: S.op('act', lambda e: e.activation(out=Y, in_=X, func=AF.Sigmoid), R=[bX], W=[bY]),
                        lambda: S.op('dve', lambda e: e.tensor_tensor(out=X, in0=box['sap'], in1=Y, op=ALU.mult), R=[box['sb'], bY], W=[bX]),
                        s6,
                        lambda: S.op('dve', lambda e: e.tensor_scalar(out=st[:, 2:3], in0=st[:, 0:1], scalar1=-1.0 / 512, scalar2=None, op0=ALU.mult),
                                     W=[bS]),
                        lambda: S.op('dve', lambda e: e.tensor_scalar(out=X, in0=X, scalar1=st[:, 2:3], scalar2=None, op0=ALU.add), R=[bS], W=[bX]),
                        lambda: S.op('act', lambda e: e.activation(out=Y, in_=X, func=AF.Square, accum_out=st[:, 1:2]), R=[bX], W=[bY, bS]),
                        lambda: S.op('act', lambda e: e.activation(out=st[:, 4:5], in_=st[:, 1:2], func=AF.Ln, scale=1.0 / 512, bias=EPS), W=[bS]),
                        lambda: S.op('act', lambda e: e.activation(out=st[:, 3:4], in_=st[:, 4:5], func=AF.Exp, scale=-0.5), W=[bS]),
                        lambda: S.op('dve', lambda e: e.scalar_tensor_tensor(out=X, in0=X, scalar=st[:, 3:4], in1=rowv[:, 64:576],
                                                                             op0=ALU.mult, op1=ALU.mult), R=[bS, bf('rowv')], W=[bX]),
                        lambda: S.op('dve', lambda e: e.tensor_tensor(out=VV, in0=X, in1=rowv[:, 576:1088], op=ALU.add), R=[bX, bf('rowv')], W=[bV]),
                        spatial(0), spatial(1), spatial(2), spatial(3),
                    ]

                skew([cu_chain(j, j % 3) for j in range(4)], 2)
                skew([cz_chain(j, (j + 1) % 3) for j in range(4)], 2)
                skew([cv_chain(t, (t + 2) % 3) for t in range(4)], 6)
                S.fence()
                r = load_chunk('O_DZ')
                for h in range(8):
                    po = rot('po', 2)
                    proj_fm(r, 0, 512, h * 64, 64, po)
                    S.op('act', lambda e, po=po: e.activation(out=tB[0:64, 0:512], in_=PO[po][0:64, :], func=AF.Sigmoid),
                         R=[bf('PO%d' % po)], W=[bf('tB')])
                    S.op('dve', lambda e, po=po, h=h: e.tensor_tensor(out=gz2[0:64, h, :], in0=PO[po][0:64, :], in1=tB[0:64, 0:512], op=ALU.mult),
                         R=[bf('PO%d' % po), bf('tB')], W=[bf('gz2')])
                r = load_chunk('O_DQ')
                psts = {}

                def projA(t, r=r, psts=psts):
                    pst = rot('ps', 2)
                    psts[t] = pst
                    proj_tm(r, 512, 512, t, pst)
                projA(0)
                for t in range(4):
                    if t + 1 < 4:
                        projA(t + 1)
                    pst = psts[t]
                    v3 = head_norm(pst, 64, 8, 64, None)
                    rstd_from(stat[:, 16:24], stat[:, 40:48], 8, 64.0)
                    S.op('dve', lambda e, v3=v3: e.tensor_tensor(out=tmb[:, 0:512].rearrange("p (h d) -> p h d", d=64), in0=v3,
                                                                 in1=stat[:, 40:48].unsqueeze(2).to_broadcast([128, 8, 64]), op=ALU.mult),
                         R=[*psb(pst), bf('stat')], W=[bf('tmb')])
                    pt = rot('pt', 2)
                    S.group('pe', [lambda e, j=j, pt=pt: e.transpose(out=PT[pt][:, j * 128:(j + 1) * 128], in_=tmb[:, j * 128:(j + 1) * 128],
                                                                    identity=ident[:]) for j in range(4)],
                            R=[bf('tmb'), bf('ident')], W=[bf('PT%d' % pt)])
                    S.op('dve', lambda e, pt=pt, t=t: e.tensor_scalar(
                        out=QT[:].rearrange("p (j two) c -> p j two c", two=2)[0:64, :, 0, t * 128:(t + 1) * 128],
                        in0=PT[pt][0:64, 0:512].rearrange("p (j c) -> p j c", c=128), scalar1=colv[0:64, 6:7], scalar2=None, op0=ALU.mult),
                         R=[bf('PT%d' % pt), bf('colv')], W=[bf('QT')])
                    S.op('dve', lambda e, pt=pt, t=t: e.tensor_scalar(
                        out=QT[:].rearrange("p (j two) c -> p j two c", two=2)[64:128, :, 1, t * 128:(t + 1) * 128],
                        in0=PT[pt][64:128, 0:512].rearrange("p (j c) -> p j c", c=128), scalar1=colv[64:128, 6:7], scalar2=None, op0=ALU.mult),
                         R=[bf('PT%d' % pt), bf('colv')], W=[bf('QT')])
                items = []
                for h in range(8):
                    hb, j = (h % 2) * 64, h // 2
                    po = rot('po', 2)
                    kts = [kt for kt in range(4 * g - 8, 4 * g + 12) if 0 <= kt < nt]
                    rtbox = {}
                    for ki, kt in enumerate(kts):
                        sl = rot('sc', 4)
                        c0 = (4 * g - kt) * 128 + C0

                        def front(h=h, hb=hb, j=j, kt=kt, ki=ki, sl=sl, c0=c0, rtbox=rtbox):
                            if ki == 0:
                                rtbox['rt'] = load_chunk('T%d' % h)
                            rt = rtbox['rt']
                            sap, sbuf_ = sc_slot(sl)
                            pap, pbuf_ = p_slot(sl)
                            S.op('pe', lambda e: e.matmul(sap, lhsT=KT[:, j, kt * 128:(kt + 1) * 128], rhs=QT[:, h, :],
                                                          start=True, stop=True), R=[bf('KT'), bf('QT')], W=[sbuf_])
                            S.op('act', lambda e: e.activation(out=pap, in_=sap, func=AF.Exp, scale=0.125), R=[sbuf_], W=[pbuf_])
                            S.op('dve',
                                 lambda e: e.tensor_tensor(out=pap, in0=pap, in1=ring[rt][:, c0:c0 + 512], op=ALU.mult),
                                 R=[rbuf(rt)], W=[pbuf_])

                        def back(h=h, kt=kt, ki=ki, sl=sl, po=po, n=len(kts)):
                            pap, pbuf_ = p_slot(sl)
                            S.op('pe', lambda e: e.matmul(PO[po][:, :], lhsT=V[:, kt, h * 65:h * 65 + 128], rhs=pap, start=(ki == 0), stop=(ki == n - 1)),
                                 R=[bf('V'), pbuf_], W=[bf('PO%d' % po)])
                            if ki == n - 1:
                                return (lambda: finalize_a(po, True), lambda: finalize_b(po, h, gz2[0:64, h, :], bf('gz2')))
                            return None
                        items.append((front, back))
                run_pipeline(items, depth=3, delay=max(0, min(6, min(12, nt) - 3)))
                out_proj2('OO', dst, base + g * G)

        if nlayers == 2:
            b0 = 0
            for slen in seqs:
                for ps_i, (isy, pf_first) in enumerate(((False, True), (False, True), (True, False), (True, True))):
                    for g in range(slen // G):
                        pref['plan'].append((isy, b0 + g * G, True if g > 0 else pf_first))
                b0 += slen
        base = 0
        last = None
        for si, slen in enumerate(seqs):
            if nlayers == 0:
                break
            seq_layer_setup(0, si)
            if nlayers == -3:
                break
            if nlayers == -2:
                stage_n(x, base)
                break
            l0_pass_a(x, base, slen)
            if nlayers < 0:
                break
            last = l0_pass_b(x, y, base, slen)
            if nlayers > 1:
                seq_layer_setup(1, si)
                l1_pass_a(y, base, slen)
                l1_pass_b(y, y, base, slen)
            base += slen

        S.sbuf_left = nc.sbuf_bytes_remaining
        block = es.enter_context(nc.Block())
        fin = [(e_[0], e_[1], 'dma') for b in B.values() for e_ in b.sems.values()]
        S.run(block, fin)
    return nc, S


_CACHE = {}


def _prep_shared(inp):
    hc = _host_consts()
    f = lambda a: np.ascontiguousarray(np.asarray(a, dtype=np.float32))
    colv = np.zeros((128, 24), np.float32)
    colv[:, 0:2] = f(inp['mla_q_norm'])[0].reshape(2, 128).T
    colv[:, 2] = f(inp['mla_kv_norm'])[0]
    colv[:, 3] = np.tile(f(inp['mla_k_gain'])[0][0:64], 2)
    colv[:, 4] = np.tile(f(inp['mla_q_gain'])[0][0:64], 2)
    colv[:, 5] = np.tile(f(inp['d_k_gain'])[0], 2)
    colv[:, 6] = np.tile(f(inp['d_q_gain'])[0], 2)
    ac = f(inp['a_conv'])[0]
    for j in range(4):
        colv[:, 8 + j * 3:8 + j * 3 + 3] = ac[:, j * 128:(j + 1) * 128].T
    rowv = np.zeros((1, 1088), np.float32)
    rowv[0, 0:32] = f(inp['mla_q_gain'])[0][64:96]
    rowv[0, 32:64] = f(inp['mla_k_gain'])[0][64:96]
    rowv[0, 64:576] = f(inp['c_vnorm_g'])[0]
    rowv[0, 576:1088] = f(inp['c_vnorm_b'])[0]
    cws = f(inp['c_ws'])[0]
    c_wsT = np.ascontiguousarray(cws.transpose(2, 0, 1).reshape(128, 512))
    shared = dict(
        norm_gT=np.ascontiguousarray(f(inp['norm_g']).reshape(2, 8, 128).transpose(0, 2, 1)),
        w_mod=f(inp['w_mod']), b_mod=f(inp['b_mod']), rel_bias=f(inp['rel_bias']),
        w_in_e=f(inp['w_in_e'])[0], w_uq=f(inp['mla_w_uq'])[0], w_ukv=f(inp['mla_w_ukv'])[0], w_out_e=f(inp['w_out_e'])[0],
        w_in_o=f(inp['w_in_o'])[0], w_out_o=f(inp['w_out_o'])[0], colv=colv, rowv=rowv,
        c_bs=np.ascontiguousarray(f(inp['c_bs'])[0].reshape(1, 512)), c_wsT=c_wsT,
        cos=hc['cos'], sin=hc['sin'], oh=hc['oh'], mult=hc['mult'], ident=hc['ident'])
    return shared


def kernel(**inputs):
    xp = np.asarray(inputs['x_prompt'], dtype=np.float32)
    xsm = np.asarray(inputs['x_sample'], dtype=np.float32)
    cp = np.asarray(inputs['c_prompt'], dtype=np.float32)
    cs = np.asarray(inputs['c_sample'], dtype=np.float32)
    ncore = 8
    seqs = [xp.shape[1]] + [xsm.shape[1]] * 4
    key = tuple(seqs)
    if key not in _CACHE:
        _CACHE[key] = build(seqs)[0]
    nc = _CACHE[key]
    shared = _prep_shared(inputs)
    in_maps = []
    for c in range(ncore):
        xc = np.concatenate([xp[c]] + [xsm[4 * c + i] for i in range(4)], axis=0)
        cc = np.concatenate([cp[c:c + 1], cs[4 * c:4 * c + 4]], axis=0)
        m = dict(shared)
        m['x'] = np.ascontiguousarray(xc)
        m['cT'] = np.ascontiguousarray(cc.T)
        in_maps.append(m)
    res = run_bass_kernel_spmd(nc, in_maps, core_ids=list(range(ncore)))
    yp = np.empty_like(xp)
    ys = np.empty_like(xsm)
    L = xp.shape[1]
    Ls = xsm.shape[1]
    for c in range(ncore):
        yc = np.asarray(res.results[c]['y'])
        yp[c] = yc[0:L]
        for i in range(4):
            ys[4 * c + i] = yc[L + i * Ls:L + (i + 1) * Ls]
    return (yp, ys)
```
